# Optimizing a Trainium2 kernel written in Bass

```python
import math
import jax
import jax.numpy as jnp
from jax import lax
import numpy as np

D_MODEL = 1024
BATCH = 4
SEQ = 8192
DEPTH = 4

GRID_W = 64
CTX_LEN = 256
N_MIXERS = 4
Q_BLOCK = 128
ROPE_BASE = 10000.0
EPS = 1e-6
NEG_INF = -1e30

GQA_HEADS = 16
GQA_KV_HEADS = 4
GQA_HEAD_DIM = 64
MLA_HEADS = 16
MLA_NOPE_DIM = 64
MLA_ROPE_DIM = 32
MLA_V_DIM = 64
MLA_Q_LORA = 384
MLA_KV_LORA = 256
DIFF_HEADS = 8
DIFF_HEAD_DIM = 64
NA_HEADS = 16
NA_HEAD_DIM = 64
NA_ROWS = 8
NA_COLS = 16
NA_QROWS = Q_BLOCK // GRID_W
FFN_DIM = 2816

kernel_name = 'hybrid_prefix_dit_trunk'


def rms_norm(x, g):
    xf = x.astype(jnp.float32)
    y = xf * lax.rsqrt(jnp.mean(xf * xf, axis=-1, keepdims=True) + EPS)
    return (y * g.astype(jnp.float32)).astype(x.dtype)


def norm_modulate(h, g, shift, scale):
    return rms_norm(h, g) * (1 + scale) + shift


def softmax_f32(s):
    return jax.nn.softmax(s.astype(jnp.float32), axis=-1)


def axial_rope(n_tokens, rot_dim, dtype):
    t = jnp.arange(n_tokens)
    row = (t // GRID_W).astype(jnp.float32)
    col = (t % GRID_W).astype(jnp.float32)
    half = rot_dim // 2
    inv = ROPE_BASE ** (-jnp.arange(0, half, 2, dtype=jnp.float32) / half)
    ar, ac = row[:, None] * inv, col[:, None] * inv
    ang = jnp.concatenate([ar, ar, ac, ac], axis=-1)
    return jnp.cos(ang).astype(dtype), jnp.sin(ang).astype(dtype)


def apply_rope(x, cos, sin):
    x1, x2, x3, x4 = jnp.split(x, 4, axis=-1)
    rot = jnp.concatenate([-x2, x1, -x4, x3], axis=-1)
    shape = (cos.shape[0],) + (1,) * (x.ndim - 3) + (cos.shape[1],)
    return x * cos.reshape(shape) + rot * sin.reshape(shape)


def sweep_query_blocks(fn, *qs):
    b, t = qs[0].shape[:2]
    nb = t // Q_BLOCK
    blocks = tuple(jnp.moveaxis(q.reshape(b, nb, Q_BLOCK, *q.shape[2:]), 1, 0) for q in qs)
    out = lax.map(lambda qb: fn(*qb), blocks)
    return jnp.moveaxis(out, 0, 1).reshape(b, t, *out.shape[3:])


def gqa_mixer(hx, hc, w_in, q_g, k_g, w_out, ctx_out):
    b, t, d = hx.shape
    grp = GQA_HEADS // GQA_KV_HEADS
    splits = [GQA_HEADS * GQA_HEAD_DIM, (GQA_HEADS + GQA_KV_HEADS) * GQA_HEAD_DIM]

    def project(h):
        n = h.shape[1]
        q, k, v = jnp.split(h @ w_in, splits, axis=-1)
        q = rms_norm(q.reshape(b, n, GQA_KV_HEADS, grp, GQA_HEAD_DIM), q_g)
        k = rms_norm(k.reshape(b, n, GQA_KV_HEADS, GQA_HEAD_DIM), k_g)
        return q, k, v.reshape(b, n, GQA_KV_HEADS, GQA_HEAD_DIM)

    qx, kx, vx = project(hx)
    qc, kc, vc = project(hc)
    cos, sin = axial_rope(t, GQA_HEAD_DIM, hx.dtype)
    qx, kx = apply_rope(qx, cos, sin), apply_rope(kx, cos, sin)
    k_all = jnp.concatenate([kc, kx], axis=1)
    v_all = jnp.concatenate([vc, vx], axis=1)
    scale = GQA_HEAD_DIM ** -0.5

    def attend(q, k, v):
        s = jnp.einsum('bqhgd,bkhd->bhgqk', q, k) * scale
        p = softmax_f32(s).astype(v.dtype)
        return jnp.einsum('bhgqk,bkhd->bqhgd', p, v)

    out_x = sweep_query_blocks(lambda qb: attend(qb, k_all, v_all), qx).reshape(b, t, d) @ w_out
    out_c = attend(qc, kc, vc).reshape(b, hc.shape[1], d) @ w_out if ctx_out else None
    return out_x, out_c


def mla_mixer(hx, hc, w_in, q_norm_g, kv_norm_g, w_uq, w_ukv, w_out, ctx_out):
    b, t, d = hx.shape
    qk_dim = MLA_NOPE_DIM + MLA_ROPE_DIM

    def project(h):
        n = h.shape[1]
        cq, ckv, k_rope = jnp.split(h @ w_in, [MLA_Q_LORA, MLA_Q_LORA + MLA_KV_LORA], axis=-1)
        q = (rms_norm(cq, q_norm_g) @ w_uq).reshape(b, n, MLA_HEADS, qk_dim)
        kv = (rms_norm(ckv, kv_norm_g) @ w_ukv).reshape(b, n, MLA_HEADS, MLA_NOPE_DIM + MLA_V_DIM)
        q_nope, q_rope = jnp.split(q, [MLA_NOPE_DIM], axis=-1)
        k_nope, v = jnp.split(kv, [MLA_NOPE_DIM], axis=-1)
        return q_nope, q_rope, k_nope, k_rope, v

    qnx, qrx, knx, krx, vx = project(hx)
    qnc, qrc, knc, krc, vc = project(hc)
    cos, sin = axial_rope(t, MLA_ROPE_DIM, hx.dtype)
    qrx, krx = apply_rope(qrx, cos, sin), apply_rope(krx, cos, sin)
    kn_all = jnp.concatenate([knc, knx], axis=1)
    kr_all = jnp.concatenate([krc, krx], axis=1)
    v_all = jnp.concatenate([vc, vx], axis=1)
    scale = qk_dim ** -0.5

    def attend(qn, qr, kn, kr, v):
        s = (jnp.einsum('bqhd,bkhd->bhqk', qn, kn) + jnp.einsum('bqhd,bkd->bhqk', qr, kr)) * scale
        p = softmax_f32(s).astype(v.dtype)
        return jnp.einsum('bhqk,bkhd->bqhd', p, v)

    ox = sweep_query_blocks(lambda qn, qr: attend(qn, qr, kn_all, kr_all, v_all), qnx, qrx)
    out_x = ox.reshape(b, t, MLA_HEADS * MLA_V_DIM) @ w_out
    out_c = attend(qnc, qrc, knc, krc, vc).reshape(b, hc.shape[1], MLA_HEADS * MLA_V_DIM) @ w_out if ctx_out else None
    return out_x, out_c


def diff_mixer(hx, hc, w_in, lam, subln_g, w_out, layer_idx, ctx_out):
    b, t, d = hx.shape
    lambda_init = 0.8 - 0.6 * math.exp(-0.3 * layer_idx)
    lf = lam.astype(jnp.float32)
    lam_full = jnp.exp(jnp.sum(lf[0] * lf[1])) - jnp.exp(jnp.sum(lf[2] * lf[3])) + lambda_init

    def project(h):
        n = h.shape[1]
        q, k, v = jnp.split(h @ w_in, 3, axis=-1)
        q = q.reshape(b, n, DIFF_HEADS, 2, DIFF_HEAD_DIM)
        k = k.reshape(b, n, DIFF_HEADS, 2, DIFF_HEAD_DIM)
        return q, k, v.reshape(b, n, DIFF_HEADS, 2 * DIFF_HEAD_DIM)

    qx, kx, vx = project(hx)
    qc, kc, vc = project(hc)
    cos, sin = axial_rope(t, DIFF_HEAD_DIM, hx.dtype)
    qx, kx = apply_rope(qx, cos, sin), apply_rope(kx, cos, sin)
    k_all = jnp.concatenate([kc, kx], axis=1)
    v_all = jnp.concatenate([vc, vx], axis=1)
    scale = DIFF_HEAD_DIM ** -0.5

    def attend(q, k, v):
        s = jnp.einsum('bqhjd,bkhjd->bhjqk', q, k) * scale
        p = softmax_f32(s)
        p = p[:, :, 0] - lam_full * p[:, :, 1]
        o = jnp.einsum('bhqk,bkhe->bqhe', p.astype(v.dtype), v)
        return rms_norm(o, subln_g) * (1 - lambda_init)

    out_x = sweep_query_blocks(lambda qb: attend(qb, k_all, v_all), qx).reshape(b, t, d) @ w_out
    out_c = attend(qc, kc, vc).reshape(b, hc.shape[1], d) @ w_out if ctx_out else None
    return out_x, out_c


def na_mixer(hx, hc, w_in, rpb, w_out, ctx_out):
    b, t, d = hx.shape
    rows = t // GRID_W
    win_rows = min(NA_ROWS, rows)
    band_rows = min(win_rows + 1, rows)
    band = band_rows * GRID_W
    n_ctx = hc.shape[1]

    def project(h):
        n = h.shape[1]
        q, k, v = jnp.split(h @ w_in, 3, axis=-1)
        return tuple(a.reshape(b, n, NA_HEADS, NA_HEAD_DIM) for a in (q, k, v))

    qx, kx, vx = project(hx)
    qc, kc, vc = project(hc)
    scale = NA_HEAD_DIM ** -0.5
    k_grid = kx.reshape(b, rows, GRID_W, NA_HEADS, NA_HEAD_DIM)
    v_grid = vx.reshape(b, rows, GRID_W, NA_HEADS, NA_HEAD_DIM)
    nb = t // Q_BLOCK
    q_blocks = jnp.moveaxis(qx.reshape(b, nb, Q_BLOCK, NA_HEADS, NA_HEAD_DIM), 1, 0)
    q_off, k_off = jnp.arange(Q_BLOCK), jnp.arange(band)
    q_dr, q_col = q_off // GRID_W, q_off % GRID_W
    k_dr, k_col = k_off // GRID_W, k_off % GRID_W
    col_start = jnp.clip(q_col - NA_COLS // 2, 0, GRID_W - NA_COLS)
    col_in = (k_col[None, :] >= col_start[:, None]) & (k_col[None, :] < col_start[:, None] + NA_COLS)
    dc_idx = jnp.clip(k_col[None, :] - q_col[:, None] + NA_COLS - 1, 0, 2 * NA_COLS - 2)

    def block(args):
        j, qb = args
        r = j * NA_QROWS + q_dr
        row_start = jnp.clip(r - win_rows // 2, 0, rows - win_rows)
        b0 = jnp.minimum(row_start[0], rows - band_rows)
        k_band = lax.dynamic_slice_in_dim(k_grid, b0, band_rows, axis=1).reshape(b, band, NA_HEADS, NA_HEAD_DIM)
        v_band = lax.dynamic_slice_in_dim(v_grid, b0, band_rows, axis=1).reshape(b, band, NA_HEADS, NA_HEAD_DIM)
        k_row = b0 + k_dr
        in_win = col_in & (k_row[None, :] >= row_start[:, None]) & (k_row[None, :] < row_start[:, None] + win_rows)
        dr_idx = jnp.clip(k_row[None, :] - r[:, None] + NA_ROWS - 1, 0, 2 * NA_ROWS - 2)
        bias = rpb[:, dr_idx, dc_idx].astype(jnp.float32)
        s_lat = jnp.einsum('bqhd,bkhd->bhqk', qb, k_band).astype(jnp.float32) * scale + bias
        s_lat = jnp.where(in_win, s_lat, NEG_INF)
        s_ctx = jnp.einsum('bqhd,bkhd->bhqk', qb, kc).astype(jnp.float32) * scale
        p = softmax_f32(jnp.concatenate([s_ctx, s_lat], axis=-1)).astype(vx.dtype)
        return (jnp.einsum('bhqk,bkhd->bqhd', p[..., :n_ctx], vc)
                + jnp.einsum('bhqk,bkhd->bqhd', p[..., n_ctx:], v_band))

    ox = lax.map(block, (jnp.arange(nb), q_blocks))
    out_x = jnp.moveaxis(ox, 0, 1).reshape(b, t, d) @ w_out
    if ctx_out:
        s = jnp.einsum('bqhd,bkhd->bhqk', qc, kc) * scale
        p = softmax_f32(s).astype(vc.dtype)
        out_c = jnp.einsum('bhqk,bkhd->bqhd', p, vc).reshape(b, n_ctx, d) @ w_out
    else:
        out_c = None
    return out_x, out_c


def dwconv_centred(u, w, bias):
    p = jnp.pad(u, ((0, 0), (1, 1), (0, 0)))
    return p[:, :-2] * w[0] + p[:, 1:-1] * w[1] + p[:, 2:] * w[2] + bias


def conv_ffn(h, w_up, conv_w, conv_b, w_down):
    u = dwconv_centred(h @ w_up, conv_w, conv_b)
    val, gate = jnp.split(u, 2, axis=-1)
    return (jax.nn.silu(gate) * val) @ w_down


def setup_inputs(seed: int = 0) -> dict:
    key = jax.random.key(seed)
    ks = iter(jax.random.split(key, 32))
    D = D_MODEL

    def nrm(shape, scale=1.0):
        return jax.random.normal(next(ks), shape, jnp.float32) * scale

    def gain(shape):
        return 1.0 + nrm(shape, 0.02)

    la, lb, lc, ld = (len(range(m, DEPTH, N_MIXERS)) for m in range(N_MIXERS))
    gqa_in = (GQA_HEADS + 2 * GQA_KV_HEADS) * GQA_HEAD_DIM
    mla_in = MLA_Q_LORA + MLA_KV_LORA + MLA_ROPE_DIM
    return {
        'x': nrm((BATCH, SEQ, D)),
        'c': nrm((BATCH, D)),
        'ctx': nrm((BATCH, CTX_LEN, D)),
        'c_ctx': nrm((D,)),
        'ada_w': nrm((DEPTH, D, 6 * D), 0.5 * D ** -0.5),
        'ada_b': nrm((DEPTH, 6 * D), 0.02),
        'norm1_g': gain((DEPTH, D)),
        'norm2_g': gain((DEPTH, D)),
        'ffn_w_up': nrm((DEPTH, D, 2 * FFN_DIM), D ** -0.5),
        'ffn_conv_w': nrm((DEPTH, 3, 2 * FFN_DIM), 3 ** -0.5),
        'ffn_conv_b': nrm((DEPTH, 2 * FFN_DIM), 0.02),
        'ffn_w_down': nrm((DEPTH, FFN_DIM, D), FFN_DIM ** -0.5),
        'gqa_w_in': nrm((la, D, gqa_in), D ** -0.5),
        'gqa_q_norm_g': gain((la, GQA_HEAD_DIM)),
        'gqa_k_norm_g': gain((la, GQA_HEAD_DIM)),
        'gqa_w_out': nrm((la, GQA_HEADS * GQA_HEAD_DIM, D), (GQA_HEADS * GQA_HEAD_DIM) ** -0.5),
        'mla_w_in': nrm((lb, D, mla_in), D ** -0.5),
        'mla_q_norm_g': gain((lb, MLA_Q_LORA)),
        'mla_kv_norm_g': gain((lb, MLA_KV_LORA)),
        'mla_w_uq': nrm((lb, MLA_Q_LORA, MLA_HEADS * (MLA_NOPE_DIM + MLA_ROPE_DIM)), MLA_Q_LORA ** -0.5),
        'mla_w_ukv': nrm((lb, MLA_KV_LORA, MLA_HEADS * (MLA_NOPE_DIM + MLA_V_DIM)), MLA_KV_LORA ** -0.5),
        'mla_w_out': nrm((lb, MLA_HEADS * MLA_V_DIM, D), (MLA_HEADS * MLA_V_DIM) ** -0.5),
        'diff_w_in': nrm((lc, D, 3 * D), D ** -0.5),
        'diff_lambda': nrm((lc, 4, DIFF_HEAD_DIM), 0.1),
        'diff_subln_g': gain((lc, 2 * DIFF_HEAD_DIM)),
        'diff_w_out': nrm((lc, D, D), D ** -0.5),
        'na_w_in': nrm((ld, D, 3 * D), D ** -0.5),
        'na_rpb': nrm((ld, NA_HEADS, 2 * NA_ROWS - 1, 2 * NA_COLS - 1), 0.05),
        'na_w_out': nrm((ld, D, D), D ** -0.5),
        'final_norm_g': gain((D,)),
    }


def reference(x, c, ctx, c_ctx, ada_w, ada_b, norm1_g, norm2_g, ffn_w_up, ffn_conv_w, ffn_conv_b, ffn_w_down,
              gqa_w_in, gqa_q_norm_g, gqa_k_norm_g, gqa_w_out,
              mla_w_in, mla_q_norm_g, mla_kv_norm_g, mla_w_uq, mla_w_ukv, mla_w_out,
              diff_w_in, diff_lambda, diff_subln_g, diff_w_out,
              na_w_in, na_rpb, na_w_out, final_norm_g):
    silu_c = jax.nn.silu(c)
    silu_cc = jax.nn.silu(c_ctx)
    for i in range(DEPTH):
        ctx_out = i < DEPTH - 1
        m, j = i % N_MIXERS, i // N_MIXERS
        sh1, sc1, g1, sh2, sc2, g2 = (a[:, None, :] for a in jnp.split(silu_c @ ada_w[i] + ada_b[i], 6, axis=-1))
        csh1, csc1, cg1, csh2, csc2, cg2 = jnp.split(silu_cc @ ada_w[i] + ada_b[i], 6, axis=-1)
        hx = norm_modulate(x, norm1_g[i], sh1, sc1)
        hc = norm_modulate(ctx, norm1_g[i], csh1, csc1)
        if m == 0:
            ox, oc = gqa_mixer(hx, hc, gqa_w_in[j], gqa_q_norm_g[j], gqa_k_norm_g[j], gqa_w_out[j], ctx_out)
        elif m == 1:
            ox, oc = mla_mixer(hx, hc, mla_w_in[j], mla_q_norm_g[j], mla_kv_norm_g[j], mla_w_uq[j],
                               mla_w_ukv[j], mla_w_out[j], ctx_out)
        elif m == 2:
            ox, oc = diff_mixer(hx, hc, diff_w_in[j], diff_lambda[j], diff_subln_g[j], diff_w_out[j], i, ctx_out)
        else:
            ox, oc = na_mixer(hx, hc, na_w_in[j], na_rpb[j], na_w_out[j], ctx_out)
        x = x + g1 * ox
        x = x + g2 * conv_ffn(norm_modulate(x, norm2_g[i], sh2, sc2),
                              ffn_w_up[i], ffn_conv_w[i], ffn_conv_b[i], ffn_w_down[i])
        if ctx_out:
            ctx = ctx + cg1 * oc
            ctx = ctx + cg2 * conv_ffn(norm_modulate(ctx, norm2_g[i], csh2, csc2),
                                       ffn_w_up[i], ffn_conv_w[i], ffn_conv_b[i], ffn_w_down[i])
    return rms_norm(x, final_norm_g)
```

```python
import math
import numpy as np
import concourse.bass as bass
import concourse.mybir as mybir
from concourse.bass_utils import run_bass_kernel_spmd

F32 = mybir.dt.float32
BF16 = mybir.dt.bfloat16
AF = mybir.ActivationFunctionType
ALU = mybir.AluOpType

D = 1024
S = 8192
C = 256
T = S + C
GRID_W = 64
DEPTH = 4
FFN = 2816
EPS = 1e-6
NEG = -1e30
NCORES = 4

CHUNKS = [(i * 512, 512) for i in range(S // 512)] + [(S, C)]


class Dep:
    __slots__ = ("sem", "val", "key", "eng", "lval")

    def __init__(self, sem, val, key, eng, lval):
        self.sem, self.val, self.key, self.eng, self.lval = sem, val, key, eng, lval


class Buf:
    __slots__ = ("w", "r", "multi", "name")

    def __init__(self, name="", multi=False):
        self.w = {}
        self.r = {}
        self.multi = multi
        self.name = name


class Sched:
    NRING = 8
    EPOCH = 30000
    DEPOCH = 1800

    def __init__(self, nc, stack):
        self.nc = nc
        self.stack = stack
        self.h = {"pe": nc.tensor, "act": nc.scalar, "dve": nc.vector, "pool": nc.gpsimd, "sp": nc.sync}
        self.sems = {}
        self.cnt = {}
        self.known = {}
        for e in self.h:
            self.sems[e] = []
            self.cnt[e] = 0
            self.known[e] = {}
        self.ring = {}
        for q in ("sp", "pool", "act"):
            self.ring[q] = {"sems": [[] for _ in range(self.NRING)], "cnt": [0] * self.NRING, "n": 0}
        self.latest = {}
        self.nsem = 0

    def _sem(self, lst, ep, name):
        while len(lst) <= ep:
            self.nsem += 1
            lst.append(self.stack.enter_context(self.nc.semaphore(f"{name}_{len(lst)}")))
        return lst[ep]

    def _wait(self, eng, d):
        kn = self.known[eng]
        if kn.get(d.key, 0) >= d.val:
            return
        self.h[eng].wait_ge(d.sem, d.lval)
        kn[d.key] = d.val

    def _waits(self, eng, reads, writes):
        deps = {}

        def add(d):
            o = deps.get(d.key)
            if o is None or o.val < d.val:
                deps[d.key] = d

        for b in reads:
            for d in b.w.values():
                add(d)
        for b in writes:
            for d in b.r.values():
                if d.eng == eng and d.key == eng:
                    continue
                add(d)
            if not b.multi:
                for d in b.w.values():
                    if d.eng == eng and d.key == eng:
                        continue
                    add(d)
        for d in deps.values():
            if d.key == "pe" and eng == "pe":
                continue
            self._wait(eng, d)

    def _record(self, me, reads, writes):
        for b in reads:
            b.r[me.key] = me
        for b in writes:
            if b.multi:
                b.w[me.key] = me
            else:
                b.w = {me.key: me}
                b.r = {}
        self.latest[me.key] = me

    def op(self, eng, fn, reads=(), writes=()):
        self._waits(eng, reads, writes)
        ins = fn(self.h[eng])
        c = self.cnt[eng]
        ep, lv = c // self.EPOCH, c % self.EPOCH + 1
        sem = self._sem(self.sems[eng], ep, "s_" + eng)
        self.cnt[eng] = c + 1
        ins.then_inc(sem, 1)
        me = Dep(sem, c + 1, eng, eng, lv)
        self._record(me, reads, writes)
        return ins

    def dma(self, q, out, in_, reads=(), writes=()):
        self._waits(q, reads, writes)
        rg = self.ring[q]
        i = rg["n"] % self.NRING
        rg["n"] += 1
        c = rg["cnt"][i]
        ep, lv = c // self.DEPOCH, (c % self.DEPOCH + 1) * 16
        sem = self._sem(rg["sems"][i], ep, f"d_{q}{i}")
        rg["cnt"][i] = c + 1
        self.h[q].dma_start(out=out, in_=in_).then_inc(sem, 16)
        me = Dep(sem, c + 1, (q, i), q, lv)
        self._record(me, reads, writes)

    def barrier(self):
        for e in self.h:
            for d in self.latest.values():
                if d.key == e:
                    continue
                self._wait(e, d)


def _rope_tables(rot_dim, nrep, scale=1.0):
    t = np.arange(S)
    row = (t // GRID_W).astype(np.float32)
    col = (t % GRID_W).astype(np.float32)
    half = rot_dim // 2
    inv = (10000.0 ** (-np.arange(0, half, 2, dtype=np.float32) / half)).astype(np.float32)
    ar, ac = row[:, None] * inv, col[:, None] * inv
    ang = np.concatenate([ar, ar, ac, ac], axis=-1)
    cos = np.ones((T, rot_dim), np.float32)
    sin = np.zeros((T, rot_dim), np.float32)
    cos[:S] = np.cos(ang)
    sin[:S] = np.sin(ang)
    cosT = np.tile(cos.T, (nrep, 1)).astype(np.float32)
    sinT = np.tile(sin.T, (nrep, 1)).astype(np.float32)
    return np.ascontiguousarray(cosT), np.ascontiguousarray(sinT)


def _rot_matrix(rot_dim, nrep, offset=0, total=None):
    n = nrep * rot_dim + offset if total is None else total
    m = np.zeros((n, n), np.float32)
    q = rot_dim // 4
    for r in range(nrep):
        b = offset + r * rot_dim
        for i in range(q):
            m[b + q + i, b + i] = -1.0
            m[b + i, b + q + i] = 1.0
            m[b + 3 * q + i, b + 2 * q + i] = -1.0
            m[b + 2 * q + i, b + 3 * q + i] = 1.0
    return m


class Builder:
    def __init__(self, layers=(0, 1, 2, 3), debug=(), stop=None):
        from contextlib import ExitStack
        self.stop = stop
        self.layers = layers
        self.debug = debug
        self.stack = ExitStack()
        nc = self.nc = bass.Bass("TRN2", target_bir_lowering=False)
        self.sc = Sched(nc, self.stack)
        self.din = {}
        self.bufs = {}

    def inp(self, name, shape, dt=F32):
        t = self.nc.dram_tensor(name, list(shape), dt, kind="ExternalInput").ap()
        self.din[name] = t
        self.bufs[name] = Buf(name, multi=True)
        return t

    def scratch(self, name, shape, dt, kind="Internal"):
        t = self.nc.dram_tensor(name, list(shape), dt, kind=kind).ap()
        self.bufs[name] = Buf(name, multi=True)
        return t

    def sb(self, ph, name, shape, dt):
        self._uid = getattr(self, "_uid", 0) + 1
        t = ph.enter_context(self.nc.sbuf_tensor(f"sb{self._uid}_{name}", list(shape), dt))
        return t

    def load_cast(self, ph, name, src_ap, shape, q="sp", cast_eng="pool", piece=2048):
        sc = self.sc
        dst = self.sb(ph, name, shape, BF16)
        dbuf = Buf(name, multi=True)
        a, b = shape[1], shape[2]
        if not hasattr(self, "_stg") or self._stg_ph is not ph:
            self._stg = [self.sb(ph, f"stg{i}", [128, piece], F32) for i in range(3)]
            self._stgb = [Buf(f"stg{i}") for i in range(3)]
            self._stg_ph = ph
            self._stg_i = 0
        for ai in range(a):
            for b0 in range(0, b, piece):
                w = min(piece, b - b0)
                i = self._stg_i % 3
                self._stg_i += 1
                st, sb_ = self._stg[i], self._stgb[i]
                sc.dma(q, st[:, 0:w], src_ap[:, ai, b0:b0 + w], reads=[], writes=[sb_])
                sc.op(cast_eng, lambda e, st=st, w=w, ai=ai, b0=b0: e.tensor_copy(out=dst[:, ai, b0:b0 + w], in_=st[:, 0:w]),
                      reads=[sb_], writes=[dbuf])
        return dst, dbuf

    def phase(self):
        from contextlib import ExitStack
        return ExitStack()

    def end_phase(self):
        self.sc.barrier()
        for b in self.bufs.values():
            b.w = {}
            b.r = {}
        self._stg_ph = None

    def consts(self):
        nc, sc, st = self.nc, self.sc, self.stack
        self.ps = st.enter_context(nc.psum_tensor("ps", [128, 8, 512], F32))
        self.ones_f = st.enter_context(nc.sbuf_tensor("c_ones_f", [128, 128], F32))
        self.ones_b = st.enter_context(nc.sbuf_tensor("c_ones_b", [128, 128], BF16))
        self.bd64 = st.enter_context(nc.sbuf_tensor("c_bd64", [128, 128], F32))
        self.mhalf = st.enter_context(nc.sbuf_tensor("c_mhalf", [128, 512], F32))
        self.modv = st.enter_context(nc.sbuf_tensor("c_modv", [128, 2, 48], F32))
        self.lvec = st.enter_context(nc.sbuf_tensor("c_lvec", [128, NV], F32))
        self.scv = st.enter_context(nc.sbuf_tensor("c_scv", [128, 8, 2], F32))
        self.cb = Buf("consts")
        self.modb = Buf("modv")
        self.lvb = Buf("lvec")
        self.scb = Buf("scv")
        sc.op("dve", lambda e: e.memset(self.ones_f[:], 1.0), writes=[self.cb])
        sc.op("dve", lambda e: e.memset(self.ones_b[:], 1.0), writes=[self.cb])
        sc.op("dve", lambda e: e.memset(self.bd64[:], 0.0), writes=[self.cb])
        sc.op("dve", lambda e: e.memset(self.bd64[0:64, 0:64], 1.0), writes=[self.cb])
        sc.op("dve", lambda e: e.memset(self.bd64[64:128, 64:128], 1.0), writes=[self.cb])
        sc.op("dve", lambda e: e.memset(self.mhalf[:], -0.5), writes=[self.cb])
        sc.dma("sp", self.scv[:], self.din["cfm"], writes=[self.scb])
        sc.op("act", lambda e: e.activation(out=self.scv[:], in_=self.scv[:], func=AF.Silu), reads=[self.scb], writes=[self.scb])

    def mod_phase(self, l):
        nc, sc = self.nc, self.sc
        with self.phase() as ph:
            sc.dma("sp", self.lvec[:], self.din["lvec"][l], writes=[self.lvb])
            wst = [self.sb(ph, f"adaw{i}", [128, 8, 512], F32) for i in range(2)]
            wb = [Buf(f"adaw{i}") for i in range(2)]
            mps = self.ps[:, 0, 0:96].rearrange("p (g s) -> p g s", s=2)
            mpb = Buf("modps")
            for pc in range(12):
                i = pc % 2
                sc.dma("sp" if pc % 2 == 0 else "pool", wst[i][:], self.din["adaw"][l, :, :, pc * 512:(pc + 1) * 512], writes=[wb[i]])
                for g4 in range(4):
                    g = pc * 4 + g4
                    for kc in range(8):
                        sc.op("pe", lambda e, i=i, g4=g4, kc=kc, g=g: e.matmul(
                            mps[:, g, :], lhsT=wst[i][:, kc, g4 * 128:(g4 + 1) * 128], rhs=self.scv[:, kc, :],
                            start=(kc == 0), stop=(kc == 7)), reads=[wb[i], self.scb], writes=[mpb])
            mod = self.sb(ph, "modraw", [128, 2, 48], F32)
            mb = Buf("modraw")
            for s_ in range(2):
                sc.op("dve", lambda e, s_=s_: e.tensor_tensor(out=mod[:, s_, :], in0=mps[:, :, s_], in1=self.lvec[:, LV_ADAB:LV_ADAB + 48], op=ALU.add),
                      reads=[mpb, self.lvb], writes=[mb])
            for s_ in range(2):
                for which, (shc, scc, gc, ng) in enumerate(((0, 8, 16, LV_N1G), (24, 32, 40, LV_N2G))):
                    o = which * 24
                    sc.op("dve", lambda e, s_=s_, scc=scc, ng=ng, o=o: e.scalar_tensor_tensor(
                        out=self.modv[:, s_, o:o + 8], in0=mod[:, s_, scc:scc + 8], scalar=1.0, in1=self.lvec[:, ng:ng + 8],
                        op0=ALU.add, op1=ALU.mult), reads=[mb, self.lvb], writes=[self.modb])
                    sc.op("dve", lambda e, s_=s_, shc=shc, o=o: e.tensor_copy(out=self.modv[:, s_, o + 8:o + 16], in_=mod[:, s_, shc:shc + 8]),
                          reads=[mb], writes=[self.modb])
                    sc.op("dve", lambda e, s_=s_, gc=gc, o=o: e.tensor_copy(out=self.modv[:, s_, o + 16:o + 24], in_=mod[:, s_, gc:gc + 8]),
                          reads=[mb], writes=[self.modb])
            self.end_phase()

    def norm_mod(self, xc, xb, w, stream, which, hT, hb, tmp, psb):
        sc = self.sc
        o = which * 24
        sq, sqb, r, rb, h1, h1b = tmp["sq"], tmp["sqb"], tmp["r"], tmp["rb"], tmp["h1"], tmp["h1b"]
        bank, pb = psb
        sc.op("pool", lambda e: e.tensor_tensor(out=sq[:, :, 0:w], in0=xc[:, :, 0:w], in1=xc[:, :, 0:w], op=ALU.mult), reads=[xb], writes=[sqb])
        for kc in range(8):
            sc.op("pe", lambda e, kc=kc: e.matmul(self.ps[:, bank, 0:w], lhsT=self.ones_f[:], rhs=sq[:, kc, 0:w], start=(kc == 0), stop=(kc == 7)),
                  reads=[sqb, self.cb], writes=[pb])
        sc.op("dve", lambda e: e.tensor_scalar(out=r[:, 0:w], in0=self.ps[:, bank, 0:w], scalar1=1.0 / D, scalar2=EPS, op0=ALU.mult, op1=ALU.add),
              reads=[pb], writes=[rb])
        sc.op("pool", lambda e: e.tensor_tensor(out=r[:, 0:w], in0=r[:, 0:w], in1=self.mhalf[:, 0:w], op=ALU.pow), reads=[rb, self.cb], writes=[rb])
        if hT is None:
            return
        A = self.modv[:, stream, o:o + 8].unsqueeze(2).to_broadcast([128, 8, w])
        Bv = self.modv[:, stream, o + 8:o + 16].unsqueeze(2).to_broadcast([128, 8, w])
        rbc = r[:, 0:w].unsqueeze(1).to_broadcast([128, 8, w])
        sc.op("dve", lambda e: e.tensor_tensor(out=h1[:, :, 0:w], in0=xc[:, :, 0:w], in1=A, op=ALU.mult), reads=[xb, self.modb], writes=[h1b])
        sc.op("dve", lambda e: e.tensor_tensor(out=h1[:, :, 0:w], in0=h1[:, :, 0:w], in1=rbc, op=ALU.mult), reads=[h1b, rb], writes=[h1b])
        sc.op("pool", lambda e: e.tensor_tensor(out=hT[:, :, 0:w], in0=h1[:, :, 0:w], in1=Bv, op=ALU.add), reads=[h1b, self.modb], writes=[hb])

    def norm_tmp(self, ph):
        return {"sq": self.sb(ph, "nsq", [128, 8, 512], F32), "sqb": Buf("nsq"),
                "r": self.sb(ph, "nr", [128, 512], F32), "rb": Buf("nr"),
                "h1": self.sb(ph, "nh1", [128, 8, 512], F32), "h1b": Buf("nh1")}

    def xres_ap(self, t0, w):
        return self.xres[:, :, t0:t0 + w].rearrange("k p t -> p k t")

    def qk_post(self, ph_t, src_bank, srcb, w, gain_ap, normalize, rope, cs, dst_ap, dstb_name, rows=128):
        sc = self.sc
        t = ph_t
        i = t["i"] % 2
        t["i"] += 1
        qs, qsb = t["qs"][i], t["qsb"][i]
        q2, q2b = t["q2"][i], t["q2b"][i]
        r2, r2b = t["r2"][i], t["r2b"][i]
        qn, qnb = t["qn"][i], t["qnb"][i]
        qf, qfb = t["qf"][i], t["qfb"][i]
        R = slice(0, rows)
        if not normalize and not rope:
            sc.op("act", lambda e: e.activation(out=qf[R, 0:w], in_=self.ps[R, src_bank, 0:w], func=AF.Copy), reads=[srcb], writes=[qfb])
            dsts = dst_ap if isinstance(dst_ap, list) else [(dst_ap, slice(0, rows))]
            for (d_ap, rs) in dsts:
                sc.dma("pool", d_ap, qf[rs, 0:w], reads=[qfb], writes=[self.bufs[dstb_name]])
            return
        sc.op("act", lambda e: e.activation(out=qs[R, 0:w], in_=self.ps[R, src_bank, 0:w], func=AF.Copy), reads=[srcb], writes=[qsb])
        cur, curb = qs, qsb
        if normalize:
            sc.op("dve", lambda e: e.tensor_tensor(out=q2[R, 0:w], in0=qs[R, 0:w], in1=qs[R, 0:w], op=ALU.mult), reads=[qsb], writes=[q2b])
            sc.op("pe", lambda e: e.matmul(self.ps[R, 3, 0:w], lhsT=self.bd64[R, R], rhs=q2[R, 0:w], start=True, stop=True), reads=[q2b, self.cb], writes=[t["ms2b"]])
            sc.op("dve", lambda e: e.tensor_scalar(out=r2[R, 0:w], in0=self.ps[R, 3, 0:w], scalar1=1.0 / 64, scalar2=EPS, op0=ALU.mult, op1=ALU.add),
                  reads=[t["ms2b"]], writes=[r2b])
            sc.op("pool", lambda e: e.tensor_tensor(out=r2[R, 0:w], in0=r2[R, 0:w], in1=self.mhalf[R, 0:w], op=ALU.pow), reads=[r2b, self.cb], writes=[r2b])
            sc.op("dve", lambda e: e.scalar_tensor_tensor(out=qn[R, 0:w], in0=qs[R, 0:w], scalar=gain_ap, in1=r2[R, 0:w], op0=ALU.mult, op1=ALU.mult),
                  reads=[qsb, r2b, t["gb"]], writes=[qnb])
            cur, curb = qn, qnb
        if rope:
            rm, cos, sin, csb = cs
            sc.op("pe", lambda e: e.matmul(self.ps[R, 4, 0:w], lhsT=rm[R, R], rhs=cur[R, 0:w], start=True, stop=True), reads=[curb, t["gb"]], writes=[t["rotb"]])
            sc.op("dve", lambda e: e.tensor_tensor(out=q2[R, 0:w], in0=cur[R, 0:w], in1=cos[R, 0:w], op=ALU.mult), reads=[curb, csb], writes=[q2b])
            sc.op("dve", lambda e: e.tensor_tensor(out=r2[R, 0:w], in0=self.ps[R, 4, 0:w], in1=sin[R, 0:w], op=ALU.mult), reads=[t["rotb"], csb], writes=[r2b])
            sc.op("pool", lambda e: e.tensor_tensor(out=qf[R, 0:w], in0=q2[R, 0:w], in1=r2[R, 0:w], op=ALU.add), reads=[q2b, r2b], writes=[qfb])
        else:
            sc.op("pool", lambda e: e.tensor_copy(out=qf[R, 0:w], in_=cur[R, 0:w]), reads=[curb], writes=[qfb])
        dsts = dst_ap if isinstance(dst_ap, list) else [(dst_ap, slice(0, rows))]
        for (d_ap, rs) in dsts:
            sc.dma("pool", d_ap, qf[rs, 0:w], reads=[qfb], writes=[self.bufs[dstb_name]])

    def qk_tmp(self, ph):
        t = {"i": 0, "ms2b": Buf("ms2"), "rotb": Buf("rot"), "gb": Buf("gains")}
        for nm, dt in (("qs", F32), ("q2", F32), ("r2", F32), ("qn", F32), ("qf", BF16)):
            t[nm] = [self.sb(ph, f"{nm}{i}", [128, 512], dt) for i in range(2)]
            t[nm + "b"] = [Buf(f"{nm}{i}") for i in range(2)]
        return t

    def proj_group(self, wt, wtb, col0, ncols, hT, hb, w, bank, pb, nk=8):
        sc = self.sc
        for kc in range(nk):
            sc.op("pe", lambda e, kc=kc: e.matmul(self.ps[0:ncols, bank, 0:w], lhsT=wt[:, kc, col0:col0 + ncols], rhs=hT[:, kc, 0:w],
                                                   start=(kc == 0), stop=(kc == nk - 1)), reads=[wtb, hb], writes=[pb])

    def v_tokmajor(self, wt, wtb, col0, ncols, hT, hb, t0, w, vt_ring, nheads, hd, ones_col, nk=8):
        sc = self.sc
        vw = hd + (1 if ones_col else 0)
        for ti in range(w // 128):
            vt, vb = vt_ring[self._vi % 2]
            self._vi += 1
            for n0 in range(0, ncols, 512):
                nn = min(512, ncols - n0)
                bank = 5 + n0 // 512
                for kc in range(nk):
                    sc.op("pe", lambda e, kc=kc, n0=n0, nn=nn, bank=bank: e.matmul(
                        self.ps[:, bank, 0:nn], lhsT=hT[:, kc, ti * 128:(ti + 1) * 128], rhs=wt[:, kc, col0 + n0:col0 + n0 + nn],
                        start=(kc == 0), stop=(kc == nk - 1)), reads=[wtb, hb], writes=[self.vpb[bank - 5]])
                h0 = n0 // hd
                nh = nn // hd
                sc.op("act", lambda e, bank=bank, nn=nn, h0=h0, nh=nh: e.activation(
                    out=vt[:, h0:h0 + nh, 0:hd], in_=self.ps[:, bank, 0:nn].rearrange("p (h d) -> p h d", d=hd), func=AF.Copy),
                    reads=[self.vpb[bank - 5]], writes=[vb])
            blk = (t0 + ti * 128) // 128
            sc.dma("sp", self.Vd[blk, :, 0:nheads * vw], vt[:, 0:nheads, 0:vw].rearrange("p h d -> p (h d)") if False else vt[:, 0:nheads, 0:vw],
                   reads=[vb], writes=[self.bufs["Vd"]])

    def v_tokmajor(self, wt, wtb, col0, ncols, hT, hb, t0, w, vt_ring, nheads, hd, ones_col, nk=8):
        sc = self.sc
        vw = hd + (1 if ones_col else 0)
        for ti in range(w // 128):
            vt, vb = vt_ring[self._vi % 2]
            self._vi += 1
            vt3 = vt[:, 0:nheads * vw].rearrange("p (h d) -> p h d", d=vw)
            for n0 in range(0, ncols, 512):
                nn = min(512, ncols - n0)
                bank = 5 + n0 // 512
                for kc in range(nk):
                    sc.op("pe", lambda e, kc=kc, n0=n0, nn=nn, bank=bank, ti=ti: e.matmul(
                        self.ps[:, bank, 0:nn], lhsT=hT[:, kc, ti * 128:(ti + 1) * 128], rhs=wt[:, kc, col0 + n0:col0 + n0 + nn],
                        start=(kc == 0), stop=(kc == nk - 1)), reads=[wtb, hb], writes=[self.vpb[bank - 5]])
                h0 = n0 // hd
                nh = nn // hd
                sc.op("act", lambda e, bank=bank, nn=nn, h0=h0, nh=nh, vt3=vt3: e.activation(
                    out=vt3[:, h0:h0 + nh, 0:hd], in_=self.ps[:, bank, 0:nn].rearrange("p (h d) -> p h d", d=hd), func=AF.Copy),
                    reads=[self.vpb[bank - 5]], writes=[vb])
            blk = (t0 + ti * 128) // 128
            sc.dma("sp", self.Vd[blk, :, 0:nheads * vw], vt[:, 0:nheads * vw], reads=[vb], writes=[self.bufs["Vd"]])

    def vt_ring(self, ph, nheads, hd, ones_col):
        sc = self.sc
        vw = hd + (1 if ones_col else 0)
        ring = []
        for i in range(2):
            vt = self.sb(ph, f"vt{i}", [128, 1040], BF16)
            vb = Buf(f"vt{i}")
            if ones_col:
                sc.op("dve", lambda e, vt=vt: e.memset(vt[:, 0:nheads * vw], 1.0), writes=[vb])
            ring.append((vt, vb))
        self._vi = 0
        self.vpb = [Buf("vps0"), Buf("vps1")]
        return ring

    def chunk_front(self, ph, tmp, xr, hr, ci, t0, w, which=0):
        sc = self.sc
        stream = 0 if t0 < S else 1
        xc, xb = xr[ci % 2]
        hT, hb = hr[ci % 2]
        sc.dma("sp", xc[:, :, 0:w], self.xres_ap(t0, w), reads=[self.bufs["xres"]], writes=[xb])
        self.norm_mod(xc, xb, w, stream, which, hT, hb, tmp, (0, self.nps))
        return hT, hb, xc, xb

    def ring2(self, ph, name, shape, dt):
        return [(self.sb(ph, f"{name}{i}", shape, dt), Buf(f"{name}{i}")) for i in range(2)]

    def p1_gqa(self, l):
        sc = self.sc
        with self.phase() as ph:
            win, winb = self.load_cast(ph, "win", self.din["gqa_win"], [128, 8, 1536])
            tmp = self.norm_tmp(ph)
            t = self.qk_tmp(ph)
            gq = self.sb(ph, "gq", [128, 2], F32)
            rm = self.sb(ph, "rm", [128, 128], F32)
            sc.dma("sp", gq[:], self.din["gqa_g"], writes=[t["gb"]])
            sc.dma("sp", rm[:], self.din["rm64"], writes=[t["gb"]])
            xr = self.ring2(ph, "xc", [128, 8, 512], F32)
            hr = self.ring2(ph, "hT", [128, 8, 512], BF16)
            cr = self.ring2(ph, "cs", [128, 2, 512], F32)
            vr = self.vt_ring(ph, 4, 64, True)
            self.nps = Buf("nps")
            pbs = [Buf("pp1"), Buf("pp2")]
            for ci, (t0, w) in enumerate(CHUNKS):
                hT, hb, xc, xb = self.chunk_front(ph, tmp, xr, hr, ci, t0, w)
                cs, csb = cr[ci % 2]
                sc.dma("sp", cs[:, 0, 0:w], self.din["cos64"][:, t0:t0 + w], writes=[csb])
                sc.dma("sp", cs[:, 1, 0:w], self.din["sin64"][:, t0:t0 + w], writes=[csb])
                for g in range(10):
                    bank = 1 + g % 2
                    self.proj_group(win, winb, g * 128, 128, hT, hb, w, bank, pbs[g % 2])
                    dst = (self.QT[g, :, t0:t0 + w], "QT") if g < 8 else (self.KT[g - 8, :, t0:t0 + w], "KT")
                    self.qk_post(t, bank, pbs[g % 2], w, gq[:, 0:1] if g < 8 else gq[:, 1:2], True, True,
                                 (rm, cs[:, 0, :], cs[:, 1, :], csb), dst[0], dst[1])
                self.v_tokmajor(win, winb, 1280, 256, hT, hb, t0, w, vr, 4, 64, True)
            self.end_phase()

    def att_std(self, nheads, dqk, qsrc, ksrc, vcol, scale):
        sc = self.sc
        with self.phase() as ph:
            qr = self.ring2(ph, "aq", [128, T], BF16)
            kr = self.ring2(ph, "ak", [128, T], BF16)
            vr = self.ring2(ph, "av", [128, 66, 65], BF16)
            pr = [(self.sb(ph, f"pT{i}", [128, 2, 512], BF16), Buf(f"pT{i}")) for i in range(3)]
            our = self.ring2(ph, "ou", [64, 512], F32)
            onr = self.ring2(ph, "on", [64, 512], BF16)
            rd = self.sb(ph, "rd", [128, 512], F32)
            rdb = Buf("rd")
            sb_ = [Buf(f"S{i}") for i in range(3)]
            accb, bcb = Buf("acc"), Buf("bc")
            pc = 0
            fi = 0
            for h in range(nheads):
                (qt, qb), (kt, kb_), (vt, vb) = qr[h % 2], kr[h % 2], vr[h % 2]
                sc.dma("sp", qt[0:dqk, :], qsrc(h), reads=[self.bufs["QT"]], writes=[qb])
                sc.dma("sp", kt[0:dqk, :], ksrc(h), reads=[self.bufs["KT"]], writes=[kb_])
                sc.dma("sp", vt[:], self.Vd[:, :, vcol(h):vcol(h) + 65].rearrange("b p d -> p b d"), reads=[self.bufs["Vd"]], writes=[vb])
                for (t0, w) in CHUNKS:
                    kbs = list(range(66)) if t0 < S else [64, 65]
                    pairs = [kbs[i:i + 2] for i in range(0, len(kbs), 2)]
                    for pi, pr_ in enumerate(pairs):
                        s3 = pc % 3
                        pc += 1
                        sbank = 2 * s3
                        n = len(pr_)
                        for j, kb in enumerate(pr_):
                            sc.op("pe", lambda e, j=j, kb=kb, sbank=sbank: e.matmul(
                                self.ps[:, sbank + j, 0:w], lhsT=kt[0:dqk, kb * 128:(kb + 1) * 128], rhs=qt[0:dqk, t0:t0 + w], start=True, stop=True),
                                reads=[qb, kb_], writes=[sb_[s3]])
                        pT, pb = pr[s3]
                        sc.op("act", lambda e, sbank=sbank, n=n, pT=pT: e.activation(
                            out=pT[:, 0:n, 0:w], in_=self.ps[:, sbank:sbank + n, 0:w], func=AF.Exp, scale=scale), reads=[sb_[s3]], writes=[pb])
                        for j, kb in enumerate(pr_):
                            first = (pi == 0 and j == 0)
                            last = (pi == len(pairs) - 1 and j == n - 1)
                            sc.op("pe", lambda e, j=j, kb=kb, pT=pT, first=first, last=last: e.matmul(
                                self.ps[0:65, 6, 0:w], lhsT=vt[:, kb, 0:65], rhs=pT[:, j, 0:w], start=first, stop=last),
                                reads=[pb, vb], writes=[accb])
                    ou, oub = our[fi % 2]
                    on, onb = onr[fi % 2]
                    fi += 1
                    sc.op("dve", lambda e, ou=ou: e.tensor_copy(out=ou[:, 0:w], in_=self.ps[0:64, 6, 0:w]), reads=[accb], writes=[oub])
                    sc.op("dve", lambda e: e.reciprocal(out=rd[64:65, 0:w], in_=self.ps[64:65, 6, 0:w]), reads=[accb], writes=[rdb])
                    sc.op("pe", lambda e: e.matmul(self.ps[0:64, 7, 0:w], lhsT=self.ones_f[64:65, 0:64], rhs=rd[64:65, 0:w], start=True, stop=True),
                          reads=[rdb, self.cb], writes=[bcb])
                    sc.op("dve", lambda e, ou=ou, on=on: e.tensor_tensor(out=on[:, 0:w], in0=ou[:, 0:w], in1=self.ps[0:64, 7, 0:w], op=ALU.mult),
                          reads=[oub, bcb], writes=[onb])
                    sc.dma("pool", self.aoT[h // 2, (h % 2) * 64:(h % 2) * 64 + 64, t0:t0 + w], on[:, 0:w], reads=[onb], writes=[self.bufs["aoT"]])
            self.end_phase()

    def out_phase(self, l, wout_name, do_ctx=True):
        sc = self.sc
        with self.phase() as ph:
            wo, wob = self.load_cast(ph, "wo", self.din[wout_name], [128, 8, 1024])
            tmp = self.norm_tmp(ph)
            xr = self.ring2(ph, "xc", [128, 8, 512], F32)
            ar = self.ring2(ph, "ao", [128, 8, 512], BF16)
            hr = self.ring2(ph, "h2", [128, 8, 512], BF16)
            self.nps = Buf("nps")
            pbs = [Buf("op1"), Buf("op2")]
            chunks = CHUNKS if do_ctx else CHUNKS[:-1]
            for ci, (t0, w) in enumerate(chunks):
                stream = 0 if t0 < S else 1
                xc, xb = xr[ci % 2]
                ao, ab = ar[ci % 2]
                h2, hb = hr[ci % 2]
                sc.dma("sp", xc[:, :, 0:w], self.xres_ap(t0, w), reads=[self.bufs["xres"]], writes=[xb])
                sc.dma("sp", ao[:, :, 0:w], self.aoT[:, :, t0:t0 + w].rearrange("k p t -> p k t"), reads=[self.bufs["aoT"]], writes=[ab])
                for og in range(8):
                    bank = 1 + og % 2
                    self.proj_group(wo, wob, og * 128, 128, ao, ab, w, bank, pbs[og % 2])
                    sc.op("dve", lambda e, og=og, bank=bank: e.scalar_tensor_tensor(
                        out=xc[:, og, 0:w], in0=self.ps[:, bank, 0:w], scalar=self.modv[:, stream, 16 + og:17 + og], in1=xc[:, og, 0:w],
                        op0=ALU.mult, op1=ALU.add), reads=[pbs[og % 2], xb, self.modb], writes=[xb])
                sc.dma("pool", self.xres_ap(t0, w), xc[:, :, 0:w], reads=[xb], writes=[self.bufs["xres"]])
                self.norm_mod(xc, xb, w, stream, 1, h2, hb, tmp, (0, self.nps))
                sc.dma("pool", self.h2T[:, :, t0:t0 + w].rearrange("k p t -> p k t"), h2[:, :, 0:w], reads=[hb], writes=[self.bufs["h2T"]])
            self.end_phase()

    def ffn_phase(self, l, do_ctx=True):
        sc = self.sc
        W = 256
        with self.phase() as ph:
            wu, wub = self.load_cast(ph, "wu", self.din["ffn_up"][l], [128, 8, 2 * FFN])
            wd, wdb = self.load_cast(ph, "wd", self.din["ffn_dn"][l], [128, 22, D])
            hr = self.ring2(ph, "fh", [128, 8, W + 2], BF16)
            gT = self.sb(ph, "gT", [128, 22, W], BF16)
            gTb = [Buf(f"gT{f}") for f in range(22)]
            ur = [(self.sb(ph, f"fu{i}", [128, 2, W], F32), Buf(f"fu{i}")) for i in range(2)]
            sgr = self.ring2(ph, "fsg", [128, W], F32)
            xor = self.ring2(ph, "fxo", [128, W], F32)
            upb = [Buf("up0"), Buf("up1"), Buf("up2"), Buf("up3")]
            dnb = [Buf("dn0"), Buf("dn1")]
            seqs = [(0, S)] + ([(S, T)] if do_ctx else [])
            chunks = [(t0, W, a, b) for (a, b) in seqs for t0 in range(a, b, W)]
            cw0 = LV_CONVW
            ui = 0
            xi = 0
            for ci, (t0, w, s0, s1) in enumerate(chunks):
                hh, hb = hr[ci % 2]
                lo = 1 if t0 > s0 else 0
                hi = 1 if t0 + w < s1 else 0
                if not lo:
                    sc.op("pool", lambda e, hh=hh: e.memset(hh[:, :, 0:1], 0.0), writes=[hb])
                if not hi:
                    sc.op("pool", lambda e, hh=hh: e.memset(hh[:, :, w + 1:w + 2], 0.0), writes=[hb])
                sc.dma("sp", hh[:, :, 1 - lo:w + 1 + hi], self.h2T[:, :, t0 - lo:t0 + w + hi].rearrange("k p t -> p k t"),
                       reads=[self.bufs["h2T"]], writes=[hb])
                for f in range(22):
                    uu, ub = ur[ui % 2]
                    ui += 1
                    for vg, grp in enumerate((f, f + 22)):
                        bank = 2 * (f % 2) + vg
                        pb = upb[bank]
                        self_ps = self.ps
                        for kc in range(8):
                            sc.op("pe", lambda e, kc=kc, grp=grp, bank=bank, hh=hh: e.matmul(
                                self_ps[:, bank, 0:w + 2], lhsT=wu[:, kc, grp * 128:(grp + 1) * 128], rhs=hh[:, kc, 0:w + 2],
                                start=(kc == 0), stop=(kc == 7)), reads=[wub, hb], writes=[pb])
                        c0 = cw0 + grp * 3
                        cbias = LV_CONVB + grp
                        sc.op("dve", lambda e, bank=bank, vg=vg, uu=uu, c0=c0, cbias=cbias: e.tensor_scalar(
                            out=uu[:, vg, 0:w], in0=self_ps[:, bank, 0:w], scalar1=self.lvec[:, c0:c0 + 1], scalar2=self.lvec[:, cbias:cbias + 1],
                            op0=ALU.mult, op1=ALU.add), reads=[pb, self.lvb], writes=[ub])
                        for tap in (1, 2):
                            sc.op("dve", lambda e, bank=bank, vg=vg, uu=uu, c0=c0, tap=tap: e.scalar_tensor_tensor(
                                out=uu[:, vg, 0:w], in0=self_ps[:, bank, tap:tap + w], scalar=self.lvec[:, c0 + tap:c0 + tap + 1], in1=uu[:, vg, 0:w],
                                op0=ALU.mult, op1=ALU.add), reads=[pb, ub, self.lvb], writes=[ub])
                    sg, sgb = sgr[f % 2]
                    sc.op("act", lambda e, uu=uu, sg=sg: e.activation(out=sg[:, 0:w], in_=uu[:, 1, 0:w], func=AF.Silu), reads=[ub], writes=[sgb])
                    sc.op("pool", lambda e, uu=uu, sg=sg, f=f: e.tensor_tensor(out=gT[:, f, 0:w], in0=sg[:, 0:w], in1=uu[:, 0, 0:w], op=ALU.mult),
                          reads=[sgb, ub], writes=[gTb[f]])
                stream = 0 if t0 < S else 1
                for og in range(8):
                    bank = 4 + og % 2
                    pb = dnb[og % 2]
                    xo, xob = xor[xi % 2]
                    xi += 1
                    sc.dma("sp", xo[:, 0:w], self.xres[og, :, t0:t0 + w], reads=[self.bufs["xres"]], writes=[xob])
                    for f in range(22):
                        sc.op("pe", lambda e, f=f, og=og, bank=bank: e.matmul(
                            self.ps[:, bank, 0:w], lhsT=wd[:, f, og * 128:(og + 1) * 128], rhs=gT[:, f, 0:w], start=(f == 0), stop=(f == 21)),
                            reads=[wdb, gTb[f]], writes=[pb])
                    sc.op("dve", lambda e, og=og, bank=bank, xo=xo: e.scalar_tensor_tensor(
                        out=xo[:, 0:w], in0=self.ps[:, bank, 0:w], scalar=self.modv[:, stream, 40 + og:41 + og], in1=xo[:, 0:w],
                        op0=ALU.mult, op1=ALU.add), reads=[pb, xob, self.modb], writes=[xob])
                    sc.dma("pool", self.xres[og, :, t0:t0 + w], xo[:, 0:w], reads=[xob], writes=[self.bufs["xres"]])
            self.end_phase()

    def final_phase(self):
        sc = self.sc
        with self.phase() as ph:
            tmp = self.norm_tmp(ph)
            xr = self.ring2(ph, "xc", [128, 8, 512], F32)
            fg = self.sb(ph, "fg", [128, 8], F32)
            fgb = Buf("fg")
            sc.dma("sp", fg[:], self.din["fng"], writes=[fgb])
            self.nps = Buf("nps")
            for ci, (t0, w) in enumerate(CHUNKS[:-1]):
                xc, xb = xr[ci % 2]
                sc.dma("sp", xc[:, :, 0:w], self.xres_ap(t0, w), reads=[self.bufs["xres"]], writes=[xb])
                self.norm_mod(xc, xb, w, 0, 0, None, None, tmp, (0, self.nps))
                h1, h1b, r, rb = tmp["h1"], tmp["h1b"], tmp["r"], tmp["rb"]
                sc.op("dve", lambda e, xc=xc: e.tensor_tensor(out=h1[:, :, 0:w], in0=xc[:, :, 0:w], in1=fg[:, 0:8].unsqueeze(2).to_broadcast([128, 8, w]), op=ALU.mult),
                      reads=[xb, fgb], writes=[h1b])
                sc.op("dve", lambda e: e.tensor_tensor(out=h1[:, :, 0:w], in0=h1[:, :, 0:w], in1=r[:, 0:w].unsqueeze(1).to_broadcast([128, 8, w]), op=ALU.mult),
                      reads=[h1b, rb], writes=[h1b])
                sc.dma("pool", self.outT[:, :, t0:t0 + w].rearrange("k p t -> p k t"), h1[:, :, 0:w], reads=[h1b], writes=[self.bufs["outT"]])
            self.end_phase()

    def build(self):
        nc, sc = self.nc, self.sc
        L = self.layers
        self.inp("xT_in", [8, 128, T])
        self.inp("cfm", [128, 8, 2])
        self.inp("adaw", [DEPTH, 128, 8, 6 * D])
        self.inp("lvec", [DEPTH, 128, NV])
        self.inp("fng", [128, 8])
        self.inp("ffn_up", [DEPTH, 128, 8, 2 * FFN])
        self.inp("ffn_dn", [DEPTH, 128, 22, D])
        self.inp("cos64", [128, T])
        self.inp("sin64", [128, T])
        self.inp("rm64", [128, 128])
        self.inp("gqa_win", [128, 8, 1536])
        self.inp("gqa_wout", [128, 8, 1024])
        self.inp("gqa_g", [128, 2])
        self.extra_inputs()
        self.xres = self.scratch("xres", [8, 128, T], F32)
        self.h2T = self.scratch("h2T", [8, 128, T], BF16)
        self.QT = self.scratch("QT", [16, 128, T], BF16)
        self.KT = self.scratch("KT", [16, 128, T], BF16)
        self.Vd = self.scratch("Vd", [66, 128, 1040], BF16)
        self.aoT = self.scratch("aoT", [8, 128, T], BF16)
        self.outT = self.scratch("outT", [8, 128, S], F32, kind="ExternalOutput")
        if "xres" in self.debug:
            self.dbg_x = self.scratch("dbg_x", [8, 128, T], F32, kind="ExternalOutput")
        self.consts()
        for k in range(8):
            sc.dma("sp" if k % 2 == 0 else "pool", self.xres[k], self.din["xT_in"][k], writes=[self.bufs["xres"]])
        self.end_phase()
        for l in L:
            last = (l == DEPTH - 1)
            self.mod_phase(l)
            if self.stop == "mod":
                break
            if l == 0:
                self.p1_gqa(l)
                if self.stop == "p1":
                    break
                self.att_std(16 if self.stop != "att1" else 1, 64,
                             lambda h: self.QT[h // 2, (h % 2) * 64:(h % 2) * 64 + 64, :],
                             lambda h: self.KT[(h // 4) // 2, ((h // 4) % 2) * 64:((h // 4) % 2) * 64 + 64, :],
                             lambda h: (h // 4) * 65, 64 ** -0.5)
                if self.stop in ("att", "att1"):
                    break
                self.out_phase(l, "gqa_wout")
                if self.stop == "out":
                    break
            elif l == 1:
                self.layer_mla(l)
            elif l == 2:
                self.layer_diff(l)
            else:
                self.layer_na(l)
            self.ffn_phase(l, do_ctx=not last)
        if "xres" in self.debug:
            for k in range(8):
                sc.dma("sp", self.dbg_x[k], self.xres[k], reads=[self.bufs["xres"]], writes=[Buf("dbg")])
            self.end_phase()
        self.final_phase()
        self.sc.barrier()
        self.stack.close()
        return nc

    def extra_inputs(self):
        pass


LV_ADAB = 0
LV_N1G = 48
LV_N2G = 56
LV_CONVB = 64
LV_CONVW = 108
NV = 108 + 132


def _fm(v):
    v = np.asarray(v, np.float32)
    return np.ascontiguousarray(v.reshape(-1, 128).T)


def _wfm(w):
    w = np.asarray(w, np.float32)
    K, N = w.shape
    return np.ascontiguousarray(w.reshape(K // 128, 128, N).transpose(1, 0, 2))


def host_inputs_core(I, b):
    m = {}
    xt = np.concatenate([np.asarray(I["x"][b], np.float32), np.asarray(I["ctx"][b], np.float32)], axis=0)
    m["xT_in"] = np.ascontiguousarray(xt.T.reshape(8, 128, T))
    cf = np.stack([_fm(I["c"][b]), _fm(I["c_ctx"])], axis=-1)
    m["cfm"] = np.ascontiguousarray(cf)
    return m


def host_inputs(inputs, b):
    I = inputs
    m = host_inputs_core(inputs, b)
    m["adaw"] = np.stack([_wfm(I["ada_w"][l]) for l in range(DEPTH)])
    lv = np.zeros((DEPTH, 128, NV), np.float32)
    for l in range(DEPTH):
        lv[l, :, LV_ADAB:LV_ADAB + 48] = _fm(I["ada_b"][l])
        lv[l, :, LV_N1G:LV_N1G + 8] = _fm(I["norm1_g"][l])
        lv[l, :, LV_N2G:LV_N2G + 8] = _fm(I["norm2_g"][l])
        lv[l, :, LV_CONVB:LV_CONVB + 44] = _fm(I["ffn_conv_b"][l])
        cw = np.stack([_fm(I["ffn_conv_w"][l][k]) for k in range(3)], axis=-1)
        lv[l, :, LV_CONVW:LV_CONVW + 132] = cw.reshape(128, 132)
    m["lvec"] = lv
    m["fng"] = _fm(I["final_norm_g"])
    m["ffn_up"] = np.stack([_wfm(I["ffn_w_up"][l]) for l in range(DEPTH)])
    m["ffn_dn"] = np.stack([_wfm(I["ffn_w_down"][l]) for l in range(DEPTH)])
    c64, s64 = _rope_tables(64, 2)
    m["cos64"], m["sin64"] = c64, s64
    m["rm64"] = _rot_matrix(64, 2)
    m["gqa_win"] = _wfm(I["gqa_w_in"][0])
    m["gqa_wout"] = _wfm(I["gqa_w_out"][0])
    m["gqa_g"] = np.ascontiguousarray(np.stack([np.tile(np.asarray(I["gqa_q_norm_g"][0], np.float32), 2),
                                                np.tile(np.asarray(I["gqa_k_norm_g"][0], np.float32), 2)], axis=-1))
    return m


def _extra_inputs(self):
    self.inp("mla_win", [128, 8, 672])
    self.inp("mla_g", [128, 5])
    self.inp("mla_wuq", [128, 3, 1536])
    self.inp("mla_wukvk", [128, 2, 1024])
    self.inp("mla_wukvv", [128, 2, 1024])
    self.inp("mla_wout", [128, 8, 1024])
    self.inp("rmq96", [96, 96])
    self.inp("cosq96", [96, T])
    self.inp("sinq96", [96, T])
    self.inp("rmk32", [32, 32])
    self.inp("cosk32", [32, T])
    self.inp("sink32", [32, T])
    self.inp("diff_win", [128, 8, 3072])
    self.inp("diff_wout", [128, 8, 1024])
    self.inp("diff_lam", [128, 256])
    self.inp("diff_subg", [128, 1])
    self.inp("na_win", [128, 8, 3072])
    self.inp("na_wout", [128, 8, 1024])
    self.inp("na_bias", [5, 16, 128, 640])


def _group_rms(self, t, srcs, srcb, w, ngr, gains, gcol0, dst, dstb, tmpq, tmpqb):
    sc = self.sc
    for g in range(ngr):
        sc.op("dve", lambda e, g=g: e.tensor_tensor(out=tmpq[:, 0:w], in0=srcs[g][:, 0:w], in1=srcs[g][:, 0:w], op=ALU.mult), reads=[srcb[g]], writes=[tmpqb])
        sc.op("pe", lambda e, g=g: e.matmul(self.ps[:, 3, 0:w], lhsT=self.ones_f[:], rhs=tmpq[:, 0:w], start=(g == 0), stop=(g == ngr - 1)),
              reads=[tmpqb, self.cb], writes=[t["ms2b"]])
    r2, r2b = t["r2"][0], t["r2b"][0]
    sc.op("dve", lambda e: e.tensor_scalar(out=r2[:, 0:w], in0=self.ps[:, 3, 0:w], scalar1=1.0 / (ngr * 128), scalar2=EPS, op0=ALU.mult, op1=ALU.add),
          reads=[t["ms2b"]], writes=[r2b])
    sc.op("pool", lambda e: e.tensor_tensor(out=r2[:, 0:w], in0=r2[:, 0:w], in1=self.mhalf[:, 0:w], op=ALU.pow), reads=[r2b, self.cb], writes=[r2b])
    for g in range(ngr):
        sc.op("dve", lambda e, g=g: e.scalar_tensor_tensor(out=dst[:, g, 0:w], in0=srcs[g][:, 0:w], scalar=gains[:, gcol0 + g:gcol0 + g + 1], in1=r2[:, 0:w],
                                                           op0=ALU.mult, op1=ALU.mult), reads=[srcb[g], r2b, t["gb"]], writes=[dstb])


def _p1_mla(self, l):
    sc = self.sc
    with self.phase() as ph:
        win, winb = self.load_cast(ph, "win", self.din["mla_win"], [128, 8, 672])
        wuq, wuqb = self.load_cast(ph, "wuq", self.din["mla_wuq"], [128, 3, 1536])
        wkk, wkkb = self.load_cast(ph, "wkk", self.din["mla_wukvk"], [128, 2, 1024])
        wkv, wkvb = self.load_cast(ph, "wkv", self.din["mla_wukvv"], [128, 2, 1024])
        tmp = self.norm_tmp(ph)
        t = self.qk_tmp(ph)
        gm = self.sb(ph, "gm", [128, 5], F32)
        rmq = self.sb(ph, "rmq", [128, 96], F32)
        rmk = self.sb(ph, "rmk", [128, 32], F32)
        sc.dma("sp", gm[:], self.din["mla_g"], writes=[t["gb"]])
        sc.dma("sp", rmq[0:96, :], self.din["rmq96"], writes=[t["gb"]])
        sc.dma("sp", rmk[0:32, :], self.din["rmk32"], writes=[t["gb"]])
        xr = self.ring2(ph, "xc", [128, 8, 512], F32)
        hr = self.ring2(ph, "hT", [128, 8, 512], BF16)
        cr = self.ring2(ph, "cs", [128, 4, 512], F32)
        vr = self.vt_ring(ph, 16, 64, True)
        cl = [self.sb(ph, f"cl{g}", [128, 512], F32) for g in range(5)]
        clb = [Buf(f"cl{g}") for g in range(5)]
        cqn = self.sb(ph, "cqn", [128, 3, 512], BF16)
        cqnb = Buf("cqn")
        ckvn = self.sb(ph, "ckvn", [128, 2, 512], BF16)
        ckvnb = Buf("ckvn")
        tq = self.sb(ph, "tq", [128, 512], F32)
        tqb = Buf("tq")
        self.nps = Buf("nps")
        pbs = [Buf("pp1"), Buf("pp2")]
        gi = 0
        for ci, (t0, w) in enumerate(CHUNKS):
            hT, hb, xc, xb = self.chunk_front(ph, tmp, xr, hr, ci, t0, w)
            cs, csb = cr[ci % 2]
            sc.dma("sp", cs[0:96, 0, 0:w], self.din["cosq96"][:, t0:t0 + w], writes=[csb])
            sc.dma("sp", cs[0:96, 1, 0:w], self.din["sinq96"][:, t0:t0 + w], writes=[csb])
            sc.dma("sp", cs[0:32, 2, 0:w], self.din["cosk32"][:, t0:t0 + w], writes=[csb])
            sc.dma("sp", cs[0:32, 3, 0:w], self.din["sink32"][:, t0:t0 + w], writes=[csb])
            for g in range(5):
                bank = 1 + gi % 2
                pb = pbs[gi % 2]
                gi += 1
                self.proj_group(win, winb, g * 128, 128, hT, hb, w, bank, pb)
                sc.op("act", lambda e, g=g, bank=bank: e.activation(out=cl[g][:, 0:w], in_=self.ps[:, bank, 0:w], func=AF.Copy), reads=[pb], writes=[clb[g]])
            _group_rms(self, t, cl[0:3], clb[0:3], w, 3, gm, 0, cqn, cqnb, tq, tqb)
            _group_rms(self, t, cl[3:5], clb[3:5], w, 2, gm, 3, ckvn, ckvnb, tq, tqb)
            bank = 1 + gi % 2
            pb = pbs[gi % 2]
            gi += 1
            self.proj_group(win, winb, 640, 32, hT, hb, w, bank, pb)
            self.qk_post(t, bank, pb, w, None, False, True, (rmk, cs[:, 2, :], cs[:, 3, :], csb),
                         [(self.KT[h, 64:96, t0:t0 + w], slice(0, 32)) for h in range(16)], "KT", rows=32)
            for h in range(16):
                bank = 1 + gi % 2
                pb = pbs[gi % 2]
                gi += 1
                self.proj_group(wuq, wuqb, h * 96, 96, cqn, cqnb, w, bank, pb, nk=3)
                self.qk_post(t, bank, pb, w, None, False, True, (rmq, cs[:, 0, :], cs[:, 1, :], csb), self.QT[h, 0:96, t0:t0 + w], "QT", rows=96)
            for g in range(8):
                bank = 1 + gi % 2
                pb = pbs[gi % 2]
                gi += 1
                self.proj_group(wkk, wkkb, g * 128, 128, ckvn, ckvnb, w, bank, pb, nk=2)
                self.qk_post(t, bank, pb, w, None, False, False, None,
                             [(self.KT[2 * g, 0:64, t0:t0 + w], slice(0, 64)), (self.KT[2 * g + 1, 0:64, t0:t0 + w], slice(64, 128))], "KT")
            self.v_tokmajor(wkv, wkvb, 0, 1024, ckvn, ckvnb, t0, w, vr, 16, 64, True, nk=2)
        self.end_phase()


def _layer_mla(self, l):
    _p1_mla(self, l)
    if self.stop == "p1":
        return
    self.att_std(16, 96, lambda h: self.QT[h, 0:96, :], lambda h: self.KT[h, 0:96, :], lambda h: h * 65, 96 ** -0.5)
    self.out_phase(l, "mla_wout")


def _p1_qkv(self, l, wname, rope, nheads_v, hd_v, ones_col):
    sc = self.sc
    with self.phase() as ph:
        win, winb = self.load_cast(ph, "win", self.din[wname], [128, 8, 3072])
        tmp = self.norm_tmp(ph)
        t = self.qk_tmp(ph)
        rm = self.sb(ph, "rm", [128, 128], F32)
        sc.dma("sp", rm[:], self.din["rm64"], writes=[t["gb"]])
        xr = self.ring2(ph, "xc", [128, 8, 512], F32)
        hr = self.ring2(ph, "hT", [128, 8, 512], BF16)
        cr = self.ring2(ph, "cs", [128, 2, 512], F32)
        vr = self.vt_ring(ph, nheads_v, hd_v, ones_col)
        self.nps = Buf("nps")
        pbs = [Buf("pp1"), Buf("pp2")]
        for ci, (t0, w) in enumerate(CHUNKS):
            hT, hb, xc, xb = self.chunk_front(ph, tmp, xr, hr, ci, t0, w)
            cs, csb = cr[ci % 2]
            if rope:
                sc.dma("sp", cs[:, 0, 0:w], self.din["cos64"][:, t0:t0 + w], writes=[csb])
                sc.dma("sp", cs[:, 1, 0:w], self.din["sin64"][:, t0:t0 + w], writes=[csb])
            for g in range(16):
                bank = 1 + g % 2
                self.proj_group(win, winb, g * 128, 128, hT, hb, w, bank, pbs[g % 2])
                dst = (self.QT[g, :, t0:t0 + w], "QT") if g < 8 else (self.KT[g - 8, :, t0:t0 + w], "KT")
                self.qk_post(t, bank, pbs[g % 2], w, None, False, rope, (rm, cs[:, 0, :], cs[:, 1, :], csb), dst[0], dst[1])
            self.v_tokmajor(win, winb, 2048, 1024, hT, hb, t0, w, vr, nheads_v, hd_v, ones_col)
        self.end_phase()


def _att_diff(self, l):
    sc = self.sc
    lam_init = 0.8 - 0.6 * math.exp(-0.3 * l)
    with self.phase() as ph:
        lam = self.sb(ph, "lam", [128, 256], F32)
        lsc = self.sb(ph, "lsc", [128, 8], F32)
        lb = Buf("lam")
        sc.dma("sp", lam[:], self.din["diff_lam"], writes=[lb])
        sc.dma("sp", lsc[:, 4:5], self.din["diff_subg"], writes=[lb])
        sc.op("dve", lambda e: e.tensor_tensor(out=lam[:, 0:64], in0=lam[:, 0:64], in1=lam[:, 64:128], op=ALU.mult), reads=[lb], writes=[lb])
        sc.op("dve", lambda e: e.tensor_tensor(out=lam[:, 128:192], in0=lam[:, 128:192], in1=lam[:, 192:256], op=ALU.mult), reads=[lb], writes=[lb])
        sc.op("dve", lambda e: e.reduce_sum(out=lsc[:, 0:1], in_=lam[:, 0:64], axis=mybir.AxisListType.X), reads=[lb], writes=[lb])
        sc.op("dve", lambda e: e.reduce_sum(out=lsc[:, 1:2], in_=lam[:, 128:192], axis=mybir.AxisListType.X), reads=[lb], writes=[lb])
        sc.op("act", lambda e: e.activation(out=lsc[:, 0:2], in_=lsc[:, 0:2], func=AF.Exp), reads=[lb], writes=[lb])
        sc.op("dve", lambda e: e.tensor_tensor(out=lsc[:, 2:3], in0=lsc[:, 1:2], in1=lsc[:, 0:1], op=ALU.subtract), reads=[lb], writes=[lb])
        sc.op("dve", lambda e: e.tensor_scalar(out=lsc[:, 2:3], in0=lsc[:, 2:3], scalar1=-lam_init, scalar2=None, op0=ALU.add), reads=[lb], writes=[lb])
        sc.op("dve", lambda e: e.tensor_scalar(out=lsc[:, 5:6], in0=lsc[:, 4:5], scalar1=1.0 - lam_init, scalar2=None, op0=ALU.mult), reads=[lb], writes=[lb])
        qr = self.ring2(ph, "aq", [128, T], BF16)
        kr = self.ring2(ph, "ak", [128, T], BF16)
        vr = self.ring2(ph, "av", [128, 66, 128], BF16)
        pr = self.ring2(ph, "pT", [128, 2, 512], BF16)
        r0 = self.sb(ph, "r0", [128, 512], F32)
        r1 = self.sb(ph, "r1", [128, 512], F32)
        o = self.sb(ph, "o", [128, 512], F32)
        o2 = self.sb(ph, "o2", [128, 512], F32)
        fb = Buf("fin")
        onr = self.ring2(ph, "on", [128, 512], BF16)
        sb_ = [Buf("S0"), Buf("S1")]
        accb = [Buf(f"acc{i}") for i in range(4)]
        pc = 0
        fi = 0
        for h in range(8):
            (qt, qb), (kt, kb_), (vt, vb) = qr[h % 2], kr[h % 2], vr[h % 2]
            sc.dma("sp", qt[:], self.QT[h], reads=[self.bufs["QT"]], writes=[qb])
            sc.dma("sp", kt[:], self.KT[h], reads=[self.bufs["KT"]], writes=[kb_])
            sc.dma("sp", vt[:], self.Vd[:, :, h * 128:(h + 1) * 128].rearrange("b p d -> p b d"), reads=[self.bufs["Vd"]], writes=[vb])
            for (t0, w) in CHUNKS:
                kbs = list(range(66)) if t0 < S else [64, 65]
                for ki, kb in enumerate(kbs):
                    s2 = pc % 2
                    pc += 1
                    sbank = 2 * s2
                    for j in range(2):
                        sc.op("pe", lambda e, j=j, kb=kb, sbank=sbank: e.matmul(
                            self.ps[:, sbank + j, 0:w], lhsT=kt[64 * j:64 * j + 64, kb * 128:(kb + 1) * 128], rhs=qt[64 * j:64 * j + 64, t0:t0 + w],
                            start=True, stop=True), reads=[qb, kb_], writes=[sb_[s2]])
                    pT, pb = pr[s2]
                    sc.op("act", lambda e, sbank=sbank, pT=pT: e.activation(
                        out=pT[:, :, 0:w], in_=self.ps[:, sbank:sbank + 2, 0:w], func=AF.Exp, scale=0.125), reads=[sb_[s2]], writes=[pb])
                    first, last = (ki == 0), (ki == len(kbs) - 1)
                    for j in range(2):
                        sc.op("pe", lambda e, j=j, kb=kb, pT=pT, first=first, last=last: e.matmul(
                            self.ps[:, 4 + 2 * j, 0:w], lhsT=vt[:, kb, :], rhs=pT[:, j, 0:w], start=first, stop=last), reads=[pb, vb], writes=[accb[2 * j]])
                        sc.op("pe", lambda e, j=j, pT=pT, first=first, last=last: e.matmul(
                            self.ps[:, 5 + 2 * j, 0:w], lhsT=self.ones_b[:], rhs=pT[:, j, 0:w], start=first, stop=last), reads=[pb, self.cb], writes=[accb[2 * j + 1]])
                on, onb = onr[fi % 2]
                fi += 1
                sc.op("dve", lambda e: e.reciprocal(out=r0[:, 0:w], in_=self.ps[:, 5, 0:w]), reads=[accb[1]], writes=[fb])
                sc.op("dve", lambda e: e.reciprocal(out=r1[:, 0:w], in_=self.ps[:, 7, 0:w]), reads=[accb[3]], writes=[fb])
                sc.op("dve", lambda e: e.tensor_tensor(out=r0[:, 0:w], in0=r0[:, 0:w], in1=self.ps[:, 4, 0:w], op=ALU.mult), reads=[accb[0], fb], writes=[fb])
                sc.op("dve", lambda e: e.tensor_tensor(out=r1[:, 0:w], in0=r1[:, 0:w], in1=self.ps[:, 6, 0:w], op=ALU.mult), reads=[accb[2], fb], writes=[fb])
                sc.op("dve", lambda e: e.scalar_tensor_tensor(out=o[:, 0:w], in0=r1[:, 0:w], scalar=lsc[:, 2:3], in1=r0[:, 0:w], op0=ALU.mult, op1=ALU.add),
                      reads=[fb, lb], writes=[fb])
                sc.op("pool", lambda e: e.tensor_tensor(out=o2[:, 0:w], in0=o[:, 0:w], in1=o[:, 0:w], op=ALU.mult), reads=[fb], writes=[fb])
                s2 = pc % 2
                pc += 1
                sc.op("pe", lambda e, s2=s2: e.matmul(self.ps[:, 2 * s2, 0:w], lhsT=self.ones_f[:], rhs=o2[:, 0:w], start=True, stop=True),
                      reads=[fb, self.cb], writes=[sb_[s2]])
                sc.op("dve", lambda e, s2=s2: e.tensor_scalar(out=o2[:, 0:w], in0=self.ps[:, 2 * s2, 0:w], scalar1=1.0 / 128, scalar2=EPS, op0=ALU.mult, op1=ALU.add),
                      reads=[sb_[s2], fb], writes=[fb])
                sc.op("pool", lambda e: e.tensor_tensor(out=o2[:, 0:w], in0=o2[:, 0:w], in1=self.mhalf[:, 0:w], op=ALU.pow), reads=[fb, self.cb], writes=[fb])
                sc.op("dve", lambda e, on=on: e.scalar_tensor_tensor(out=on[:, 0:w], in0=o[:, 0:w], scalar=lsc[:, 5:6], in1=o2[:, 0:w], op0=ALU.mult, op1=ALU.mult),
                      reads=[fb, lb], writes=[onb])
                sc.dma("pool", self.aoT[h, :, t0:t0 + w], on[:, 0:w], reads=[onb], writes=[self.bufs["aoT"]])
        self.end_phase()


def _layer_diff(self, l):
    _p1_qkv(self, l, "diff_win", True, 8, 128, False)
    if self.stop == "p1":
        return
    _att_diff(self, l)
    self.out_phase(l, "diff_wout")


def _att_na(self, l):
    sc = self.sc
    with self.phase() as ph:
        qr = self.ring2(ph, "aq", [64, S], BF16)
        kr = self.ring2(ph, "ak", [64, T], BF16)
        vr = self.ring2(ph, "av", [128, 66, 65], BF16)
        br = self.ring2(ph, "nb", [128, 5, 640], F32)
        tbr = self.ring2(ph, "tb", [128, 640], F32)
        pr = self.ring2(ph, "pT", [128, 896], BF16)
        our = self.ring2(ph, "ou", [64, 128], F32)
        onr = self.ring2(ph, "on", [64, 512], BF16)
        rd = self.sb(ph, "rd", [128, 128], F32)
        rdb = Buf("rd")
        sb_ = [Buf("S0"), Buf("S1")]
        accb = [Buf("acc0"), Buf("acc1")]
        bcb = Buf("bc")
        it = 0
        for h in range(16):
            (qt, qb), (kt, kb_), (vt, vb), (bt, bb) = qr[h % 2], kr[h % 2], vr[h % 2], br[h % 2]
            rows = slice((h % 2) * 64, (h % 2) * 64 + 64)
            sc.dma("sp", qt[:], self.QT[h // 2, rows, 0:S], reads=[self.bufs["QT"]], writes=[qb])
            sc.dma("sp", kt[:], self.KT[h // 2, rows, :], reads=[self.bufs["KT"]], writes=[kb_])
            sc.dma("sp", vt[:], self.Vd[:, :, h * 65:h * 65 + 65].rearrange("b p d -> p b d"), reads=[self.bufs["Vd"]], writes=[vb])
            sc.dma("sp", bt[:], self.din["na_bias"][:, h].rearrange("t p c -> p t c"), writes=[bb])
            for j in range(64):
                pat = {0: 1, 1: 2, 62: 3, 63: 4}.get(j, 0)
                b0 = min(max(2 * j - 4, 0), 118)
                blks = [b0 // 2 + m for m in range(5)] + [64, 65]
                s2 = it % 2
                it += 1
                sbank = 2 * s2
                for m, blk in enumerate(blks):
                    bank, c0 = (sbank, m * 128) if m < 4 else (sbank + 1, (m - 4) * 128)
                    sc.op("pe", lambda e, bank=bank, c0=c0, blk=blk: e.matmul(
                        self.ps[:, bank, c0:c0 + 128], lhsT=kt[:, blk * 128:(blk + 1) * 128], rhs=qt[:, j * 128:(j + 1) * 128], start=True, stop=True),
                        reads=[qb, kb_], writes=[sb_[s2]])
                tb, tbb = tbr[s2]
                pT, pb = pr[s2]
                sc.op("dve", lambda e, tb=tb, sbank=sbank, pat=pat: e.scalar_tensor_tensor(
                    out=tb[:, 0:512], in0=self.ps[:, sbank, 0:512], scalar=0.125, in1=bt[:, pat, 0:512], op0=ALU.mult, op1=ALU.add),
                    reads=[sb_[s2], bb], writes=[tbb])
                sc.op("dve", lambda e, tb=tb, sbank=sbank, pat=pat: e.scalar_tensor_tensor(
                    out=tb[:, 512:640], in0=self.ps[:, sbank + 1, 0:128], scalar=0.125, in1=bt[:, pat, 512:640], op0=ALU.mult, op1=ALU.add),
                    reads=[sb_[s2], bb], writes=[tbb])
                sc.op("act", lambda e, tb=tb, pT=pT: e.activation(out=pT[:, 0:640], in_=tb[:, 0:640], func=AF.Exp), reads=[tbb], writes=[pb])
                sc.op("act", lambda e, pT=pT, sbank=sbank: e.activation(out=pT[:, 640:896], in_=self.ps[:, sbank + 1, 128:384], func=AF.Exp, scale=0.125),
                      reads=[sb_[s2]], writes=[pb])
                abank = 4 + s2
                for m, blk in enumerate(blks):
                    sc.op("pe", lambda e, m=m, blk=blk, abank=abank, pT=pT: e.matmul(
                        self.ps[0:65, abank, 0:128], lhsT=vt[:, blk, 0:65], rhs=pT[:, m * 128:(m + 1) * 128], start=(m == 0), stop=(m == 6)),
                        reads=[pb, vb], writes=[accb[s2]])
                ou, oub = our[s2]
                on, onb = onr[(j // 4) % 2]
                sc.op("dve", lambda e, ou=ou, abank=abank: e.tensor_copy(out=ou[:, :], in_=self.ps[0:64, abank, 0:128]), reads=[accb[s2]], writes=[oub])
                sc.op("dve", lambda e, abank=abank: e.reciprocal(out=rd[64:65, :], in_=self.ps[64:65, abank, 0:128]), reads=[accb[s2]], writes=[rdb])
                sc.op("pe", lambda e: e.matmul(self.ps[0:64, 6, 0:128], lhsT=self.ones_f[64:65, 0:64], rhs=rd[64:65, :], start=True, stop=True),
                      reads=[rdb, self.cb], writes=[bcb])
                jj = j % 4
                sc.op("dve", lambda e, ou=ou, on=on, jj=jj: e.tensor_tensor(out=on[:, jj * 128:(jj + 1) * 128], in0=ou[:, :], in1=self.ps[0:64, 6, 0:128], op=ALU.mult),
                      reads=[oub, bcb], writes=[onb])
                if jj == 3:
                    t0 = (j // 4) * 512
                    sc.dma("pool", self.aoT[h // 2, rows, t0:t0 + 512], on[:, :], reads=[onb], writes=[self.bufs["aoT"]])
        self.end_phase()


def _layer_na(self, l):
    _p1_qkv(self, l, "na_win", False, 16, 64, True)
    if self.stop == "p1":
        return
    _att_na(self, l)
    self.out_phase(l, "na_wout", do_ctx=False)


Builder.extra_inputs = _extra_inputs
Builder.layer_mla = _layer_mla
Builder.layer_diff = _layer_diff
Builder.layer_na = _layer_na


def _na_bias_tables(rpb):
    rpb = np.asarray(rpb, np.float32)
    out = np.empty((5, 16, 128, 640), np.float32)
    q = np.arange(128)
    qdr, qcol = q // 64, q % 64
    i = np.arange(640)
    for p, j in enumerate((2, 0, 1, 62, 63)):
        r = 2 * j + qdr
        rs = np.clip(r - 4, 0, 120)
        b0 = min(max(2 * j - 4, 0), 118)
        krow, kcol = b0 + i // 64, i % 64
        cs = np.clip(qcol - 8, 0, 48)
        inw = ((kcol[:, None] >= cs[None]) & (kcol[:, None] < cs[None] + 16) & (krow[:, None] >= rs[None]) & (krow[:, None] < rs[None] + 8))
        dr = np.clip(krow[:, None] - r[None] + 7, 0, 14)
        dc = np.clip(kcol[:, None] - qcol[None] + 15, 0, 30)
        for h in range(16):
            tbl = np.where(inw, rpb[h][dr, dc], np.float32(NEG)).astype(np.float32)
            out[p, h] = tbl.reshape(5, 128, 128).transpose(1, 0, 2).reshape(128, 640)
    return out


def host_inputs_extra(I, m):
    m["mla_win"] = _wfm(I["mla_w_in"][0])
    m["mla_g"] = np.ascontiguousarray(np.concatenate([_fm(I["mla_q_norm_g"][0]), _fm(I["mla_kv_norm_g"][0])], axis=1))
    m["mla_wuq"] = _wfm(I["mla_w_uq"][0])
    wukv = np.asarray(I["mla_w_ukv"][0], np.float32).reshape(256, 16, 128)
    m["mla_wukvk"] = _wfm(np.ascontiguousarray(wukv[:, :, :64]).reshape(256, 1024))
    m["mla_wukvv"] = _wfm(np.ascontiguousarray(wukv[:, :, 64:]).reshape(256, 1024))
    m["mla_wout"] = _wfm(I["mla_w_out"][0])
    c32, s32 = _rope_tables(32, 1)
    cq = np.ones((96, T), np.float32)
    sq = np.zeros((96, T), np.float32)
    cq[64:], sq[64:] = c32, s32
    m["cosq96"], m["sinq96"] = cq, sq
    m["cosk32"], m["sink32"] = c32, s32
    m["rmq96"] = _rot_matrix(32, 1, offset=64)
    m["rmk32"] = _rot_matrix(32, 1)
    m["diff_win"] = _wfm(I["diff_w_in"][0])
    m["diff_wout"] = _wfm(I["diff_w_out"][0])
    m["diff_lam"] = np.ascontiguousarray(np.broadcast_to(np.asarray(I["diff_lambda"][0], np.float32).reshape(1, 256), (128, 256)))
    m["diff_subg"] = _fm(I["diff_subln_g"][0])
    m["na_win"] = _wfm(I["na_w_in"][0])
    m["na_wout"] = _wfm(I["na_w_out"][0])
    m["na_bias"] = _na_bias_tables(I["na_rpb"][0])
    return m


_NC_CACHE = {}


def kernel(**inputs):
    if "nc" not in _NC_CACHE:
        b = Builder()
        _NC_CACHE["nc"] = b.build()
        _NC_CACHE["names"] = list(b.din.keys())
    nc = _NC_CACHE["nc"]
    shared = host_inputs(inputs, 0)
    host_inputs_extra(inputs, shared)
    in_maps = []
    for b in range(NCORES):
        m = dict(shared)
        if b > 0:
            m.update(host_inputs_core(inputs, b))
        in_maps.append({k: m[k] for k in _NC_CACHE["names"]})
    res = run_bass_kernel_spmd(nc, in_maps, core_ids=list(range(NCORES)))
    out = np.empty((NCORES, S, D), np.float32)
    for b in range(NCORES):
        out[b] = np.asarray(res.results[b]["outT"]).reshape(D, S).T
    return out
```

```python
import math
import numpy as np
import concourse.bass as bass
import concourse.mybir as mybir
from concourse.bass_utils import run_bass_kernel_spmd

F32 = mybir.dt.float32
BF16 = mybir.dt.bfloat16
AF = mybir.ActivationFunctionType
ALU = mybir.AluOpType

D = 1024
S = 8192
C = 256
T = S + C
GRID_W = 64
DEPTH = 4
FFN = 2816
EPS = 1e-6
NEG = -1e30
NCORES = 4

CHUNKS = [(i * 512, 512) for i in range(S // 512)] + [(S, C)]


class Dep:
    __slots__ = ("sem", "val", "key", "eng", "lval")

    def __init__(self, sem, val, key, eng, lval):
        self.sem, self.val, self.key, self.eng, self.lval = sem, val, key, eng, lval


class Buf:
    __slots__ = ("w", "r", "multi", "name")

    def __init__(self, name="", multi=False):
        self.w = {}
        self.r = {}
        self.multi = multi
        self.name = name


class Sched:
    NRING = 8
    EPOCH = 30000
    DEPOCH = 1800

    def __init__(self, nc, stack):
        self.nc = nc
        self.stack = stack
        self.h = {"pe": nc.tensor, "act": nc.scalar, "dve": nc.vector, "pool": nc.gpsimd, "sp": nc.sync}
        self.sems = {}
        self.cnt = {}
        self.known = {}
        for e in self.h:
            self.sems[e] = []
            self.cnt[e] = 0
            self.known[e] = {}
        self.ring = {}
        for q in ("sp", "pool", "act"):
            self.ring[q] = {"sems": [[] for _ in range(self.NRING)], "cnt": [0] * self.NRING, "n": 0}
        self.latest = {}
        self.nsem = 0

    def _sem(self, lst, ep, name):
        while len(lst) <= ep:
            self.nsem += 1
            lst.append(self.stack.enter_context(self.nc.semaphore(f"{name}_{len(lst)}")))
        return lst[ep]

    def _wait(self, eng, d):
        kn = self.known[eng]
        if kn.get(d.key, 0) >= d.val:
            return
        self.h[eng].wait_ge(d.sem, d.lval)
        kn[d.key] = d.val

    def _waits(self, eng, reads, writes):
        deps = {}

        def add(d):
            o = deps.get(d.key)
            if o is None or o.val < d.val:
                deps[d.key] = d

        for b in reads:
            for d in b.w.values():
                add(d)
        for b in writes:
            for d in b.r.values():
                if d.eng == eng and d.key == eng:
                    continue
                add(d)
            if not b.multi:
                for d in b.w.values():
                    if d.eng == eng and d.key == eng:
                        continue
                    add(d)
        for d in deps.values():
            if d.key == "pe" and eng == "pe":
                continue
            self._wait(eng, d)

    def _record(self, me, reads, writes):
        for b in reads:
            b.r[me.key] = me
        for b in writes:
            if b.multi:
                b.w[me.key] = me
            else:
                b.w = {me.key: me}
                b.r = {}
        self.latest[me.key] = me

    def op(self, eng, fn, reads=(), writes=()):
        self._waits(eng, reads, writes)
        ins = fn(self.h[eng])
        c = self.cnt[eng]
        ep, lv = c // self.EPOCH, c % self.EPOCH + 1
        sem = self._sem(self.sems[eng], ep, "s_" + eng)
        self.cnt[eng] = c + 1
        ins.then_inc(sem, 1)
        me = Dep(sem, c + 1, eng, eng, lv)
        self._record(me, reads, writes)
        return ins

    def dma(self, q, out, in_, reads=(), writes=()):
        self._waits(q, reads, writes)
        rg = self.ring[q]
        i = rg["n"] % self.NRING
        rg["n"] += 1
        c = rg["cnt"][i]
        ep, lv = c // self.DEPOCH, (c % self.DEPOCH + 1) * 16
        sem = self._sem(rg["sems"][i], ep, f"d_{q}{i}")
        rg["cnt"][i] = c + 1
        self.h[q].dma_start(out=out, in_=in_).then_inc(sem, 16)
        me = Dep(sem, c + 1, (q, i), q, lv)
        self._record(me, reads, writes)

    def barrier(self):
        for e in self.h:
            for d in self.latest.values():
                if d.key == e:
                    continue
                self._wait(e, d)


def _rope_tables(rot_dim, nrep, scale=1.0):
    t = np.arange(S)
    row = (t // GRID_W).astype(np.float32)
    col = (t % GRID_W).astype(np.float32)
    half = rot_dim // 2
    inv = (10000.0 ** (-np.arange(0, half, 2, dtype=np.float32) / half)).astype(np.float32)
    ar, ac = row[:, None] * inv, col[:, None] * inv
    ang = np.concatenate([ar, ar, ac, ac], axis=-1)
    cos = np.ones((T, rot_dim), np.float32)
    sin = np.zeros((T, rot_dim), np.float32)
    cos[:S] = np.cos(ang)
    sin[:S] = np.sin(ang)
    cosT = np.tile(cos.T, (nrep, 1)).astype(np.float32)
    sinT = np.tile(sin.T, (nrep, 1)).astype(np.float32)
    return np.ascontiguousarray(cosT), np.ascontiguousarray(sinT)


def _rot_matrix(rot_dim, nrep, offset=0, total=None):
    n = nrep * rot_dim + offset if total is None else total
    m = np.zeros((n, n), np.float32)
    q = rot_dim // 4
    for r in range(nrep):
        b = offset + r * rot_dim
        for i in range(q):
            m[b + q + i, b + i] = -1.0
            m[b + i, b + q + i] = 1.0
            m[b + 3 * q + i, b + 2 * q + i] = -1.0
            m[b + 2 * q + i, b + 3 * q + i] = 1.0
    return m


class Builder:
    def __init__(self, layers=(0, 1, 2, 3), debug=(), stop=None):
        from contextlib import ExitStack
        self.stop = stop
        self.layers = layers
        self.debug = debug
        self.stack = ExitStack()
        nc = self.nc = bass.Bass("TRN2", target_bir_lowering=False)
        self.sc = Sched(nc, self.stack)
        self.din = {}
        self.bufs = {}

    def inp(self, name, shape, dt=F32):
        t = self.nc.dram_tensor(name, list(shape), dt, kind="ExternalInput").ap()
        self.din[name] = t
        self.bufs[name] = Buf(name, multi=True)
        return t

    def scratch(self, name, shape, dt, kind="Internal"):
        t = self.nc.dram_tensor(name, list(shape), dt, kind=kind).ap()
        self.bufs[name] = Buf(name, multi=True)
        return t

    def sb(self, ph, name, shape, dt):
        self._uid = getattr(self, "_uid", 0) + 1
        t = ph.enter_context(self.nc.sbuf_tensor(f"sb{self._uid}_{name}", list(shape), dt))
        return t

    def load_cast(self, ph, name, src_ap, shape, q="sp", cast_eng="pool", piece=2048):
        sc = self.sc
        dst = self.sb(ph, name, shape, BF16)
        dbuf = Buf(name, multi=True)
        a, b = shape[1], shape[2]
        if not hasattr(self, "_stg") or self._stg_ph is not ph:
            self._stg = [self.sb(ph, f"stg{i}", [128, piece], F32) for i in range(3)]
            self._stgb = [Buf(f"stg{i}") for i in range(3)]
            self._dq = 0
            self._stg_ph = ph
            self._stg_i = 0
        for ai in range(a):
            for b0 in range(0, b, piece):
                w = min(piece, b - b0)
                i = self._stg_i % 3
                self._stg_i += 1
                st, sb_ = self._stg[i], self._stgb[i]
                self._dq += 1
                sc.dma(("sp", "pool")[self._dq % 2], st[:, 0:w], src_ap[:, ai, b0:b0 + w], reads=[], writes=[sb_])
                ce = ("pool", "act", "dve")[self._stg_i % 3]
                if ce == "act":
                    sc.op("act", lambda e, st=st, w=w, ai=ai, b0=b0: e.activation(out=dst[:, ai, b0:b0 + w], in_=st[:, 0:w], func=AF.Copy),
                          reads=[sb_], writes=[dbuf])
                else:
                    sc.op(ce, lambda e, st=st, w=w, ai=ai, b0=b0: e.tensor_copy(out=dst[:, ai, b0:b0 + w], in_=st[:, 0:w]),
                          reads=[sb_], writes=[dbuf])
        return dst, dbuf

    def phase(self):
        from contextlib import ExitStack
        return ExitStack()

    def end_phase(self):
        self.sc.barrier()
        for b in self.bufs.values():
            b.w = {}
            b.r = {}
        self._stg_ph = None

    def consts(self):
        nc, sc, st = self.nc, self.sc, self.stack
        self.ps = st.enter_context(nc.psum_tensor("ps", [128, 8, 512], F32))
        self.ones_f = st.enter_context(nc.sbuf_tensor("c_ones_f", [128, 128], F32))
        self.ones_b = st.enter_context(nc.sbuf_tensor("c_ones_b", [128, 128], BF16))
        self.bd64 = st.enter_context(nc.sbuf_tensor("c_bd64", [128, 128], F32))
        self.mhalf = st.enter_context(nc.sbuf_tensor("c_mhalf", [128, 512], F32))
        self.modv = st.enter_context(nc.sbuf_tensor("c_modv", [128, 2, 48], F32))
        self.lvec = st.enter_context(nc.sbuf_tensor("c_lvec", [128, NV], F32))
        self.scv = st.enter_context(nc.sbuf_tensor("c_scv", [128, 8, 2], F32))
        self.epst = st.enter_context(nc.sbuf_tensor("c_eps", [128, 1], F32))
        self.cb = Buf("consts")
        self.modb = Buf("modv")
        self.lvb = Buf("lvec")
        self.scb = Buf("scv")
        sc.op("dve", lambda e: e.memset(self.ones_f[:], 1.0), writes=[self.cb])
        sc.op("dve", lambda e: e.memset(self.ones_b[:], 1.0), writes=[self.cb])
        sc.op("dve", lambda e: e.memset(self.bd64[:], 0.0), writes=[self.cb])
        sc.op("dve", lambda e: e.memset(self.bd64[0:64, 0:64], 1.0), writes=[self.cb])
        sc.op("dve", lambda e: e.memset(self.bd64[64:128, 64:128], 1.0), writes=[self.cb])
        sc.op("dve", lambda e: e.memset(self.mhalf[:], -0.5), writes=[self.cb])
        sc.op("dve", lambda e: e.memset(self.epst[:], EPS), writes=[self.cb])
        sc.dma("sp", self.scv[:], self.din["cfm"], writes=[self.scb])
        sc.op("act", lambda e: e.activation(out=self.scv[:], in_=self.scv[:], func=AF.Silu), reads=[self.scb], writes=[self.scb])

    def mod_phase(self, l):
        nc, sc = self.nc, self.sc
        with self.phase() as ph:
            sc.dma("sp", self.lvec[:], self.din["lvec"][l], writes=[self.lvb])
            wst = [self.sb(ph, f"adaw{i}", [128, 8, 512], F32) for i in range(2)]
            wb = [Buf(f"adaw{i}") for i in range(2)]
            mps = self.ps[:, 0, 0:96].rearrange("p (g s) -> p g s", s=2)
            mpb = Buf("modps")
            for pc in range(12):
                i = pc % 2
                sc.dma("sp" if pc % 2 == 0 else "pool", wst[i][:], self.din["adaw"][l, :, :, pc * 512:(pc + 1) * 512], writes=[wb[i]])
                for g4 in range(4):
                    g = pc * 4 + g4
                    for kc in range(8):
                        sc.op("pe", lambda e, i=i, g4=g4, kc=kc, g=g: e.matmul(
                            mps[:, g, :], lhsT=wst[i][:, kc, g4 * 128:(g4 + 1) * 128], rhs=self.scv[:, kc, :],
                            start=(kc == 0), stop=(kc == 7)), reads=[wb[i], self.scb], writes=[mpb])
            mod = self.sb(ph, "modraw", [128, 2, 48], F32)
            mb = Buf("modraw")
            for s_ in range(2):
                sc.op("dve", lambda e, s_=s_: e.tensor_tensor(out=mod[:, s_, :], in0=mps[:, :, s_], in1=self.lvec[:, LV_ADAB:LV_ADAB + 48], op=ALU.add),
                      reads=[mpb, self.lvb], writes=[mb])
            for s_ in range(2):
                for which, (shc, scc, gc, ng) in enumerate(((0, 8, 16, LV_N1G), (24, 32, 40, LV_N2G))):
                    o = which * 24
                    sc.op("dve", lambda e, s_=s_, scc=scc, ng=ng, o=o: e.scalar_tensor_tensor(
                        out=self.modv[:, s_, o:o + 8], in0=mod[:, s_, scc:scc + 8], scalar=1.0, in1=self.lvec[:, ng:ng + 8],
                        op0=ALU.add, op1=ALU.mult), reads=[mb, self.lvb], writes=[self.modb])
                    sc.op("dve", lambda e, s_=s_, shc=shc, o=o: e.tensor_copy(out=self.modv[:, s_, o + 8:o + 16], in_=mod[:, s_, shc:shc + 8]),
                          reads=[mb], writes=[self.modb])
                    sc.op("dve", lambda e, s_=s_, gc=gc, o=o: e.tensor_copy(out=self.modv[:, s_, o + 16:o + 24], in_=mod[:, s_, gc:gc + 8]),
                          reads=[mb], writes=[self.modb])
            self.end_phase()

    def rstd(self, out_ap, in_ap, inv_n, reads, outb, rows=128):
        sc = self.sc
        sc.op("act", lambda e: e.activation(out=out_ap, in_=in_ap, func=AF.Sqrt, bias=self.epst[0:rows, 0:1], scale=inv_n), reads=list(reads) + [self.cb], writes=[outb])
        sc.op("dve", lambda e: e.reciprocal(out=out_ap, in_=out_ap), reads=[outb], writes=[outb])

    def norm_mod(self, xc, xb, w, stream, which, hT, hb, tmp, psb):
        sc = self.sc
        o = which * 24
        sq, sqb, r, rb, h1, h1b = tmp["sq"], tmp["sqb"], tmp["r"], tmp["rb"], tmp["h1"], tmp["h1b"]
        bank, pb = psb
        sc.op("act", lambda e: e.activation(out=sq[:, :, 0:w], in_=xc[:, :, 0:w], func=AF.Square), reads=[xb], writes=[sqb])
        for kc in range(8):
            sc.op("pe", lambda e, kc=kc: e.matmul(self.ps[:, bank, 0:w], lhsT=self.ones_f[:], rhs=sq[:, kc, 0:w], start=(kc == 0), stop=(kc == 7)),
                  reads=[sqb, self.cb], writes=[pb])
        self.rstd(r[:, 0:w], self.ps[:, bank, 0:w], 1.0 / D, [pb], rb)
        if hT is None:
            return
        A = self.modv[:, stream, o:o + 8].unsqueeze(2).to_broadcast([128, 8, w])
        rbc = r[:, 0:w].unsqueeze(1).to_broadcast([128, 8, w])
        sc.op("dve", lambda e: e.tensor_tensor(out=h1[:, :, 0:w], in0=xc[:, :, 0:w], in1=A, op=ALU.mult), reads=[xb, self.modb], writes=[h1b])
        sc.op("dve", lambda e: e.tensor_tensor(out=h1[:, :, 0:w], in0=h1[:, :, 0:w], in1=rbc, op=ALU.mult), reads=[h1b, rb], writes=[h1b])
        for kc in range(8):
            sc.op("act", lambda e, kc=kc: e.activation(out=hT[:, kc, 0:w], in_=h1[:, kc, 0:w], func=AF.Identity,
                                                       bias=self.modv[:, stream, o + 8 + kc:o + 9 + kc], scale=1.0), reads=[h1b, self.modb], writes=[hb])

    def norm_tmp(self, ph):
        return {"sq": self.sb(ph, "nsq", [128, 8, 512], F32), "sqb": Buf("nsq"),
                "r": self.sb(ph, "nr", [128, 512], F32), "rb": Buf("nr"),
                "h1": self.sb(ph, "nh1", [128, 8, 512], F32), "h1b": Buf("nh1")}

    def xres_ap(self, t0, w):
        return self.xres[:, :, t0:t0 + w].rearrange("k p t -> p k t")

    def qk_post(self, ph_t, src_bank, srcb, w, gain_ap, normalize, rope, cs, dst_ap, dstb_name, rows=128):
        sc = self.sc
        t = ph_t
        i = t["i"] % 2
        t["i"] += 1
        qs, qsb = t["qs"][i], t["qsb"][i]
        q2, q2b = t["q2"][i], t["q2b"][i]
        r2, r2b = t["r2"][i], t["r2b"][i]
        qn, qnb = t["qn"][i], t["qnb"][i]
        qf, qfb = t["qf"][i], t["qfb"][i]
        R = slice(0, rows)
        if not normalize and not rope:
            sc.op("act", lambda e: e.activation(out=qf[R, 0:w], in_=self.ps[R, src_bank, 0:w], func=AF.Copy), reads=[srcb], writes=[qfb])
            dsts = dst_ap if isinstance(dst_ap, list) else [(dst_ap, slice(0, rows))]
            for (d_ap, rs) in dsts:
                sc.dma("pool", d_ap, qf[rs, 0:w], reads=[qfb], writes=[self.bufs[dstb_name]])
            return
        sc.op("act", lambda e: e.activation(out=qs[R, 0:w], in_=self.ps[R, src_bank, 0:w], func=AF.Copy), reads=[srcb], writes=[qsb])
        cur, curb = qs, qsb
        if normalize:
            sc.op("dve", lambda e: e.tensor_tensor(out=q2[R, 0:w], in0=qs[R, 0:w], in1=qs[R, 0:w], op=ALU.mult), reads=[qsb], writes=[q2b])
            sc.op("pe", lambda e: e.matmul(self.ps[R, 3, 0:w], lhsT=self.bd64[R, R], rhs=q2[R, 0:w], start=True, stop=True), reads=[q2b, self.cb], writes=[t["ms2b"]])
            self.rstd(r2[R, 0:w], self.ps[R, 3, 0:w], 1.0 / 64, [t["ms2b"]], r2b, rows=rows)
            sc.op("dve", lambda e: e.scalar_tensor_tensor(out=qn[R, 0:w], in0=qs[R, 0:w], scalar=gain_ap, in1=r2[R, 0:w], op0=ALU.mult, op1=ALU.mult),
                  reads=[qsb, r2b, t["gb"]], writes=[qnb])
            cur, curb = qn, qnb
        if rope:
            rm, cos, sin, csb = cs
            sc.op("pe", lambda e: e.matmul(self.ps[R, 4, 0:w], lhsT=rm[R, R], rhs=cur[R, 0:w], start=True, stop=True), reads=[curb, t["gb"]], writes=[t["rotb"]])
            sc.op("dve", lambda e: e.tensor_tensor(out=q2[R, 0:w], in0=cur[R, 0:w], in1=cos[R, 0:w], op=ALU.mult), reads=[curb, csb], writes=[q2b])
            sc.op("dve", lambda e: e.tensor_tensor(out=r2[R, 0:w], in0=self.ps[R, 4, 0:w], in1=sin[R, 0:w], op=ALU.mult), reads=[t["rotb"], csb], writes=[r2b])
            sc.op("dve", lambda e: e.tensor_tensor(out=qf[R, 0:w], in0=q2[R, 0:w], in1=r2[R, 0:w], op=ALU.add), reads=[q2b, r2b], writes=[qfb])
        else:
            sc.op("act", lambda e: e.activation(out=qf[R, 0:w], in_=cur[R, 0:w], func=AF.Copy), reads=[curb], writes=[qfb])
        dsts = dst_ap if isinstance(dst_ap, list) else [(dst_ap, slice(0, rows))]
        for (d_ap, rs) in dsts:
            sc.dma("sp", d_ap, qf[rs, 0:w], reads=[qfb], writes=[self.bufs[dstb_name]])

    def qk_tmp(self, ph):
        t = {"i": 0, "ms2b": Buf("ms2"), "rotb": Buf("rot"), "gb": Buf("gains")}
        for nm, dt in (("qs", F32), ("q2", F32), ("r2", F32), ("qn", F32), ("qf", BF16)):
            t[nm] = [self.sb(ph, f"{nm}{i}", [128, 512], dt) for i in range(2)]
            t[nm + "b"] = [Buf(f"{nm}{i}") for i in range(2)]
        return t

    def proj_group(self, wt, wtb, col0, ncols, hT, hb, w, bank, pb, nk=8):
        sc = self.sc
        for kc in range(nk):
            sc.op("pe", lambda e, kc=kc: e.matmul(self.ps[0:ncols, bank, 0:w], lhsT=wt[:, kc, col0:col0 + ncols], rhs=hT[:, kc, 0:w],
                                                   start=(kc == 0), stop=(kc == nk - 1)), reads=[wtb, hb], writes=[pb])

    def v_tokmajor(self, wt, wtb, col0, ncols, hT, hb, t0, w, vt_ring, nheads, hd, ones_col, nk=8):
        sc = self.sc
        vw = hd + (1 if ones_col else 0)
        for ti in range(w // 128):
            vt, vb = vt_ring[self._vi % 2]
            self._vi += 1
            for n0 in range(0, ncols, 512):
                nn = min(512, ncols - n0)
                bank = 5 + n0 // 512
                for kc in range(nk):
                    sc.op("pe", lambda e, kc=kc, n0=n0, nn=nn, bank=bank: e.matmul(
                        self.ps[:, bank, 0:nn], lhsT=hT[:, kc, ti * 128:(ti + 1) * 128], rhs=wt[:, kc, col0 + n0:col0 + n0 + nn],
                        start=(kc == 0), stop=(kc == nk - 1)), reads=[wtb, hb], writes=[self.vpb[bank - 5]])
                h0 = n0 // hd
                nh = nn // hd
                sc.op("act", lambda e, bank=bank, nn=nn, h0=h0, nh=nh: e.activation(
                    out=vt[:, h0:h0 + nh, 0:hd], in_=self.ps[:, bank, 0:nn].rearrange("p (h d) -> p h d", d=hd), func=AF.Copy),
                    reads=[self.vpb[bank - 5]], writes=[vb])
            blk = (t0 + ti * 128) // 128
            sc.dma("sp", self.Vd[blk, :, 0:nheads * vw], vt[:, 0:nheads, 0:vw].rearrange("p h d -> p (h d)") if False else vt[:, 0:nheads, 0:vw],
                   reads=[vb], writes=[self.bufs["Vd"]])

    def v_tokmajor(self, wt, wtb, col0, ncols, hT, hb, t0, w, vt_ring, nheads, hd, ones_col, nk=8):
        sc = self.sc
        vw = hd + (1 if ones_col else 0)
        for ti in range(w // 128):
            vt, vb = vt_ring[self._vi % 2]
            self._vi += 1
            vt3 = vt[:, 0:nheads * vw].rearrange("p (h d) -> p h d", d=vw)
            for n0 in range(0, ncols, 512):
                nn = min(512, ncols - n0)
                bank = 5 + n0 // 512
                for kc in range(nk):
                    sc.op("pe", lambda e, kc=kc, n0=n0, nn=nn, bank=bank, ti=ti: e.matmul(
                        self.ps[:, bank, 0:nn], lhsT=hT[:, kc, ti * 128:(ti + 1) * 128], rhs=wt[:, kc, col0 + n0:col0 + n0 + nn],
                        start=(kc == 0), stop=(kc == nk - 1)), reads=[wtb, hb], writes=[self.vpb[bank - 5]])
                h0 = n0 // hd
                nh = nn // hd
                sc.op("act", lambda e, bank=bank, nn=nn, h0=h0, nh=nh, vt3=vt3: e.activation(
                    out=vt3[:, h0:h0 + nh, 0:hd], in_=self.ps[:, bank, 0:nn].rearrange("p (h d) -> p h d", d=hd), func=AF.Copy),
                    reads=[self.vpb[bank - 5]], writes=[vb])
            blk = (t0 + ti * 128) // 128
            sc.dma("sp", self.Vd[blk, :, 0:nheads * vw], vt[:, 0:nheads * vw], reads=[vb], writes=[self.bufs["Vd"]])

    def vt_ring(self, ph, nheads, hd, ones_col):
        sc = self.sc
        vw = hd + (1 if ones_col else 0)
        ring = []
        for i in range(2):
            vt = self.sb(ph, f"vt{i}", [128, 1040], BF16)
            vb = Buf(f"vt{i}")
            if ones_col:
                sc.op("dve", lambda e, vt=vt: e.memset(vt[:, 0:nheads * vw], 1.0), writes=[vb])
            ring.append((vt, vb))
        self._vi = 0
        self.vpb = [Buf("vps0"), Buf("vps1")]
        return ring

    def chunk_front(self, ph, tmp, xr, hr, ci, t0, w, which=0):
        sc = self.sc
        stream = 0 if t0 < S else 1
        xc, xb = xr[ci % 2]
        hT, hb = hr[ci % 2]
        sc.dma("sp", xc[:, :, 0:w], self.xres_ap(t0, w), reads=[self.bufs["xres"]], writes=[xb])
        self.norm_mod(xc, xb, w, stream, which, hT, hb, tmp, (0, self.nps))
        return hT, hb, xc, xb

    def ring2(self, ph, name, shape, dt):
        return [(self.sb(ph, f"{name}{i}", shape, dt), Buf(f"{name}{i}")) for i in range(2)]

    def p1_gqa(self, l):
        sc = self.sc
        with self.phase() as ph:
            win, winb = self.load_cast(ph, "win", self.din["gqa_win"], [128, 8, 1536])
            tmp = self.norm_tmp(ph)
            t = self.qk_tmp(ph)
            gq = self.sb(ph, "gq", [128, 2], F32)
            rm = self.sb(ph, "rm", [128, 128], F32)
            sc.dma("sp", gq[:], self.din["gqa_g"], writes=[t["gb"]])
            sc.dma("sp", rm[:], self.din["rm64"], writes=[t["gb"]])
            xr = self.ring2(ph, "xc", [128, 8, 512], F32)
            hr = self.ring2(ph, "hT", [128, 8, 512], BF16)
            cr = self.ring2(ph, "cs", [128, 2, 512], F32)
            vr = self.vt_ring(ph, 4, 64, True)
            self.nps = Buf("nps")
            pbs = [Buf("pp1"), Buf("pp2")]
            for ci, (t0, w) in enumerate(CHUNKS):
                hT, hb, xc, xb = self.chunk_front(ph, tmp, xr, hr, ci, t0, w)
                cs, csb = cr[ci % 2]
                sc.dma("sp", cs[:, 0, 0:w], self.din["cos64"][:, t0:t0 + w], writes=[csb])
                sc.dma("sp", cs[:, 1, 0:w], self.din["sin64"][:, t0:t0 + w], writes=[csb])
                for g in range(10):
                    bank = 1 + g % 2
                    self.proj_group(win, winb, g * 128, 128, hT, hb, w, bank, pbs[g % 2])
                    dst = (self.QT[g, :, t0:t0 + w], "QT") if g < 8 else (self.KT[g - 8, :, t0:t0 + w], "KT")
                    self.qk_post(t, bank, pbs[g % 2], w, gq[:, 0:1] if g < 8 else gq[:, 1:2], True, True,
                                 (rm, cs[:, 0, :], cs[:, 1, :], csb), dst[0], dst[1])
                self.v_tokmajor(win, winb, 1280, 256, hT, hb, t0, w, vr, 4, 64, True)
            self.end_phase()

    def att_std(self, nheads, dqk, qsrc, ksrc, vcol, scale):
        sc = self.sc
        with self.phase() as ph:
            qr = self.ring2(ph, "aq", [128, T], BF16)
            kr = self.ring2(ph, "ak", [128, T], BF16)
            vr = self.ring2(ph, "av", [128, 66, 65], BF16)
            pr = [(self.sb(ph, f"pT{i}", [128, 2, 512], BF16), Buf(f"pT{i}")) for i in range(3)]
            our = self.ring2(ph, "ou", [64, 512], F32)
            onr = self.ring2(ph, "on", [64, 512], BF16)
            rdr = self.ring2(ph, "rd", [128, 512], F32)
            sb_ = [Buf(f"S{i}") for i in range(2)]
            accb = [Buf("acc0"), Buf("acc1")]
            bcb = Buf("bc")
            jobs = []
            for h in range(nheads):
                for ci, (t0, w) in enumerate(CHUNKS):
                    kbs = list(range(66)) if t0 < S else [64, 65]
                    pairs = [kbs[i:i + 2] for i in range(0, len(kbs), 2)]
                    for pi, pr_ in enumerate(pairs):
                        jobs.append((h, ci, t0, w, pr_, pi == 0, pi == len(pairs) - 1))
            loaded = set()

            def load(h):
                if h in loaded or h >= nheads:
                    return
                loaded.add(h)
                (qt, qb), (kt, kb_), (vt, vb) = qr[h % 2], kr[h % 2], vr[h % 2]
                sc.dma("sp", qt[0:dqk, :], qsrc(h), reads=[self.bufs["QT"]], writes=[qb])
                sc.dma("sp", kt[0:dqk, :], ksrc(h), reads=[self.bufs["KT"]], writes=[kb_])
                sc.dma("sp", vt[:], self.Vd[:, :, vcol(h):vcol(h) + 65].rearrange("b p d -> p b d"), reads=[self.bufs["Vd"]], writes=[vb])

            def emit_S(n):
                h, ci, t0, w, pr_, first, last = jobs[n]
                load(h)
                (qt, qb), (kt, kb_) = qr[h % 2], kr[h % 2]
                sbank = 2 * (n % 2)
                for j, kb in enumerate(pr_):
                    sc.op("pe", lambda e, j=j, kb=kb: e.matmul(
                        self.ps[:, sbank + j, 0:w], lhsT=kt[0:dqk, kb * 128:(kb + 1) * 128], rhs=qt[0:dqk, t0:t0 + w], start=True, stop=True),
                        reads=[qb, kb_], writes=[sb_[n % 2]])

            chunk_ctr = [0]
            pending = [None]

            def fin2():
                if pending[0] is None:
                    return
                (h, t0, w, ou, oub, on, onb, rd, rdb) = pending[0]
                pending[0] = None
                sc.op("pe", lambda e: e.matmul(self.ps[0:64, 6, 0:w], lhsT=self.ones_f[64:65, 0:64], rhs=rd[64:65, 0:w], start=True, stop=True),
                      reads=[rdb, self.cb], writes=[bcb])
                sc.op("dve", lambda e: e.tensor_tensor(out=on[:, 0:w], in0=ou[:, 0:w], in1=self.ps[0:64, 6, 0:w], op=ALU.mult),
                      reads=[oub, bcb], writes=[onb])
                sc.dma("pool", self.aoT[h // 2, (h % 2) * 64:(h % 2) * 64 + 64, t0:t0 + w], on[:, 0:w], reads=[onb], writes=[self.bufs["aoT"]])

            emit_S(0)
            for n in range(len(jobs)):
                h, ci, t0, w, pr_, first, last = jobs[n]
                if n + 1 < len(jobs):
                    emit_S(n + 1)
                vt, vb = vr[h % 2]
                sbank = 2 * (n % 2)
                npair = len(pr_)
                pT, pb = pr[n % 3]
                sc.op("act", lambda e: e.activation(out=pT[:, 0:npair, 0:w], in_=self.ps[:, sbank:sbank + npair, 0:w], func=AF.Exp, scale=scale),
                      reads=[sb_[n % 2]], writes=[pb])
                ab = chunk_ctr[0] % 2
                for j, kb in enumerate(pr_):
                    sc.op("pe", lambda e, j=j, kb=kb: e.matmul(
                        self.ps[0:65, 4 + ab, 0:w], lhsT=vt[:, kb, 0:65], rhs=pT[:, j, 0:w], start=(first and j == 0), stop=(last and j == npair - 1)),
                        reads=[pb, vb], writes=[accb[ab]])
                if ci == 0 and first:
                    load(h + 1)
                fin2()
                if last:
                    fi = chunk_ctr[0]
                    chunk_ctr[0] += 1
                    ou, oub = our[fi % 2]
                    on, onb = onr[fi % 2]
                    rd, rdb = rdr[fi % 2]
                    sc.op("dve", lambda e: e.tensor_copy(out=ou[:, 0:w], in_=self.ps[0:64, 4 + ab, 0:w]), reads=[accb[ab]], writes=[oub])
                    sc.op("dve", lambda e: e.reciprocal(out=rd[64:65, 0:w], in_=self.ps[64:65, 4 + ab, 0:w]), reads=[accb[ab]], writes=[rdb])
                    pending[0] = (h, t0, w, ou, oub, on, onb, rd, rdb)
            fin2()
            self.end_phase()

    def out_phase(self, l, wout_name, do_ctx=True):
        sc = self.sc
        with self.phase() as ph:
            wo, wob = self.load_cast(ph, "wo", self.din[wout_name], [128, 8, 1024])
            tmp = self.norm_tmp(ph)
            xr = self.ring2(ph, "xc", [128, 8, 512], F32)
            ar = self.ring2(ph, "ao", [128, 8, 512], BF16)
            hr = self.ring2(ph, "h2", [128, 8, 512], BF16)
            self.nps = Buf("nps")
            pbs = [Buf("op1"), Buf("op2")]
            chunks = CHUNKS if do_ctx else CHUNKS[:-1]
            for ci, (t0, w) in enumerate(chunks):
                stream = 0 if t0 < S else 1
                xc, xb = xr[ci % 2]
                ao, ab = ar[ci % 2]
                h2, hb = hr[ci % 2]
                sc.dma("sp", xc[:, :, 0:w], self.xres_ap(t0, w), reads=[self.bufs["xres"]], writes=[xb])
                sc.dma("sp", ao[:, :, 0:w], self.aoT[:, :, t0:t0 + w].rearrange("k p t -> p k t"), reads=[self.bufs["aoT"]], writes=[ab])
                for og in range(8):
                    bank = 1 + og % 2
                    self.proj_group(wo, wob, og * 128, 128, ao, ab, w, bank, pbs[og % 2])
                    sc.op("dve", lambda e, og=og, bank=bank: e.scalar_tensor_tensor(
                        out=xc[:, og, 0:w], in0=self.ps[:, bank, 0:w], scalar=self.modv[:, stream, 16 + og:17 + og], in1=xc[:, og, 0:w],
                        op0=ALU.mult, op1=ALU.add), reads=[pbs[og % 2], xb, self.modb], writes=[xb])
                sc.dma("pool", self.xres_ap(t0, w), xc[:, :, 0:w], reads=[xb], writes=[self.bufs["xres"]])
                self.norm_mod(xc, xb, w, stream, 1, h2, hb, tmp, (0, self.nps))
                sc.dma("pool", self.h2T[:, :, t0:t0 + w].rearrange("k p t -> p k t"), h2[:, :, 0:w], reads=[hb], writes=[self.bufs["h2T"]])
            self.end_phase()

    def ffn_phase(self, l, do_ctx=True):
        sc = self.sc
        W = 256
        with self.phase() as ph:
            wu, wub = self.load_cast(ph, "wu", self.din["ffn_up"][l], [128, 8, 2 * FFN])
            wd, wdb = self.load_cast(ph, "wd", self.din["ffn_dn"][l], [128, 22, D])
            hr = self.ring2(ph, "fh", [128, 8, W + 2], BF16)
            gT = self.sb(ph, "gT", [128, 22, W], BF16)
            gTb = [Buf(f"gT{f}") for f in range(22)]
            ur = [(self.sb(ph, f"fu{i}", [128, 2, W], F32), Buf(f"fu{i}")) for i in range(2)]
            sgr = self.ring2(ph, "fsg", [128, W], F32)
            xor = self.ring2(ph, "fxo", [128, W], F32)
            upb = [Buf("up0"), Buf("up1"), Buf("up2"), Buf("up3")]
            dnb = [Buf("dn0"), Buf("dn1")]
            seqs = [(0, S)] + ([(S, T)] if do_ctx else [])
            chunks = [(t0, W, a, b) for (a, b) in seqs for t0 in range(a, b, W)]
            cw0 = LV_CONVW
            ui = 0
            xi = 0
            for ci, (t0, w, s0, s1) in enumerate(chunks):
                hh, hb = hr[ci % 2]
                lo = 1 if t0 > s0 else 0
                hi = 1 if t0 + w < s1 else 0
                if not lo:
                    sc.op("dve", lambda e, hh=hh: e.memset(hh[:, :, 0:1], 0.0), writes=[hb])
                if not hi:
                    sc.op("dve", lambda e, hh=hh: e.memset(hh[:, :, w + 1:w + 2], 0.0), writes=[hb])
                sc.dma("sp", hh[:, :, 1 - lo:w + 1 + hi], self.h2T[:, :, t0 - lo:t0 + w + hi].rearrange("k p t -> p k t"),
                       reads=[self.bufs["h2T"]], writes=[hb])
                for f in range(22):
                    uu, ub = ur[ui % 2]
                    ui += 1
                    for vg, grp in enumerate((f, f + 22)):
                        bank = 2 * (f % 2) + vg
                        pb = upb[bank]
                        self_ps = self.ps
                        for kc in range(8):
                            sc.op("pe", lambda e, kc=kc, grp=grp, bank=bank, hh=hh: e.matmul(
                                self_ps[:, bank, 0:w + 2], lhsT=wu[:, kc, grp * 128:(grp + 1) * 128], rhs=hh[:, kc, 0:w + 2],
                                start=(kc == 0), stop=(kc == 7)), reads=[wub, hb], writes=[pb])
                        c0 = cw0 + grp * 3
                        cbias = LV_CONVB + grp
                        sc.op("act", lambda e, bank=bank, vg=vg, uu=uu, c0=c0, cbias=cbias: e.activation(
                            out=uu[:, vg, 0:w], in_=self_ps[:, bank, 0:w], func=AF.Identity, bias=self.lvec[:, cbias:cbias + 1], scale=self.lvec[:, c0:c0 + 1]),
                            reads=[pb, self.lvb], writes=[ub])
                        for tap in (1, 2):
                            sc.op("dve", lambda e, bank=bank, vg=vg, uu=uu, c0=c0, tap=tap: e.scalar_tensor_tensor(
                                out=uu[:, vg, 0:w], in0=self_ps[:, bank, tap:tap + w], scalar=self.lvec[:, c0 + tap:c0 + tap + 1], in1=uu[:, vg, 0:w],
                                op0=ALU.mult, op1=ALU.add), reads=[pb, ub, self.lvb], writes=[ub])
                    sg, sgb = sgr[f % 2]
                    sc.op("act", lambda e, uu=uu, sg=sg: e.activation(out=sg[:, 0:w], in_=uu[:, 1, 0:w], func=AF.Silu), reads=[ub], writes=[sgb])
                    sc.op("dve", lambda e, uu=uu, sg=sg, f=f: e.tensor_tensor(out=gT[:, f, 0:w], in0=sg[:, 0:w], in1=uu[:, 0, 0:w], op=ALU.mult),
                          reads=[sgb, ub], writes=[gTb[f]])
                stream = 0 if t0 < S else 1
                for og in range(8):
                    bank = 4 + og % 2
                    pb = dnb[og % 2]
                    xo, xob = xor[xi % 2]
                    xi += 1
                    sc.dma("sp", xo[:, 0:w], self.xres[og, :, t0:t0 + w], reads=[self.bufs["xres"]], writes=[xob])
                    for f in range(22):
                        sc.op("pe", lambda e, f=f, og=og, bank=bank: e.matmul(
                            self.ps[:, bank, 0:w], lhsT=wd[:, f, og * 128:(og + 1) * 128], rhs=gT[:, f, 0:w], start=(f == 0), stop=(f == 21)),
                            reads=[wdb, gTb[f]], writes=[pb])
                    sc.op("dve", lambda e, og=og, bank=bank, xo=xo: e.scalar_tensor_tensor(
                        out=xo[:, 0:w], in0=self.ps[:, bank, 0:w], scalar=self.modv[:, stream, 40 + og:41 + og], in1=xo[:, 0:w],
                        op0=ALU.mult, op1=ALU.add), reads=[pb, xob, self.modb], writes=[xob])
                    sc.dma("pool", self.xres[og, :, t0:t0 + w], xo[:, 0:w], reads=[xob], writes=[self.bufs["xres"]])
            self.end_phase()

    def final_phase(self):
        sc = self.sc
        with self.phase() as ph:
            tmp = self.norm_tmp(ph)
            xr = self.ring2(ph, "xc", [128, 8, 512], F32)
            fg = self.sb(ph, "fg", [128, 8], F32)
            fgb = Buf("fg")
            sc.dma("sp", fg[:], self.din["fng"], writes=[fgb])
            self.nps = Buf("nps")
            for ci, (t0, w) in enumerate(CHUNKS[:-1]):
                xc, xb = xr[ci % 2]
                sc.dma("sp", xc[:, :, 0:w], self.xres_ap(t0, w), reads=[self.bufs["xres"]], writes=[xb])
                self.norm_mod(xc, xb, w, 0, 0, None, None, tmp, (0, self.nps))
                h1, h1b, r, rb = tmp["h1"], tmp["h1b"], tmp["r"], tmp["rb"]
                sc.op("dve", lambda e, xc=xc: e.tensor_tensor(out=h1[:, :, 0:w], in0=xc[:, :, 0:w], in1=fg[:, 0:8].unsqueeze(2).to_broadcast([128, 8, w]), op=ALU.mult),
                      reads=[xb, fgb], writes=[h1b])
                sc.op("dve", lambda e: e.tensor_tensor(out=h1[:, :, 0:w], in0=h1[:, :, 0:w], in1=r[:, 0:w].unsqueeze(1).to_broadcast([128, 8, w]), op=ALU.mult),
                      reads=[h1b, rb], writes=[h1b])
                sc.dma("pool", self.outT[:, :, t0:t0 + w].rearrange("k p t -> p k t"), h1[:, :, 0:w], reads=[h1b], writes=[self.bufs["outT"]])
            self.end_phase()

    def build(self):
        nc, sc = self.nc, self.sc
        L = self.layers
        self.inp("xT_in", [8, 128, T])
        self.inp("cfm", [128, 8, 2])
        self.inp("adaw", [DEPTH, 128, 8, 6 * D])
        self.inp("lvec", [DEPTH, 128, NV])
        self.inp("fng", [128, 8])
        self.inp("ffn_up", [DEPTH, 128, 8, 2 * FFN])
        self.inp("ffn_dn", [DEPTH, 128, 22, D])
        self.inp("cos64", [128, T])
        self.inp("sin64", [128, T])
        self.inp("rm64", [128, 128])
        self.inp("gqa_win", [128, 8, 1536])
        self.inp("gqa_wout", [128, 8, 1024])
        self.inp("gqa_g", [128, 2])
        self.extra_inputs()
        self.xres = self.scratch("xres", [8, 128, T], F32)
        self.h2T = self.scratch("h2T", [8, 128, T], BF16)
        self.QT = self.scratch("QT", [16, 128, T], BF16)
        self.KT = self.scratch("KT", [16, 128, T], BF16)
        self.Vd = self.scratch("Vd", [66, 128, 1040], BF16)
        self.aoT = self.scratch("aoT", [8, 128, T], BF16)
        self.outT = self.scratch("outT", [8, 128, S], F32, kind="ExternalOutput")
        if "xres" in self.debug:
            self.dbg_x = self.scratch("dbg_x", [8, 128, T], F32, kind="ExternalOutput")
        self.consts()
        for k in range(8):
            sc.dma("sp" if k % 2 == 0 else "pool", self.xres[k], self.din["xT_in"][k], writes=[self.bufs["xres"]])
        self.end_phase()
        for l in L:
            last = (l == DEPTH - 1)
            self.mod_phase(l)
            if self.stop == "mod":
                break
            if l == 0:
                self.p1_gqa(l)
                if self.stop == "p1":
                    break
                self.att_std(16 if self.stop != "att1" else 1, 64,
                             lambda h: self.QT[h // 2, (h % 2) * 64:(h % 2) * 64 + 64, :],
                             lambda h: self.KT[(h // 4) // 2, ((h // 4) % 2) * 64:((h // 4) % 2) * 64 + 64, :],
                             lambda h: (h // 4) * 65, 64 ** -0.5)
                if self.stop in ("att", "att1"):
                    break
                self.out_phase(l, "gqa_wout")
                if self.stop == "out":
                    break
            elif l == 1:
                self.layer_mla(l)
            elif l == 2:
                self.layer_diff(l)
            else:
                self.layer_na(l)
            self.ffn_phase(l, do_ctx=not last)
        if "xres" in self.debug:
            for k in range(8):
                sc.dma("sp", self.dbg_x[k], self.xres[k], reads=[self.bufs["xres"]], writes=[Buf("dbg")])
            self.end_phase()
        self.final_phase()
        self.sc.barrier()
        self.stack.close()
        return nc

    def extra_inputs(self):
        pass


LV_ADAB = 0
LV_N1G = 48
LV_N2G = 56
LV_CONVB = 64
LV_CONVW = 108
NV = 108 + 132


def _fm(v):
    v = np.asarray(v, np.float32)
    return np.ascontiguousarray(v.reshape(-1, 128).T)


def _wfm(w):
    w = np.asarray(w, np.float32)
    K, N = w.shape
    return np.ascontiguousarray(w.reshape(K // 128, 128, N).transpose(1, 0, 2))


def host_inputs_core(I, b):
    m = {}
    xt = np.concatenate([np.asarray(I["x"][b], np.float32), np.asarray(I["ctx"][b], np.float32)], axis=0)
    m["xT_in"] = np.ascontiguousarray(xt.T.reshape(8, 128, T))
    cf = np.stack([_fm(I["c"][b]), _fm(I["c_ctx"])], axis=-1)
    m["cfm"] = np.ascontiguousarray(cf)
    return m


def host_inputs(inputs, b):
    I = inputs
    m = host_inputs_core(inputs, b)
    m["adaw"] = np.stack([_wfm(I["ada_w"][l]) for l in range(DEPTH)])
    lv = np.zeros((DEPTH, 128, NV), np.float32)
    for l in range(DEPTH):
        lv[l, :, LV_ADAB:LV_ADAB + 48] = _fm(I["ada_b"][l])
        lv[l, :, LV_N1G:LV_N1G + 8] = _fm(I["norm1_g"][l])
        lv[l, :, LV_N2G:LV_N2G + 8] = _fm(I["norm2_g"][l])
        lv[l, :, LV_CONVB:LV_CONVB + 44] = _fm(I["ffn_conv_b"][l])
        cw = np.stack([_fm(I["ffn_conv_w"][l][k]) for k in range(3)], axis=-1)
        lv[l, :, LV_CONVW:LV_CONVW + 132] = cw.reshape(128, 132)
    m["lvec"] = lv
    m["fng"] = _fm(I["final_norm_g"])
    m["ffn_up"] = np.stack([_wfm(I["ffn_w_up"][l]) for l in range(DEPTH)])
    m["ffn_dn"] = np.stack([_wfm(I["ffn_w_down"][l]) for l in range(DEPTH)])
    c64, s64 = _rope_tables(64, 2)
    m["cos64"], m["sin64"] = c64, s64
    m["rm64"] = _rot_matrix(64, 2)
    m["gqa_win"] = _wfm(I["gqa_w_in"][0])
    m["gqa_wout"] = _wfm(I["gqa_w_out"][0])
    m["gqa_g"] = np.ascontiguousarray(np.stack([np.tile(np.asarray(I["gqa_q_norm_g"][0], np.float32), 2),
                                                np.tile(np.asarray(I["gqa_k_norm_g"][0], np.float32), 2)], axis=-1))
    return m


def _extra_inputs(self):
    self.inp("mla_win", [128, 8, 672])
    self.inp("mla_g", [128, 5])
    self.inp("mla_wuq", [128, 3, 1536])
    self.inp("mla_wukvk", [128, 2, 1024])
    self.inp("mla_wukvv", [128, 2, 1024])
    self.inp("mla_wout", [128, 8, 1024])
    self.inp("rmq96", [96, 96])
    self.inp("cosq96", [96, T])
    self.inp("sinq96", [96, T])
    self.inp("rmk32", [32, 32])
    self.inp("cosk32", [32, T])
    self.inp("sink32", [32, T])
    self.inp("diff_win", [128, 8, 3072])
    self.inp("diff_wout", [128, 8, 1024])
    self.inp("diff_lam", [128, 256])
    self.inp("diff_subg", [128, 1])
    self.inp("na_win", [128, 8, 3072])
    self.inp("na_wout", [128, 8, 1024])
    self.inp("na_bias", [5, 16, 128, 640])


def _group_rms(self, t, srcs, srcb, w, ngr, gains, gcol0, dst, dstb, tmpq, tmpqb):
    sc = self.sc
    for g in range(ngr):
        sc.op("dve", lambda e, g=g: e.tensor_tensor(out=tmpq[:, 0:w], in0=srcs[g][:, 0:w], in1=srcs[g][:, 0:w], op=ALU.mult), reads=[srcb[g]], writes=[tmpqb])
        sc.op("pe", lambda e, g=g: e.matmul(self.ps[:, 3, 0:w], lhsT=self.ones_f[:], rhs=tmpq[:, 0:w], start=(g == 0), stop=(g == ngr - 1)),
              reads=[tmpqb, self.cb], writes=[t["ms2b"]])
    r2, r2b = t["r2"][0], t["r2b"][0]
    self.rstd(r2[:, 0:w], self.ps[:, 3, 0:w], 1.0 / (ngr * 128), [t["ms2b"]], r2b)
    for g in range(ngr):
        sc.op("dve", lambda e, g=g: e.scalar_tensor_tensor(out=dst[:, g, 0:w], in0=srcs[g][:, 0:w], scalar=gains[:, gcol0 + g:gcol0 + g + 1], in1=r2[:, 0:w],
                                                           op0=ALU.mult, op1=ALU.mult), reads=[srcb[g], r2b, t["gb"]], writes=[dstb])


def _p1_mla(self, l):
    sc = self.sc
    with self.phase() as ph:
        win, winb = self.load_cast(ph, "win", self.din["mla_win"], [128, 8, 672])
        wuq, wuqb = self.load_cast(ph, "wuq", self.din["mla_wuq"], [128, 3, 1536])
        wkk, wkkb = self.load_cast(ph, "wkk", self.din["mla_wukvk"], [128, 2, 1024])
        wkv, wkvb = self.load_cast(ph, "wkv", self.din["mla_wukvv"], [128, 2, 1024])
        tmp = self.norm_tmp(ph)
        t = self.qk_tmp(ph)
        gm = self.sb(ph, "gm", [128, 5], F32)
        rmq = self.sb(ph, "rmq", [128, 96], F32)
        rmk = self.sb(ph, "rmk", [128, 32], F32)
        sc.dma("sp", gm[:], self.din["mla_g"], writes=[t["gb"]])
        sc.dma("sp", rmq[0:96, :], self.din["rmq96"], writes=[t["gb"]])
        sc.dma("sp", rmk[0:32, :], self.din["rmk32"], writes=[t["gb"]])
        xr = self.ring2(ph, "xc", [128, 8, 512], F32)
        hr = self.ring2(ph, "hT", [128, 8, 512], BF16)
        cr = self.ring2(ph, "cs", [128, 4, 512], F32)
        vr = self.vt_ring(ph, 16, 64, True)
        cl = [self.sb(ph, f"cl{g}", [128, 512], F32) for g in range(5)]
        clb = [Buf(f"cl{g}") for g in range(5)]
        cqn = self.sb(ph, "cqn", [128, 3, 512], BF16)
        cqnb = Buf("cqn")
        ckvn = self.sb(ph, "ckvn", [128, 2, 512], BF16)
        ckvnb = Buf("ckvn")
        tq = self.sb(ph, "tq", [128, 512], F32)
        tqb = Buf("tq")
        self.nps = Buf("nps")
        pbs = [Buf("pp1"), Buf("pp2")]
        gi = 0
        for ci, (t0, w) in enumerate(CHUNKS):
            hT, hb, xc, xb = self.chunk_front(ph, tmp, xr, hr, ci, t0, w)
            cs, csb = cr[ci % 2]
            sc.dma("sp", cs[0:96, 0, 0:w], self.din["cosq96"][:, t0:t0 + w], writes=[csb])
            sc.dma("sp", cs[0:96, 1, 0:w], self.din["sinq96"][:, t0:t0 + w], writes=[csb])
            sc.dma("sp", cs[0:32, 2, 0:w], self.din["cosk32"][:, t0:t0 + w], writes=[csb])
            sc.dma("sp", cs[0:32, 3, 0:w], self.din["sink32"][:, t0:t0 + w], writes=[csb])
            for g in range(5):
                bank = 1 + gi % 2
                pb = pbs[gi % 2]
                gi += 1
                self.proj_group(win, winb, g * 128, 128, hT, hb, w, bank, pb)
                sc.op("act", lambda e, g=g, bank=bank: e.activation(out=cl[g][:, 0:w], in_=self.ps[:, bank, 0:w], func=AF.Copy), reads=[pb], writes=[clb[g]])
            _group_rms(self, t, cl[0:3], clb[0:3], w, 3, gm, 0, cqn, cqnb, tq, tqb)
            _group_rms(self, t, cl[3:5], clb[3:5], w, 2, gm, 3, ckvn, ckvnb, tq, tqb)
            bank = 1 + gi % 2
            pb = pbs[gi % 2]
            gi += 1
            self.proj_group(win, winb, 640, 32, hT, hb, w, bank, pb)
            self.qk_post(t, bank, pb, w, None, False, True, (rmk, cs[:, 2, :], cs[:, 3, :], csb),
                         [(self.KT[h, 64:96, t0:t0 + w], slice(0, 32)) for h in range(16)], "KT", rows=32)
            for h in range(16):
                bank = 1 + gi % 2
                pb = pbs[gi % 2]
                gi += 1
                self.proj_group(wuq, wuqb, h * 96, 96, cqn, cqnb, w, bank, pb, nk=3)
                self.qk_post(t, bank, pb, w, None, False, True, (rmq, cs[:, 0, :], cs[:, 1, :], csb), self.QT[h, 0:96, t0:t0 + w], "QT", rows=96)
            for g in range(8):
                bank = 1 + gi % 2
                pb = pbs[gi % 2]
                gi += 1
                self.proj_group(wkk, wkkb, g * 128, 128, ckvn, ckvnb, w, bank, pb, nk=2)
                self.qk_post(t, bank, pb, w, None, False, False, None,
                             [(self.KT[2 * g, 0:64, t0:t0 + w], slice(0, 64)), (self.KT[2 * g + 1, 0:64, t0:t0 + w], slice(64, 128))], "KT")
            self.v_tokmajor(wkv, wkvb, 0, 1024, ckvn, ckvnb, t0, w, vr, 16, 64, True, nk=2)
        self.end_phase()


def _layer_mla(self, l):
    _p1_mla(self, l)
    if self.stop == "p1":
        return
    self.att_std(16, 96, lambda h: self.QT[h, 0:96, :], lambda h: self.KT[h, 0:96, :], lambda h: h * 65, 96 ** -0.5)
    self.out_phase(l, "mla_wout")


def _p1_qkv(self, l, wname, rope, nheads_v, hd_v, ones_col):
    sc = self.sc
    with self.phase() as ph:
        win, winb = self.load_cast(ph, "win", self.din[wname], [128, 8, 3072])
        tmp = self.norm_tmp(ph)
        t = self.qk_tmp(ph)
        rm = self.sb(ph, "rm", [128, 128], F32)
        sc.dma("sp", rm[:], self.din["rm64"], writes=[t["gb"]])
        xr = self.ring2(ph, "xc", [128, 8, 512], F32)
        hr = self.ring2(ph, "hT", [128, 8, 512], BF16)
        cr = self.ring2(ph, "cs", [128, 2, 512], F32)
        vr = self.vt_ring(ph, nheads_v, hd_v, ones_col)
        self.nps = Buf("nps")
        pbs = [Buf("pp1"), Buf("pp2")]
        for ci, (t0, w) in enumerate(CHUNKS):
            hT, hb, xc, xb = self.chunk_front(ph, tmp, xr, hr, ci, t0, w)
            cs, csb = cr[ci % 2]
            if rope:
                sc.dma("sp", cs[:, 0, 0:w], self.din["cos64"][:, t0:t0 + w], writes=[csb])
                sc.dma("sp", cs[:, 1, 0:w], self.din["sin64"][:, t0:t0 + w], writes=[csb])
            for g in range(16):
                bank = 1 + g % 2
                self.proj_group(win, winb, g * 128, 128, hT, hb, w, bank, pbs[g % 2])
                dst = (self.QT[g, :, t0:t0 + w], "QT") if g < 8 else (self.KT[g - 8, :, t0:t0 + w], "KT")
                self.qk_post(t, bank, pbs[g % 2], w, None, False, rope, (rm, cs[:, 0, :], cs[:, 1, :], csb), dst[0], dst[1])
            self.v_tokmajor(win, winb, 2048, 1024, hT, hb, t0, w, vr, nheads_v, hd_v, ones_col)
        self.end_phase()


def _att_diff(self, l):
    sc = self.sc
    lam_init = 0.8 - 0.6 * math.exp(-0.3 * l)
    with self.phase() as ph:
        lam = self.sb(ph, "lam", [128, 256], F32)
        lsc = self.sb(ph, "lsc", [128, 8], F32)
        lb = Buf("lam")
        sc.dma("sp", lam[:], self.din["diff_lam"], writes=[lb])
        sc.dma("sp", lsc[:, 4:5], self.din["diff_subg"], writes=[lb])
        sc.op("dve", lambda e: e.tensor_tensor(out=lam[:, 0:64], in0=lam[:, 0:64], in1=lam[:, 64:128], op=ALU.mult), reads=[lb], writes=[lb])
        sc.op("dve", lambda e: e.tensor_tensor(out=lam[:, 128:192], in0=lam[:, 128:192], in1=lam[:, 192:256], op=ALU.mult), reads=[lb], writes=[lb])
        sc.op("dve", lambda e: e.reduce_sum(out=lsc[:, 0:1], in_=lam[:, 0:64], axis=mybir.AxisListType.X), reads=[lb], writes=[lb])
        sc.op("dve", lambda e: e.reduce_sum(out=lsc[:, 1:2], in_=lam[:, 128:192], axis=mybir.AxisListType.X), reads=[lb], writes=[lb])
        sc.op("act", lambda e: e.activation(out=lsc[:, 0:2], in_=lsc[:, 0:2], func=AF.Exp), reads=[lb], writes=[lb])
        sc.op("dve", lambda e: e.tensor_tensor(out=lsc[:, 2:3], in0=lsc[:, 1:2], in1=lsc[:, 0:1], op=ALU.subtract), reads=[lb], writes=[lb])
        sc.op("dve", lambda e: e.tensor_scalar(out=lsc[:, 2:3], in0=lsc[:, 2:3], scalar1=-lam_init, scalar2=None, op0=ALU.add), reads=[lb], writes=[lb])
        sc.op("dve", lambda e: e.tensor_scalar(out=lsc[:, 5:6], in0=lsc[:, 4:5], scalar1=1.0 - lam_init, scalar2=None, op0=ALU.mult), reads=[lb], writes=[lb])
        qr = self.ring2(ph, "aq", [128, T], BF16)
        kr = self.ring2(ph, "ak", [128, T], BF16)
        vr = self.ring2(ph, "av", [128, 66, 128], BF16)
        pr = [(self.sb(ph, f"pT{i}", [128, 2, 512], BF16), Buf(f"pT{i}")) for i in range(3)]
        r0 = self.sb(ph, "r0", [128, 512], F32)
        r1 = self.sb(ph, "r1", [128, 512], F32)
        o = self.sb(ph, "o", [128, 512], F32)
        o2 = self.sb(ph, "o2", [128, 512], F32)
        fb = Buf("fin")
        onr = self.ring2(ph, "on", [128, 512], BF16)
        sb_ = [Buf("S0"), Buf("S1")]
        accb = [Buf(f"acc{i}") for i in range(4)]
        jobs = []
        for h in range(8):
            for ci, (t0, w) in enumerate(CHUNKS):
                kbs = list(range(66)) if t0 < S else [64, 65]
                for ki, kb in enumerate(kbs):
                    jobs.append((h, t0, w, kb, ki == 0, ki == len(kbs) - 1))
        loaded = set()

        def load(h):
            if h in loaded or h >= 8:
                return
            loaded.add(h)
            (qt, qb), (kt, kb_), (vt, vb) = qr[h % 2], kr[h % 2], vr[h % 2]
            sc.dma("sp", qt[:], self.QT[h], reads=[self.bufs["QT"]], writes=[qb])
            sc.dma("sp", kt[:], self.KT[h], reads=[self.bufs["KT"]], writes=[kb_])
            sc.dma("sp", vt[:], self.Vd[:, :, h * 128:(h + 1) * 128].rearrange("b p d -> p b d"), reads=[self.bufs["Vd"]], writes=[vb])

        def emit_S(n):
            h, t0, w, kb, first, last = jobs[n]
            load(h)
            (qt, qb), (kt, kb_) = qr[h % 2], kr[h % 2]
            sbank = 2 * (n % 2)
            for j in range(2):
                sc.op("pe", lambda e, j=j: e.matmul(
                    self.ps[:, sbank + j, 0:w], lhsT=kt[64 * j:64 * j + 64, kb * 128:(kb + 1) * 128], rhs=qt[64 * j:64 * j + 64, t0:t0 + w],
                    start=True, stop=True), reads=[qb, kb_], writes=[sb_[n % 2]])

        fi = 0
        emit_S(0)
        for n in range(len(jobs)):
            h, t0, w, kb, first, last = jobs[n]
            if n + 1 < len(jobs):
                emit_S(n + 1)
            vt, vb = vr[h % 2]
            sbank = 2 * (n % 2)
            pT, pb = pr[n % 3]
            sc.op("act", lambda e: e.activation(out=pT[:, :, 0:w], in_=self.ps[:, sbank:sbank + 2, 0:w], func=AF.Exp, scale=0.125), reads=[sb_[n % 2]], writes=[pb])
            for j in range(2):
                sc.op("pe", lambda e, j=j: e.matmul(self.ps[:, 4 + 2 * j, 0:w], lhsT=vt[:, kb, :], rhs=pT[:, j, 0:w], start=first, stop=last),
                      reads=[pb, vb], writes=[accb[2 * j]])
                sc.op("pe", lambda e, j=j: e.matmul(self.ps[:, 5 + 2 * j, 0:w], lhsT=self.ones_b[:], rhs=pT[:, j, 0:w], start=first, stop=last),
                      reads=[pb, self.cb], writes=[accb[2 * j + 1]])
            if t0 == 0 and first:
                load(h + 1)
            if not last:
                continue
            on, onb = onr[fi % 2]
            fi += 1
            sc.op("dve", lambda e: e.reciprocal(out=r0[:, 0:w], in_=self.ps[:, 5, 0:w]), reads=[accb[1]], writes=[fb])
            sc.op("dve", lambda e: e.reciprocal(out=r1[:, 0:w], in_=self.ps[:, 7, 0:w]), reads=[accb[3]], writes=[fb])
            sc.op("dve", lambda e: e.tensor_tensor(out=r0[:, 0:w], in0=r0[:, 0:w], in1=self.ps[:, 4, 0:w], op=ALU.mult), reads=[accb[0], fb], writes=[fb])
            sc.op("dve", lambda e: e.tensor_tensor(out=r1[:, 0:w], in0=r1[:, 0:w], in1=self.ps[:, 6, 0:w], op=ALU.mult), reads=[accb[2], fb], writes=[fb])
            sc.op("dve", lambda e: e.scalar_tensor_tensor(out=o[:, 0:w], in0=r1[:, 0:w], scalar=lsc[:, 2:3], in1=r0[:, 0:w], op0=ALU.mult, op1=ALU.add),
                  reads=[fb, lb], writes=[fb])
            sc.op("dve", lambda e: e.tensor_tensor(out=o2[:, 0:w], in0=o[:, 0:w], in1=o[:, 0:w], op=ALU.mult), reads=[fb], writes=[fb])
            sc.op("pe", lambda e: e.matmul(self.ps[:, 5, 0:w], lhsT=self.ones_f[:], rhs=o2[:, 0:w], start=True, stop=True),
                  reads=[fb, self.cb], writes=[accb[1]])
            self.rstd(o2[:, 0:w], self.ps[:, 5, 0:w], 1.0 / 128, [accb[1], fb], fb)
            sc.op("dve", lambda e: e.scalar_tensor_tensor(out=on[:, 0:w], in0=o[:, 0:w], scalar=lsc[:, 5:6], in1=o2[:, 0:w], op0=ALU.mult, op1=ALU.mult),
                  reads=[fb, lb], writes=[onb])
            sc.dma("pool", self.aoT[h, :, t0:t0 + w], on[:, 0:w], reads=[onb], writes=[self.bufs["aoT"]])
        self.end_phase()


def _layer_diff(self, l):
    _p1_qkv(self, l, "diff_win", True, 8, 128, False)
    if self.stop == "p1":
        return
    _att_diff(self, l)
    self.out_phase(l, "diff_wout")


def _att_na(self, l):
    sc = self.sc
    with self.phase() as ph:
        qr = self.ring2(ph, "aq", [64, S], BF16)
        kr = self.ring2(ph, "ak", [64, T], BF16)
        vr = self.ring2(ph, "av", [128, 66, 65], BF16)
        br = self.ring2(ph, "nb", [128, 5, 640], F32)
        tbr = self.ring2(ph, "tb", [128, 640], F32)
        pr = [(self.sb(ph, f"pT{i}", [128, 896], BF16), Buf(f"pT{i}")) for i in range(3)]
        our = self.ring2(ph, "ou", [64, 128], F32)
        onr = self.ring2(ph, "on", [64, 512], BF16)
        rdr = self.ring2(ph, "rd", [128, 128], F32)
        sb_ = [Buf("S0"), Buf("S1")]
        accb = [Buf("acc0"), Buf("acc1")]
        bcb = Buf("bc")
        jobs = [(h, j) for h in range(16) for j in range(64)]
        loaded = set()

        def blocks(j):
            b0 = min(max(2 * j - 4, 0), 118)
            return [b0 // 2 + m for m in range(5)] + [64, 65]

        def load(h):
            if h in loaded or h >= 16:
                return
            loaded.add(h)
            (qt, qb), (kt, kb_), (vt, vb), (bt, bb) = qr[h % 2], kr[h % 2], vr[h % 2], br[h % 2]
            rows = slice((h % 2) * 64, (h % 2) * 64 + 64)
            sc.dma("sp", qt[:], self.QT[h // 2, rows, 0:S], reads=[self.bufs["QT"]], writes=[qb])
            sc.dma("sp", kt[:], self.KT[h // 2, rows, :], reads=[self.bufs["KT"]], writes=[kb_])
            sc.dma("sp", vt[:], self.Vd[:, :, h * 65:h * 65 + 65].rearrange("b p d -> p b d"), reads=[self.bufs["Vd"]], writes=[vb])
            sc.dma("sp", bt[:], self.din["na_bias"][:, h].rearrange("t p c -> p t c"), writes=[bb])

        def emit_S(n):
            h, j = jobs[n]
            load(h)
            (qt, qb), (kt, kb_) = qr[h % 2], kr[h % 2]
            sbank = 2 * (n % 2)
            for m, blk in enumerate(blocks(j)):
                bank, c0 = (sbank, m * 128) if m < 4 else (sbank + 1, (m - 4) * 128)
                sc.op("pe", lambda e: e.matmul(self.ps[:, bank, c0:c0 + 128], lhsT=kt[:, blk * 128:(blk + 1) * 128], rhs=qt[:, j * 128:(j + 1) * 128],
                                              start=True, stop=True), reads=[qb, kb_], writes=[sb_[n % 2]])

        pending = [None]

        def fin2():
            if pending[0] is None:
                return
            (h, j, ou, oub, on, onb, rd, rdb) = pending[0]
            pending[0] = None
            rows = slice((h % 2) * 64, (h % 2) * 64 + 64)
            jj = j % 4
            sc.op("pe", lambda e: e.matmul(self.ps[0:64, 6, 0:128], lhsT=self.ones_f[64:65, 0:64], rhs=rd[64:65, :], start=True, stop=True),
                  reads=[rdb, self.cb], writes=[bcb])
            sc.op("dve", lambda e: e.tensor_tensor(out=on[:, jj * 128:(jj + 1) * 128], in0=ou[:, :], in1=self.ps[0:64, 6, 0:128], op=ALU.mult),
                  reads=[oub, bcb], writes=[onb])
            if jj == 3:
                t0 = (j // 4) * 512
                sc.dma("pool", self.aoT[h // 2, rows, t0:t0 + 512], on[:, :], reads=[onb], writes=[self.bufs["aoT"]])

        emit_S(0)
        for n in range(len(jobs)):
            h, j = jobs[n]
            if n + 1 < len(jobs):
                emit_S(n + 1)
            (vt, vb), (bt, bb) = vr[h % 2], br[h % 2]
            pat = {0: 1, 1: 2, 62: 3, 63: 4}.get(j, 0)
            s2 = n % 2
            sbank = 2 * s2
            tb, tbb = tbr[s2]
            pT, pb = pr[n % 3]
            sc.op("dve", lambda e: e.scalar_tensor_tensor(out=tb[:, 0:512], in0=self.ps[:, sbank, 0:512], scalar=0.125, in1=bt[:, pat, 0:512],
                                                          op0=ALU.mult, op1=ALU.add), reads=[sb_[s2], bb], writes=[tbb])
            sc.op("dve", lambda e: e.scalar_tensor_tensor(out=tb[:, 512:640], in0=self.ps[:, sbank + 1, 0:128], scalar=0.125, in1=bt[:, pat, 512:640],
                                                          op0=ALU.mult, op1=ALU.add), reads=[sb_[s2], bb], writes=[tbb])
            sc.op("act", lambda e: e.activation(out=pT[:, 0:640], in_=tb[:, 0:640], func=AF.Exp), reads=[tbb], writes=[pb])
            sc.op("act", lambda e: e.activation(out=pT[:, 640:896], in_=self.ps[:, sbank + 1, 128:384], func=AF.Exp, scale=0.125),
                  reads=[sb_[s2]], writes=[pb])
            abank = 4 + s2
            for m, blk in enumerate(blocks(j)):
                sc.op("pe", lambda e: e.matmul(self.ps[0:65, abank, 0:128], lhsT=vt[:, blk, 0:65], rhs=pT[:, m * 128:(m + 1) * 128], start=(m == 0), stop=(m == 6)),
                      reads=[pb, vb], writes=[accb[s2]])
            if j == 0:
                load(h + 1)
            fin2()
            ou, oub = our[s2]
            on, onb = onr[(j // 4) % 2]
            rd, rdb = rdr[s2]
            sc.op("dve", lambda e: e.tensor_copy(out=ou[:, :], in_=self.ps[0:64, abank, 0:128]), reads=[accb[s2]], writes=[oub])
            sc.op("dve", lambda e: e.reciprocal(out=rd[64:65, :], in_=self.ps[64:65, abank, 0:128]), reads=[accb[s2]], writes=[rdb])
            pending[0] = (h, j, ou, oub, on, onb, rd, rdb)
        fin2()
        self.end_phase()


def _layer_na(self, l):
    _p1_qkv(self, l, "na_win", False, 16, 64, True)
    if self.stop == "p1":
        return
    _att_na(self, l)
    self.out_phase(l, "na_wout", do_ctx=False)


Builder.extra_inputs = _extra_inputs
Builder.layer_mla = _layer_mla
Builder.layer_diff = _layer_diff
Builder.layer_na = _layer_na


def _na_bias_tables(rpb):
    rpb = np.asarray(rpb, np.float32)
    out = np.empty((5, 16, 128, 640), np.float32)
    q = np.arange(128)
    qdr, qcol = q // 64, q % 64
    i = np.arange(640)
    for p, j in enumerate((2, 0, 1, 62, 63)):
        r = 2 * j + qdr
        rs = np.clip(r - 4, 0, 120)
        b0 = min(max(2 * j - 4, 0), 118)
        krow, kcol = b0 + i // 64, i % 64
        cs = np.clip(qcol - 8, 0, 48)
        inw = ((kcol[:, None] >= cs[None]) & (kcol[:, None] < cs[None] + 16) & (krow[:, None] >= rs[None]) & (krow[:, None] < rs[None] + 8))
        dr = np.clip(krow[:, None] - r[None] + 7, 0, 14)
        dc = np.clip(kcol[:, None] - qcol[None] + 15, 0, 30)
        for h in range(16):
            tbl = np.where(inw, rpb[h][dr, dc], np.float32(NEG)).astype(np.float32)
            out[p, h] = tbl.reshape(5, 128, 128).transpose(1, 0, 2).reshape(128, 640)
    return out


def host_inputs_extra(I, m):
    m["mla_win"] = _wfm(I["mla_w_in"][0])
    m["mla_g"] = np.ascontiguousarray(np.concatenate([_fm(I["mla_q_norm_g"][0]), _fm(I["mla_kv_norm_g"][0])], axis=1))
    m["mla_wuq"] = _wfm(I["mla_w_uq"][0])
    wukv = np.asarray(I["mla_w_ukv"][0], np.float32).reshape(256, 16, 128)
    m["mla_wukvk"] = _wfm(np.ascontiguousarray(wukv[:, :, :64]).reshape(256, 1024))
    m["mla_wukvv"] = _wfm(np.ascontiguousarray(wukv[:, :, 64:]).reshape(256, 1024))
    m["mla_wout"] = _wfm(I["mla_w_out"][0])
    c32, s32 = _rope_tables(32, 1)
    cq = np.ones((96, T), np.float32)
    sq = np.zeros((96, T), np.float32)
    cq[64:], sq[64:] = c32, s32
    m["cosq96"], m["sinq96"] = cq, sq
    m["cosk32"], m["sink32"] = c32, s32
    m["rmq96"] = _rot_matrix(32, 1, offset=64)
    m["rmk32"] = _rot_matrix(32, 1)
    m["diff_win"] = _wfm(I["diff_w_in"][0])
    m["diff_wout"] = _wfm(I["diff_w_out"][0])
    m["diff_lam"] = np.ascontiguousarray(np.broadcast_to(np.asarray(I["diff_lambda"][0], np.float32).reshape(1, 256), (128, 256)))
    m["diff_subg"] = _fm(I["diff_subln_g"][0])
    m["na_win"] = _wfm(I["na_w_in"][0])
    m["na_wout"] = _wfm(I["na_w_out"][0])
    m["na_bias"] = _na_bias_tables(I["na_rpb"][0])
    return m


_NC_CACHE = {}


def kernel(**inputs):
    if "nc" not in _NC_CACHE:
        b = Builder()
        _NC_CACHE["nc"] = b.build()
        _NC_CACHE["names"] = list(b.din.keys())
    nc = _NC_CACHE["nc"]
    shared = host_inputs(inputs, 0)
    host_inputs_extra(inputs, shared)
    in_maps = []
    for b in range(NCORES):
        m = dict(shared)
        if b > 0:
            m.update(host_inputs_core(inputs, b))
        in_maps.append({k: m[k] for k in _NC_CACHE["names"]})
    res = run_bass_kernel_spmd(nc, in_maps, core_ids=list(range(NCORES)))
    out = np.empty((NCORES, S, D), np.float32)
    for b in range(NCORES):
        out[b] = np.asarray(res.results[b]["outT"]).reshape(D, S).T
    return out
```

```python
import math
import numpy as np
import concourse.bass as bass
import concourse.mybir as mybir
from concourse.bass_utils import run_bass_kernel_spmd

F32 = mybir.dt.float32
BF16 = mybir.dt.bfloat16
AF = mybir.ActivationFunctionType
ALU = mybir.AluOpType

D = 1024
S = 8192
C = 256
T = S + C
GRID_W = 64
DEPTH = 4
FFN = 2816
EPS = 1e-6
NEG = -1e30
NCORES = 4

CHUNKS = [(i * 512, 512) for i in range(S // 512)] + [(S, C)]


class Dep:
    __slots__ = ("sem", "val", "key", "eng", "lval")

    def __init__(self, sem, val, key, eng, lval):
        self.sem, self.val, self.key, self.eng, self.lval = sem, val, key, eng, lval


class Buf:
    __slots__ = ("w", "r", "multi", "name")

    def __init__(self, name="", multi=False):
        self.w = {}
        self.r = {}
        self.multi = multi
        self.name = name


class Sched:
    NRING = 8
    EPOCH = 30000
    DEPOCH = 1800

    def __init__(self, nc, stack):
        self.nc = nc
        self.stack = stack
        self.h = {"pe": nc.tensor, "act": nc.scalar, "dve": nc.vector, "pool": nc.gpsimd, "sp": nc.sync}
        self.sems = {}
        self.cnt = {}
        self.known = {}
        for e in self.h:
            self.sems[e] = []
            self.cnt[e] = 0
            self.known[e] = {}
        self.ring = {}
        for q in ("sp", "pool", "act"):
            self.ring[q] = {"sems": [[] for _ in range(self.NRING)], "cnt": [0] * self.NRING, "n": 0}
        self.latest = {}
        self.nsem = 0

    def _sem(self, lst, ep, name):
        while len(lst) <= ep:
            self.nsem += 1
            lst.append(self.stack.enter_context(self.nc.semaphore(f"{name}_{len(lst)}")))
        return lst[ep]

    def _wait(self, eng, d):
        kn = self.known[eng]
        if kn.get(d.key, 0) >= d.val:
            return
        self.h[eng].wait_ge(d.sem, d.lval)
        kn[d.key] = d.val

    def _waits(self, eng, reads, writes):
        deps = {}

        def add(d):
            o = deps.get(d.key)
            if o is None or o.val < d.val:
                deps[d.key] = d

        for b in reads:
            for d in b.w.values():
                add(d)
        for b in writes:
            for d in b.r.values():
                if d.eng == eng and d.key == eng:
                    continue
                add(d)
            if not b.multi:
                for d in b.w.values():
                    if d.eng == eng and d.key == eng:
                        continue
                    add(d)
        for d in deps.values():
            if d.key == "pe" and eng == "pe":
                continue
            self._wait(eng, d)

    def _record(self, me, reads, writes):
        for b in reads:
            b.r[me.key] = me
        for b in writes:
            if b.multi:
                b.w[me.key] = me
            else:
                b.w = {me.key: me}
                b.r = {}
        self.latest[me.key] = me

    def op(self, eng, fn, reads=(), writes=()):
        self._waits(eng, reads, writes)
        ins = fn(self.h[eng])
        c = self.cnt[eng]
        ep, lv = c // self.EPOCH, c % self.EPOCH + 1
        sem = self._sem(self.sems[eng], ep, "s_" + eng)
        self.cnt[eng] = c + 1
        ins.then_inc(sem, 1)
        me = Dep(sem, c + 1, eng, eng, lv)
        self._record(me, reads, writes)
        return ins

    def dma(self, q, out, in_, reads=(), writes=()):
        self._waits(q, reads, writes)
        rg = self.ring[q]
        i = rg["n"] % self.NRING
        rg["n"] += 1
        c = rg["cnt"][i]
        ep, lv = c // self.DEPOCH, (c % self.DEPOCH + 1) * 16
        sem = self._sem(rg["sems"][i], ep, f"d_{q}{i}")
        rg["cnt"][i] = c + 1
        self.h[q].dma_start(out=out, in_=in_).then_inc(sem, 16)
        me = Dep(sem, c + 1, (q, i), q, lv)
        self._record(me, reads, writes)

    def barrier(self):
        for e in self.h:
            for d in self.latest.values():
                if d.key == e:
                    continue
                self._wait(e, d)


def _rope_tables(rot_dim, nrep, scale=1.0):
    t = np.arange(S)
    row = (t // GRID_W).astype(np.float32)
    col = (t % GRID_W).astype(np.float32)
    half = rot_dim // 2
    inv = (10000.0 ** (-np.arange(0, half, 2, dtype=np.float32) / half)).astype(np.float32)
    ar, ac = row[:, None] * inv, col[:, None] * inv
    ang = np.concatenate([ar, ar, ac, ac], axis=-1)
    cos = np.ones((T, rot_dim), np.float32)
    sin = np.zeros((T, rot_dim), np.float32)
    cos[:S] = np.cos(ang)
    sin[:S] = np.sin(ang)
    cosT = np.tile(cos.T, (nrep, 1)).astype(np.float32)
    sinT = np.tile(sin.T, (nrep, 1)).astype(np.float32)
    return np.ascontiguousarray(cosT), np.ascontiguousarray(sinT)


def _rot_matrix(rot_dim, nrep, offset=0, total=None):
    n = nrep * rot_dim + offset if total is None else total
    m = np.zeros((n, n), np.float32)
    q = rot_dim // 4
    for r in range(nrep):
        b = offset + r * rot_dim
        for i in range(q):
            m[b + q + i, b + i] = -1.0
            m[b + i, b + q + i] = 1.0
            m[b + 3 * q + i, b + 2 * q + i] = -1.0
            m[b + 2 * q + i, b + 3 * q + i] = 1.0
    return m


class Builder:
    def __init__(self, layers=(0, 1, 2, 3), debug=(), stop=None):
        from contextlib import ExitStack
        self.stop = stop
        self.layers = layers
        self.debug = debug
        self.stack = ExitStack()
        nc = self.nc = bass.Bass("TRN2", target_bir_lowering=False)
        self.sc = Sched(nc, self.stack)
        self.din = {}
        self.bufs = {}

    def inp(self, name, shape, dt=F32):
        t = self.nc.dram_tensor(name, list(shape), dt, kind="ExternalInput").ap()
        self.din[name] = t
        self.bufs[name] = Buf(name, multi=True)
        return t

    def scratch(self, name, shape, dt, kind="Internal"):
        t = self.nc.dram_tensor(name, list(shape), dt, kind=kind).ap()
        self.bufs[name] = Buf(name, multi=True)
        return t

    def sb(self, ph, name, shape, dt):
        self._uid = getattr(self, "_uid", 0) + 1
        t = ph.enter_context(self.nc.sbuf_tensor(f"sb{self._uid}_{name}", list(shape), dt))
        return t

    def load_cast(self, ph, name, src_ap, shape, q="sp", cast_eng="pool", piece=2048):
        sc = self.sc
        dst = self.sb(ph, name, shape, BF16)
        dbuf = Buf(name, multi=True)
        a, b = shape[1], shape[2]
        if not hasattr(self, "_stg") or self._stg_ph is not ph:
            self._stg = [self.sb(ph, f"stg{i}", [128, piece], F32) for i in range(3)]
            self._stgb = [Buf(f"stg{i}") for i in range(3)]
            self._dq = 0
            self._stg_ph = ph
            self._stg_i = 0
        for ai in range(a):
            for b0 in range(0, b, piece):
                w = min(piece, b - b0)
                i = self._stg_i % 3
                self._stg_i += 1
                st, sb_ = self._stg[i], self._stgb[i]
                self._dq += 1
                sc.dma(("sp", "pool")[self._dq % 2], st[:, 0:w], src_ap[:, ai, b0:b0 + w], reads=[], writes=[sb_])
                ce = ("pool", "act", "dve")[self._stg_i % 3]
                if ce == "act":
                    sc.op("act", lambda e, st=st, w=w, ai=ai, b0=b0: e.activation(out=dst[:, ai, b0:b0 + w], in_=st[:, 0:w], func=AF.Copy),
                          reads=[sb_], writes=[dbuf])
                else:
                    sc.op(ce, lambda e, st=st, w=w, ai=ai, b0=b0: e.tensor_copy(out=dst[:, ai, b0:b0 + w], in_=st[:, 0:w]),
                          reads=[sb_], writes=[dbuf])
        return dst, dbuf

    def phase(self):
        from contextlib import ExitStack
        return ExitStack()

    def end_phase(self):
        self.sc.barrier()
        for b in self.bufs.values():
            b.w = {}
            b.r = {}
        self._stg_ph = None

    def consts(self):
        nc, sc, st = self.nc, self.sc, self.stack
        self.ps = st.enter_context(nc.psum_tensor("ps", [128, 8, 512], F32))
        self.ones_f = st.enter_context(nc.sbuf_tensor("c_ones_f", [128, 128], F32))
        self.ones_b = st.enter_context(nc.sbuf_tensor("c_ones_b", [128, 128], BF16))
        self.bd64 = st.enter_context(nc.sbuf_tensor("c_bd64", [128, 128], F32))
        self.mhalf = st.enter_context(nc.sbuf_tensor("c_mhalf", [128, 512], F32))
        self.modv = st.enter_context(nc.sbuf_tensor("c_modv", [128, 2, 48], F32))
        self.lvec = st.enter_context(nc.sbuf_tensor("c_lvec", [128, NV], F32))
        self.scv = st.enter_context(nc.sbuf_tensor("c_scv", [128, 8, 2], F32))
        self.epst = st.enter_context(nc.sbuf_tensor("c_eps", [128, 1], F32))
        self.sel64 = st.enter_context(nc.sbuf_tensor("c_sel64", [128, 128], F32))
        self.cb = Buf("consts")
        self.modb = Buf("modv")
        self.lvb = Buf("lvec")
        self.scb = Buf("scv")
        sc.op("dve", lambda e: e.memset(self.ones_f[:], 1.0), writes=[self.cb])
        sc.op("dve", lambda e: e.memset(self.ones_b[:], 1.0), writes=[self.cb])
        sc.op("dve", lambda e: e.memset(self.bd64[:], 0.0), writes=[self.cb])
        sc.op("dve", lambda e: e.memset(self.bd64[0:64, 0:64], 1.0), writes=[self.cb])
        sc.op("dve", lambda e: e.memset(self.bd64[64:128, 64:128], 1.0), writes=[self.cb])
        sc.op("dve", lambda e: e.memset(self.mhalf[:], -0.5), writes=[self.cb])
        sc.op("dve", lambda e: e.memset(self.epst[:], EPS), writes=[self.cb])
        sc.op("dve", lambda e: e.memset(self.sel64[:], 0.0), writes=[self.cb])
        sc.op("dve", lambda e: e.memset(self.sel64[64:65, :], 1.0), writes=[self.cb])
        sc.dma("sp", self.scv[:], self.din["cfm"], writes=[self.scb])
        sc.op("act", lambda e: e.activation(out=self.scv[:], in_=self.scv[:], func=AF.Silu), reads=[self.scb], writes=[self.scb])

    def mod_phase(self, l):
        nc, sc = self.nc, self.sc
        with self.phase() as ph:
            sc.dma("sp", self.lvec[:], self.din["lvec"][l], writes=[self.lvb])
            wst = [self.sb(ph, f"adaw{i}", [128, 8, 512], F32) for i in range(2)]
            wb = [Buf(f"adaw{i}") for i in range(2)]
            mps = self.ps[:, 0, 0:96].rearrange("p (g s) -> p g s", s=2)
            mpb = Buf("modps")
            for pc in range(12):
                i = pc % 2
                sc.dma("sp" if pc % 2 == 0 else "pool", wst[i][:], self.din["adaw"][l, :, :, pc * 512:(pc + 1) * 512], writes=[wb[i]])
                for g4 in range(4):
                    g = pc * 4 + g4
                    for kc in range(8):
                        sc.op("pe", lambda e, i=i, g4=g4, kc=kc, g=g: e.matmul(
                            mps[:, g, :], lhsT=wst[i][:, kc, g4 * 128:(g4 + 1) * 128], rhs=self.scv[:, kc, :],
                            start=(kc == 0), stop=(kc == 7)), reads=[wb[i], self.scb], writes=[mpb])
            mod = self.sb(ph, "modraw", [128, 2, 48], F32)
            mb = Buf("modraw")
            for s_ in range(2):
                sc.op("dve", lambda e, s_=s_: e.tensor_tensor(out=mod[:, s_, :], in0=mps[:, :, s_], in1=self.lvec[:, LV_ADAB:LV_ADAB + 48], op=ALU.add),
                      reads=[mpb, self.lvb], writes=[mb])
            for s_ in range(2):
                for which, (shc, scc, gc, ng) in enumerate(((0, 8, 16, LV_N1G), (24, 32, 40, LV_N2G))):
                    o = which * 24
                    sc.op("dve", lambda e, s_=s_, scc=scc, ng=ng, o=o: e.scalar_tensor_tensor(
                        out=self.modv[:, s_, o:o + 8], in0=mod[:, s_, scc:scc + 8], scalar=1.0, in1=self.lvec[:, ng:ng + 8],
                        op0=ALU.add, op1=ALU.mult), reads=[mb, self.lvb], writes=[self.modb])
                    sc.op("dve", lambda e, s_=s_, shc=shc, o=o: e.tensor_copy(out=self.modv[:, s_, o + 8:o + 16], in_=mod[:, s_, shc:shc + 8]),
                          reads=[mb], writes=[self.modb])
                    sc.op("dve", lambda e, s_=s_, gc=gc, o=o: e.tensor_copy(out=self.modv[:, s_, o + 16:o + 24], in_=mod[:, s_, gc:gc + 8]),
                          reads=[mb], writes=[self.modb])
            self.end_phase()

    def rstd(self, out_ap, in_ap, inv_n, reads, outb, rows=128):
        sc = self.sc
        sc.op("act", lambda e: e.activation(out=out_ap, in_=in_ap, func=AF.Sqrt, bias=self.epst[0:rows, 0:1], scale=inv_n), reads=list(reads) + [self.cb], writes=[outb])
        sc.op("dve", lambda e: e.reciprocal(out=out_ap, in_=out_ap), reads=[outb], writes=[outb])

    def norm_mod(self, xc, xb, w, stream, which, hT, hb, tmp, psb):
        sc = self.sc
        o = which * 24
        sq, sqb, r, rb, h1, h1b = tmp["sq"], tmp["sqb"], tmp["r"], tmp["rb"], tmp["h1"], tmp["h1b"]
        bank, pb = psb
        sc.op("act", lambda e: e.activation(out=sq[:, :, 0:w], in_=xc[:, :, 0:w], func=AF.Square), reads=[xb], writes=[sqb])
        for kc in range(8):
            sc.op("pe", lambda e, kc=kc: e.matmul(self.ps[:, bank, 0:w], lhsT=self.ones_f[:], rhs=sq[:, kc, 0:w], start=(kc == 0), stop=(kc == 7)),
                  reads=[sqb, self.cb], writes=[pb])
        self.rstd(r[:, 0:w], self.ps[:, bank, 0:w], 1.0 / D, [pb], rb)
        if hT is None:
            return
        A = self.modv[:, stream, o:o + 8].unsqueeze(2).to_broadcast([128, 8, w])
        rbc = r[:, 0:w].unsqueeze(1).to_broadcast([128, 8, w])
        sc.op("dve", lambda e: e.tensor_tensor(out=h1[:, :, 0:w], in0=xc[:, :, 0:w], in1=A, op=ALU.mult), reads=[xb, self.modb], writes=[h1b])
        sc.op("dve", lambda e: e.tensor_tensor(out=h1[:, :, 0:w], in0=h1[:, :, 0:w], in1=rbc, op=ALU.mult), reads=[h1b, rb], writes=[h1b])
        for kc in range(8):
            sc.op("act", lambda e, kc=kc: e.activation(out=hT[:, kc, 0:w], in_=h1[:, kc, 0:w], func=AF.Identity,
                                                       bias=self.modv[:, stream, o + 8 + kc:o + 9 + kc], scale=1.0), reads=[h1b, self.modb], writes=[hb])

    def norm_tmp(self, ph):
        return {"sq": self.sb(ph, "nsq", [128, 8, 512], F32), "sqb": Buf("nsq"),
                "r": self.sb(ph, "nr", [128, 512], F32), "rb": Buf("nr"),
                "h1": self.sb(ph, "nh1", [128, 8, 512], F32), "h1b": Buf("nh1")}

    def xres_ap(self, t0, w):
        return self.xres[:, :, t0:t0 + w].rearrange("k p t -> p k t")

    def qk_post(self, ph_t, src_bank, srcb, w, gain_ap, normalize, rope, cs, dst_ap, dstb_name, rows=128):
        sc = self.sc
        t = ph_t
        i = t["i"] % 2
        t["i"] += 1
        qs, qsb = t["qs"][i], t["qsb"][i]
        q2, q2b = t["q2"][i], t["q2b"][i]
        r2, r2b = t["r2"][i], t["r2b"][i]
        qn, qnb = t["qn"][i], t["qnb"][i]
        qf, qfb = t["qf"][i], t["qfb"][i]
        R = slice(0, rows)
        if not normalize and not rope:
            sc.op("act", lambda e: e.activation(out=qf[R, 0:w], in_=self.ps[R, src_bank, 0:w], func=AF.Copy), reads=[srcb], writes=[qfb])
            dsts = dst_ap if isinstance(dst_ap, list) else [(dst_ap, slice(0, rows))]
            for (d_ap, rs) in dsts:
                sc.dma("pool", d_ap, qf[rs, 0:w], reads=[qfb], writes=[self.bufs[dstb_name]])
            return
        sc.op("act", lambda e: e.activation(out=qs[R, 0:w], in_=self.ps[R, src_bank, 0:w], func=AF.Copy), reads=[srcb], writes=[qsb])
        cur, curb = qs, qsb
        if normalize:
            sc.op("dve", lambda e: e.tensor_tensor(out=q2[R, 0:w], in0=qs[R, 0:w], in1=qs[R, 0:w], op=ALU.mult), reads=[qsb], writes=[q2b])
            sc.op("pe", lambda e: e.matmul(self.ps[R, 3, 0:w], lhsT=self.bd64[R, R], rhs=q2[R, 0:w], start=True, stop=True), reads=[q2b, self.cb], writes=[t["ms2b"]])
            self.rstd(r2[R, 0:w], self.ps[R, 3, 0:w], 1.0 / 64, [t["ms2b"]], r2b, rows=rows)
            sc.op("dve", lambda e: e.scalar_tensor_tensor(out=qn[R, 0:w], in0=qs[R, 0:w], scalar=gain_ap, in1=r2[R, 0:w], op0=ALU.mult, op1=ALU.mult),
                  reads=[qsb, r2b, t["gb"]], writes=[qnb])
            cur, curb = qn, qnb
        if rope:
            rm, cos, sin, csb = cs
            sc.op("pe", lambda e: e.matmul(self.ps[R, 4, 0:w], lhsT=rm[R, R], rhs=cur[R, 0:w], start=True, stop=True), reads=[curb, t["gb"]], writes=[t["rotb"]])
            sc.op("dve", lambda e: e.tensor_tensor(out=q2[R, 0:w], in0=cur[R, 0:w], in1=cos[R, 0:w], op=ALU.mult), reads=[curb, csb], writes=[q2b])
            sc.op("dve", lambda e: e.tensor_tensor(out=r2[R, 0:w], in0=self.ps[R, 4, 0:w], in1=sin[R, 0:w], op=ALU.mult), reads=[t["rotb"], csb], writes=[r2b])
            sc.op("dve", lambda e: e.tensor_tensor(out=qf[R, 0:w], in0=q2[R, 0:w], in1=r2[R, 0:w], op=ALU.add), reads=[q2b, r2b], writes=[qfb])
        else:
            sc.op("act", lambda e: e.activation(out=qf[R, 0:w], in_=cur[R, 0:w], func=AF.Copy), reads=[curb], writes=[qfb])
        dsts = dst_ap if isinstance(dst_ap, list) else [(dst_ap, slice(0, rows))]
        for (d_ap, rs) in dsts:
            sc.dma("sp", d_ap, qf[rs, 0:w], reads=[qfb], writes=[self.bufs[dstb_name]])

    def qk_tmp(self, ph):
        t = {"i": 0, "ms2b": Buf("ms2"), "rotb": Buf("rot"), "gb": Buf("gains")}
        for nm, dt in (("qs", F32), ("q2", F32), ("r2", F32), ("qn", F32), ("qf", BF16)):
            t[nm] = [self.sb(ph, f"{nm}{i}", [128, 512], dt) for i in range(2)]
            t[nm + "b"] = [Buf(f"{nm}{i}") for i in range(2)]
        return t

    def proj_group(self, wt, wtb, col0, ncols, hT, hb, w, bank, pb, nk=8):
        sc = self.sc
        for kc in range(nk):
            sc.op("pe", lambda e, kc=kc: e.matmul(self.ps[0:ncols, bank, 0:w], lhsT=wt[:, kc, col0:col0 + ncols], rhs=hT[:, kc, 0:w],
                                                   start=(kc == 0), stop=(kc == nk - 1)), reads=[wtb, hb], writes=[pb])

    def v_tokmajor(self, wt, wtb, col0, ncols, hT, hb, t0, w, vt_ring, nheads, hd, ones_col, nk=8):
        sc = self.sc
        vw = hd + (1 if ones_col else 0)
        for ti in range(w // 128):
            vt, vb = vt_ring[self._vi % 2]
            self._vi += 1
            for n0 in range(0, ncols, 512):
                nn = min(512, ncols - n0)
                bank = 5 + n0 // 512
                for kc in range(nk):
                    sc.op("pe", lambda e, kc=kc, n0=n0, nn=nn, bank=bank: e.matmul(
                        self.ps[:, bank, 0:nn], lhsT=hT[:, kc, ti * 128:(ti + 1) * 128], rhs=wt[:, kc, col0 + n0:col0 + n0 + nn],
                        start=(kc == 0), stop=(kc == nk - 1)), reads=[wtb, hb], writes=[self.vpb[bank - 5]])
                h0 = n0 // hd
                nh = nn // hd
                sc.op("act", lambda e, bank=bank, nn=nn, h0=h0, nh=nh: e.activation(
                    out=vt[:, h0:h0 + nh, 0:hd], in_=self.ps[:, bank, 0:nn].rearrange("p (h d) -> p h d", d=hd), func=AF.Copy),
                    reads=[self.vpb[bank - 5]], writes=[vb])
            blk = (t0 + ti * 128) // 128
            sc.dma("sp", self.Vd[blk, :, 0:nheads * vw], vt[:, 0:nheads, 0:vw].rearrange("p h d -> p (h d)") if False else vt[:, 0:nheads, 0:vw],
                   reads=[vb], writes=[self.bufs["Vd"]])

    def v_tokmajor(self, wt, wtb, col0, ncols, hT, hb, t0, w, vt_ring, nheads, hd, ones_col, nk=8):
        sc = self.sc
        vw = hd + (1 if ones_col else 0)
        for ti in range(w // 128):
            vt, vb = vt_ring[self._vi % 2]
            self._vi += 1
            vt3 = vt[:, 0:nheads * vw].rearrange("p (h d) -> p h d", d=vw)
            for n0 in range(0, ncols, 512):
                nn = min(512, ncols - n0)
                bank = 5 + n0 // 512
                for kc in range(nk):
                    sc.op("pe", lambda e, kc=kc, n0=n0, nn=nn, bank=bank, ti=ti: e.matmul(
                        self.ps[:, bank, 0:nn], lhsT=hT[:, kc, ti * 128:(ti + 1) * 128], rhs=wt[:, kc, col0 + n0:col0 + n0 + nn],
                        start=(kc == 0), stop=(kc == nk - 1)), reads=[wtb, hb], writes=[self.vpb[bank - 5]])
                h0 = n0 // hd
                nh = nn // hd
                sc.op("act", lambda e, bank=bank, nn=nn, h0=h0, nh=nh, vt3=vt3: e.activation(
                    out=vt3[:, h0:h0 + nh, 0:hd], in_=self.ps[:, bank, 0:nn].rearrange("p (h d) -> p h d", d=hd), func=AF.Copy),
                    reads=[self.vpb[bank - 5]], writes=[vb])
            blk = (t0 + ti * 128) // 128
            sc.dma("sp", self.Vd[blk, :, 0:nheads * vw], vt[:, 0:nheads * vw], reads=[vb], writes=[self.bufs["Vd"]])

    def vt_ring(self, ph, nheads, hd, ones_col):
        sc = self.sc
        vw = hd + (1 if ones_col else 0)
        ring = []
        for i in range(2):
            vt = self.sb(ph, f"vt{i}", [128, 1040], BF16)
            vb = Buf(f"vt{i}")
            if ones_col:
                sc.op("dve", lambda e, vt=vt: e.memset(vt[:, 0:nheads * vw], 1.0), writes=[vb])
            ring.append((vt, vb))
        self._vi = 0
        self.vpb = [Buf("vps0"), Buf("vps1")]
        return ring

    def chunk_front(self, ph, tmp, xr, hr, ci, t0, w, which=0):
        sc = self.sc
        stream = 0 if t0 < S else 1
        xc, xb = xr[ci % 2]
        hT, hb = hr[ci % 2]
        sc.dma("sp", xc[:, :, 0:w], self.xres_ap(t0, w), reads=[self.bufs["xres"]], writes=[xb])
        self.norm_mod(xc, xb, w, stream, which, hT, hb, tmp, (0, self.nps))
        return hT, hb, xc, xb

    def ring2(self, ph, name, shape, dt):
        return [(self.sb(ph, f"{name}{i}", shape, dt), Buf(f"{name}{i}")) for i in range(2)]

    def p1_gqa(self, l):
        sc = self.sc
        with self.phase() as ph:
            win, winb = self.load_cast(ph, "win", self.din["gqa_win"], [128, 8, 1536])
            tmp = self.norm_tmp(ph)
            t = self.qk_tmp(ph)
            gq = self.sb(ph, "gq", [128, 2], F32)
            rm = self.sb(ph, "rm", [128, 128], F32)
            sc.dma("sp", gq[:], self.din["gqa_g"], writes=[t["gb"]])
            sc.dma("sp", rm[:], self.din["rm64"], writes=[t["gb"]])
            xr = self.ring2(ph, "xc", [128, 8, 512], F32)
            hr = self.ring2(ph, "hT", [128, 8, 512], BF16)
            cr = self.ring2(ph, "cs", [128, 2, 512], F32)
            vr = self.vt_ring(ph, 4, 64, True)
            self.nps = Buf("nps")
            pbs = [Buf("pp1"), Buf("pp2")]
            for ci, (t0, w) in enumerate(CHUNKS):
                hT, hb, xc, xb = self.chunk_front(ph, tmp, xr, hr, ci, t0, w)
                cs, csb = cr[ci % 2]
                sc.dma("sp", cs[:, 0, 0:w], self.din["cos64"][:, t0:t0 + w], writes=[csb])
                sc.dma("sp", cs[:, 1, 0:w], self.din["sin64"][:, t0:t0 + w], writes=[csb])
                for g in range(10):
                    bank = 1 + g % 2
                    self.proj_group(win, winb, g * 128, 128, hT, hb, w, bank, pbs[g % 2])
                    dst = (self.QT[g, :, t0:t0 + w], "QT") if g < 8 else (self.KT[g - 8, :, t0:t0 + w], "KT")
                    self.qk_post(t, bank, pbs[g % 2], w, gq[:, 0:1] if g < 8 else gq[:, 1:2], True, True,
                                 (rm, cs[:, 0, :], cs[:, 1, :], csb), dst[0], dst[1])
                self.v_tokmajor(win, winb, 1280, 256, hT, hb, t0, w, vr, 4, 64, True)
            self.end_phase()

    def att_std(self, nheads, dqk, qsrc, ksrc, vcol, scale):
        sc = self.sc
        with self.phase() as ph:
            qr = self.ring2(ph, "aq", [128, T], BF16)
            kr = self.ring2(ph, "ak", [128, T], BF16)
            vr = self.ring2(ph, "av", [128, 66, 65], BF16)
            pr = [(self.sb(ph, f"pT{i}", [128, 2, 512], BF16), Buf(f"pT{i}")) for i in range(3)]
            our = self.ring2(ph, "ou", [64, 512], F32)
            onr = self.ring2(ph, "on", [64, 512], BF16)
            rdr = self.ring2(ph, "rd", [128, 512], F32)
            for (tl, tb_) in rdr:
                sc.op("dve", lambda e: e.memset(tl[:], 0.0), writes=[tb_])
            if dqk < 128:
                for (tl, tb_) in qr + kr:
                    sc.op("dve", lambda e: e.memset(tl[64:128, :], 0.0), writes=[tb_])
            sb_ = [Buf(f"S{i}") for i in range(2)]
            accb = [Buf("acc0"), Buf("acc1")]
            bcb = Buf("bc")
            jobs = []
            for h in range(nheads):
                for ci, (t0, w) in enumerate(CHUNKS):
                    kbs = list(range(66)) if t0 < S else [64, 65]
                    pairs = [kbs[i:i + 2] for i in range(0, len(kbs), 2)]
                    for pi, pr_ in enumerate(pairs):
                        jobs.append((h, ci, t0, w, pr_, pi == 0, pi == len(pairs) - 1))
            loaded = set()

            def load(h):
                if h in loaded or h >= nheads:
                    return
                loaded.add(h)
                (qt, qb), (kt, kb_), (vt, vb) = qr[h % 2], kr[h % 2], vr[h % 2]
                sc.dma("sp", qt[0:dqk, :], qsrc(h), reads=[self.bufs["QT"]], writes=[qb])
                sc.dma("sp", kt[0:dqk, :], ksrc(h), reads=[self.bufs["KT"]], writes=[kb_])
                sc.dma("sp", vt[:], self.Vd[:, :, vcol(h):vcol(h) + 65].rearrange("b p d -> p b d"), reads=[self.bufs["Vd"]], writes=[vb])

            def emit_S(n):
                h, ci, t0, w, pr_, first, last = jobs[n]
                load(h)
                (qt, qb), (kt, kb_) = qr[h % 2], kr[h % 2]
                sbank = 2 * (n % 2)
                for j, kb in enumerate(pr_):
                    sc.op("pe", lambda e, j=j, kb=kb: e.matmul(
                        self.ps[:, sbank + j, 0:w], lhsT=kt[:, kb * 128:(kb + 1) * 128], rhs=qt[:, t0:t0 + w], start=True, stop=True),
                        reads=[qb, kb_], writes=[sb_[n % 2]])

            chunk_ctr = [0]
            pending = [None]

            def fin2():
                if pending[0] is None:
                    return
                (h, t0, w, ou, oub, on, onb, rd, rdb) = pending[0]
                pending[0] = None
                sc.op("pe", lambda e: e.matmul(self.ps[:, 6, 0:w], lhsT=self.sel64[:], rhs=rd[:, 0:w], start=True, stop=True),
                      reads=[rdb, self.cb], writes=[bcb])
                sc.op("dve", lambda e: e.tensor_tensor(out=on[:, 0:w], in0=ou[:, 0:w], in1=self.ps[0:64, 6, 0:w], op=ALU.mult),
                      reads=[oub, bcb], writes=[onb])
                sc.dma("pool", self.aoT[h // 2, (h % 2) * 64:(h % 2) * 64 + 64, t0:t0 + w], on[:, 0:w], reads=[onb], writes=[self.bufs["aoT"]])

            emit_S(0)
            for n in range(len(jobs)):
                h, ci, t0, w, pr_, first, last = jobs[n]
                if n + 1 < len(jobs):
                    emit_S(n + 1)
                vt, vb = vr[h % 2]
                sbank = 2 * (n % 2)
                npair = len(pr_)
                pT, pb = pr[n % 3]
                sc.op("act", lambda e: e.activation(out=pT[:, 0:npair, 0:w], in_=self.ps[:, sbank:sbank + npair, 0:w], func=AF.Exp, scale=scale),
                      reads=[sb_[n % 2]], writes=[pb])
                ab = chunk_ctr[0] % 2
                for j, kb in enumerate(pr_):
                    sc.op("pe", lambda e, j=j, kb=kb: e.matmul(
                        self.ps[0:65, 4 + ab, 0:w], lhsT=vt[:, kb, 0:65], rhs=pT[:, j, 0:w], start=(first and j == 0), stop=(last and j == npair - 1)),
                        reads=[pb, vb], writes=[accb[ab]])
                if ci == 0 and first:
                    load(h + 1)
                fin2()
                if last:
                    fi = chunk_ctr[0]
                    chunk_ctr[0] += 1
                    ou, oub = our[fi % 2]
                    on, onb = onr[fi % 2]
                    rd, rdb = rdr[fi % 2]
                    sc.op("dve", lambda e: e.tensor_copy(out=ou[:, 0:w], in_=self.ps[0:64, 4 + ab, 0:w]), reads=[accb[ab]], writes=[oub])
                    sc.op("dve", lambda e: e.reciprocal(out=rd[64:65, 0:w], in_=self.ps[64:65, 4 + ab, 0:w]), reads=[accb[ab]], writes=[rdb])
                    pending[0] = (h, t0, w, ou, oub, on, onb, rd, rdb)
            fin2()
            self.end_phase()

    def out_phase(self, l, wout_name, do_ctx=True):
        sc = self.sc
        with self.phase() as ph:
            wo, wob = self.load_cast(ph, "wo", self.din[wout_name], [128, 8, 1024])
            tmp = self.norm_tmp(ph)
            xr = self.ring2(ph, "xc", [128, 8, 512], F32)
            ar = self.ring2(ph, "ao", [128, 8, 512], BF16)
            hr = self.ring2(ph, "h2", [128, 8, 512], BF16)
            self.nps = Buf("nps")
            pbs = [Buf("op1"), Buf("op2")]
            chunks = CHUNKS if do_ctx else CHUNKS[:-1]
            for ci, (t0, w) in enumerate(chunks):
                stream = 0 if t0 < S else 1
                xc, xb = xr[ci % 2]
                ao, ab = ar[ci % 2]
                h2, hb = hr[ci % 2]
                sc.dma("sp", xc[:, :, 0:w], self.xres_ap(t0, w), reads=[self.bufs["xres"]], writes=[xb])
                sc.dma("sp", ao[:, :, 0:w], self.aoT[:, :, t0:t0 + w].rearrange("k p t -> p k t"), reads=[self.bufs["aoT"]], writes=[ab])
                for og in range(8):
                    bank = 1 + og % 2
                    self.proj_group(wo, wob, og * 128, 128, ao, ab, w, bank, pbs[og % 2])
                    sc.op("dve", lambda e, og=og, bank=bank: e.scalar_tensor_tensor(
                        out=xc[:, og, 0:w], in0=self.ps[:, bank, 0:w], scalar=self.modv[:, stream, 16 + og:17 + og], in1=xc[:, og, 0:w],
                        op0=ALU.mult, op1=ALU.add), reads=[pbs[og % 2], xb, self.modb], writes=[xb])
                sc.dma("pool", self.xres_ap(t0, w), xc[:, :, 0:w], reads=[xb], writes=[self.bufs["xres"]])
                self.norm_mod(xc, xb, w, stream, 1, h2, hb, tmp, (0, self.nps))
                sc.dma("pool", self.h2T[:, :, t0:t0 + w].rearrange("k p t -> p k t"), h2[:, :, 0:w], reads=[hb], writes=[self.bufs["h2T"]])
            self.end_phase()

    def ffn_phase(self, l, do_ctx=True):
        sc = self.sc
        W = 256
        with self.phase() as ph:
            wu, wub = self.load_cast(ph, "wu", self.din["ffn_up"][l], [128, 8, 2 * FFN])
            wd, wdb = self.load_cast(ph, "wd", self.din["ffn_dn"][l], [128, 22, D])
            hr = self.ring2(ph, "fh", [128, 8, W + 2], BF16)
            gT = self.sb(ph, "gT", [128, 22, W], BF16)
            gTb = [Buf(f"gT{f}") for f in range(22)]
            ur = [(self.sb(ph, f"fu{i}", [128, 2, W], F32), Buf(f"fu{i}")) for i in range(3)]
            sgr = [(self.sb(ph, f"fsg{i}", [128, W], F32), Buf(f"fsg{i}")) for i in range(3)]
            xor = self.ring2(ph, "fxo", [128, W], F32)
            upb = [Buf(f"up{i}") for i in range(6)]
            dnb = [Buf("dn0"), Buf("dn1")]
            seqs = [(0, S)] + ([(S, T)] if do_ctx else [])
            chunks = [(t0, W, a, b) for (a, b) in seqs for t0 in range(a, b, W)]
            cw0 = LV_CONVW
            ui = 0
            xi = 0
            for ci, (t0, w, s0, s1) in enumerate(chunks):
                hh, hb = hr[ci % 2]
                lo = 1 if t0 > s0 else 0
                hi = 1 if t0 + w < s1 else 0
                if not lo:
                    sc.op("dve", lambda e, hh=hh: e.memset(hh[:, :, 0:1], 0.0), writes=[hb])
                if not hi:
                    sc.op("dve", lambda e, hh=hh: e.memset(hh[:, :, w + 1:w + 2], 0.0), writes=[hb])
                sc.dma("sp", hh[:, :, 1 - lo:w + 1 + hi], self.h2T[:, :, t0 - lo:t0 + w + hi].rearrange("k p t -> p k t"),
                       reads=[self.bufs["h2T"]], writes=[hb])
                for f in range(22):
                    uu, ub = ur[ui % 3]
                    ui += 1
                    for vg, grp in enumerate((f, f + 22)):
                        bank = 2 * ((ui - 1) % 3) + vg
                        pb = upb[bank]
                        self_ps = self.ps
                        for kc in range(8):
                            sc.op("pe", lambda e, kc=kc, grp=grp, bank=bank, hh=hh: e.matmul(
                                self_ps[:, bank, 0:w + 2], lhsT=wu[:, kc, grp * 128:(grp + 1) * 128], rhs=hh[:, kc, 0:w + 2],
                                start=(kc == 0), stop=(kc == 7)), reads=[wub, hb], writes=[pb])
                        c0 = cw0 + grp * 3
                        cbias = LV_CONVB + grp
                        sc.op("act", lambda e, bank=bank, vg=vg, uu=uu, c0=c0, cbias=cbias: e.activation(
                            out=uu[:, vg, 0:w], in_=self_ps[:, bank, 0:w], func=AF.Identity, bias=self.lvec[:, cbias:cbias + 1], scale=self.lvec[:, c0:c0 + 1]),
                            reads=[pb, self.lvb], writes=[ub])
                        for tap in (1, 2):
                            sc.op("dve", lambda e, bank=bank, vg=vg, uu=uu, c0=c0, tap=tap: e.scalar_tensor_tensor(
                                out=uu[:, vg, 0:w], in0=self_ps[:, bank, tap:tap + w], scalar=self.lvec[:, c0 + tap:c0 + tap + 1], in1=uu[:, vg, 0:w],
                                op0=ALU.mult, op1=ALU.add), reads=[pb, ub, self.lvb], writes=[ub])
                    sg, sgb = sgr[(ui - 1) % 3]
                    sc.op("act", lambda e, uu=uu, sg=sg: e.activation(out=sg[:, 0:w], in_=uu[:, 1, 0:w], func=AF.Silu), reads=[ub], writes=[sgb])
                    sc.op("dve", lambda e, uu=uu, sg=sg, f=f: e.tensor_tensor(out=gT[:, f, 0:w], in0=sg[:, 0:w], in1=uu[:, 0, 0:w], op=ALU.mult),
                          reads=[sgb, ub], writes=[gTb[f]])
                stream = 0 if t0 < S else 1
                for og in range(8):
                    bank = 6 + og % 2
                    pb = dnb[og % 2]
                    xo, xob = xor[xi % 2]
                    xi += 1
                    sc.dma("sp", xo[:, 0:w], self.xres[og, :, t0:t0 + w], reads=[self.bufs["xres"]], writes=[xob])
                    for f in range(22):
                        sc.op("pe", lambda e, f=f, og=og, bank=bank: e.matmul(
                            self.ps[:, bank, 0:w], lhsT=wd[:, f, og * 128:(og + 1) * 128], rhs=gT[:, f, 0:w], start=(f == 0), stop=(f == 21)),
                            reads=[wdb, gTb[f]], writes=[pb])
                    sc.op("dve", lambda e, og=og, bank=bank, xo=xo: e.scalar_tensor_tensor(
                        out=xo[:, 0:w], in0=self.ps[:, bank, 0:w], scalar=self.modv[:, stream, 40 + og:41 + og], in1=xo[:, 0:w],
                        op0=ALU.mult, op1=ALU.add), reads=[pb, xob, self.modb], writes=[xob])
                    sc.dma("pool", self.xres[og, :, t0:t0 + w], xo[:, 0:w], reads=[xob], writes=[self.bufs["xres"]])
            self.end_phase()

    def final_phase(self):
        sc = self.sc
        with self.phase() as ph:
            tmp = self.norm_tmp(ph)
            xr = self.ring2(ph, "xc", [128, 8, 512], F32)
            fg = self.sb(ph, "fg", [128, 8], F32)
            fgb = Buf("fg")
            sc.dma("sp", fg[:], self.din["fng"], writes=[fgb])
            self.nps = Buf("nps")
            for ci, (t0, w) in enumerate(CHUNKS[:-1]):
                xc, xb = xr[ci % 2]
                sc.dma("sp", xc[:, :, 0:w], self.xres_ap(t0, w), reads=[self.bufs["xres"]], writes=[xb])
                self.norm_mod(xc, xb, w, 0, 0, None, None, tmp, (0, self.nps))
                h1, h1b, r, rb = tmp["h1"], tmp["h1b"], tmp["r"], tmp["rb"]
                sc.op("dve", lambda e, xc=xc: e.tensor_tensor(out=h1[:, :, 0:w], in0=xc[:, :, 0:w], in1=fg[:, 0:8].unsqueeze(2).to_broadcast([128, 8, w]), op=ALU.mult),
                      reads=[xb, fgb], writes=[h1b])
                sc.op("dve", lambda e: e.tensor_tensor(out=h1[:, :, 0:w], in0=h1[:, :, 0:w], in1=r[:, 0:w].unsqueeze(1).to_broadcast([128, 8, w]), op=ALU.mult),
                      reads=[h1b, rb], writes=[h1b])
                sc.dma("pool", self.outT[:, :, t0:t0 + w].rearrange("k p t -> p k t"), h1[:, :, 0:w], reads=[h1b], writes=[self.bufs["outT"]])
            self.end_phase()

    def build(self):
        nc, sc = self.nc, self.sc
        L = self.layers
        self.inp("xT_in", [8, 128, T])
        self.inp("cfm", [128, 8, 2])
        self.inp("adaw", [DEPTH, 128, 8, 6 * D])
        self.inp("lvec", [DEPTH, 128, NV])
        self.inp("fng", [128, 8])
        self.inp("ffn_up", [DEPTH, 128, 8, 2 * FFN])
        self.inp("ffn_dn", [DEPTH, 128, 22, D])
        self.inp("cos64", [128, T])
        self.inp("sin64", [128, T])
        self.inp("rm64", [128, 128])
        self.inp("gqa_win", [128, 8, 1536])
        self.inp("gqa_wout", [128, 8, 1024])
        self.inp("gqa_g", [128, 2])
        self.extra_inputs()
        self.xres = self.scratch("xres", [8, 128, T], F32)
        self.h2T = self.scratch("h2T", [8, 128, T], BF16)
        self.QT = self.scratch("QT", [16, 128, T], BF16)
        self.KT = self.scratch("KT", [16, 128, T], BF16)
        self.Vd = self.scratch("Vd", [66, 128, 1040], BF16)
        self.aoT = self.scratch("aoT", [8, 128, T], BF16)
        self.outT = self.scratch("outT", [8, 128, S], F32, kind="ExternalOutput")
        if "xres" in self.debug:
            self.dbg_x = self.scratch("dbg_x", [8, 128, T], F32, kind="ExternalOutput")
        self.consts()
        for k in range(8):
            sc.dma("sp" if k % 2 == 0 else "pool", self.xres[k], self.din["xT_in"][k], writes=[self.bufs["xres"]])
        self.end_phase()
        for l in L:
            last = (l == DEPTH - 1)
            self.mod_phase(l)
            if self.stop == "mod":
                break
            if l == 0:
                self.p1_gqa(l)
                if self.stop == "p1":
                    break
                self.att_std(16 if self.stop != "att1" else 1, 64,
                             lambda h: self.QT[h // 2, (h % 2) * 64:(h % 2) * 64 + 64, :],
                             lambda h: self.KT[(h // 4) // 2, ((h // 4) % 2) * 64:((h // 4) % 2) * 64 + 64, :],
                             lambda h: (h // 4) * 65, 64 ** -0.5)
                if self.stop in ("att", "att1"):
                    break
                self.out_phase(l, "gqa_wout")
                if self.stop == "out":
                    break
            elif l == 1:
                self.layer_mla(l)
            elif l == 2:
                self.layer_diff(l)
            else:
                self.layer_na(l)
            self.ffn_phase(l, do_ctx=not last)
        if "xres" in self.debug:
            for k in range(8):
                sc.dma("sp", self.dbg_x[k], self.xres[k], reads=[self.bufs["xres"]], writes=[Buf("dbg")])
            self.end_phase()
        self.final_phase()
        self.sc.barrier()
        self.stack.close()
        return nc

    def extra_inputs(self):
        pass


LV_ADAB = 0
LV_N1G = 48
LV_N2G = 56
LV_CONVB = 64
LV_CONVW = 108
NV = 108 + 132


def _fm(v):
    v = np.asarray(v, np.float32)
    return np.ascontiguousarray(v.reshape(-1, 128).T)


def _wfm(w):
    w = np.asarray(w, np.float32)
    K, N = w.shape
    return np.ascontiguousarray(w.reshape(K // 128, 128, N).transpose(1, 0, 2))


def host_inputs_core(I, b):
    m = {}
    xt = np.concatenate([np.asarray(I["x"][b], np.float32), np.asarray(I["ctx"][b], np.float32)], axis=0)
    m["xT_in"] = np.ascontiguousarray(xt.T.reshape(8, 128, T))
    cf = np.stack([_fm(I["c"][b]), _fm(I["c_ctx"])], axis=-1)
    m["cfm"] = np.ascontiguousarray(cf)
    return m


def host_inputs(inputs, b):
    I = inputs
    m = host_inputs_core(inputs, b)
    m["adaw"] = np.stack([_wfm(I["ada_w"][l]) for l in range(DEPTH)])
    lv = np.zeros((DEPTH, 128, NV), np.float32)
    for l in range(DEPTH):
        lv[l, :, LV_ADAB:LV_ADAB + 48] = _fm(I["ada_b"][l])
        lv[l, :, LV_N1G:LV_N1G + 8] = _fm(I["norm1_g"][l])
        lv[l, :, LV_N2G:LV_N2G + 8] = _fm(I["norm2_g"][l])
        lv[l, :, LV_CONVB:LV_CONVB + 44] = _fm(I["ffn_conv_b"][l])
        cw = np.stack([_fm(I["ffn_conv_w"][l][k]) for k in range(3)], axis=-1)
        lv[l, :, LV_CONVW:LV_CONVW + 132] = cw.reshape(128, 132)
    m["lvec"] = lv
    m["fng"] = _fm(I["final_norm_g"])
    m["ffn_up"] = np.stack([_wfm(I["ffn_w_up"][l]) for l in range(DEPTH)])
    m["ffn_dn"] = np.stack([_wfm(I["ffn_w_down"][l]) for l in range(DEPTH)])
    c64, s64 = _rope_tables(64, 2)
    m["cos64"], m["sin64"] = c64, s64
    m["rm64"] = _rot_matrix(64, 2)
    m["gqa_win"] = _wfm(I["gqa_w_in"][0])
    m["gqa_wout"] = _wfm(I["gqa_w_out"][0])
    m["gqa_g"] = np.ascontiguousarray(np.stack([np.tile(np.asarray(I["gqa_q_norm_g"][0], np.float32), 2),
                                                np.tile(np.asarray(I["gqa_k_norm_g"][0], np.float32), 2)], axis=-1))
    return m


def _extra_inputs(self):
    self.inp("mla_win", [128, 8, 672])
    self.inp("mla_g", [128, 5])
    self.inp("mla_wuq", [128, 3, 1536])
    self.inp("mla_wukvk", [128, 2, 1024])
    self.inp("mla_wukvv", [128, 2, 1024])
    self.inp("mla_wout", [128, 8, 1024])
    self.inp("rmq96", [96, 96])
    self.inp("cosq96", [96, T])
    self.inp("sinq96", [96, T])
    self.inp("rmk32", [32, 32])
    self.inp("cosk32", [32, T])
    self.inp("sink32", [32, T])
    self.inp("diff_win", [128, 8, 3072])
    self.inp("diff_wout", [128, 8, 1024])
    self.inp("diff_lam", [128, 256])
    self.inp("diff_subg", [128, 1])
    self.inp("na_win", [128, 8, 3072])
    self.inp("na_wout", [128, 8, 1024])
    self.inp("na_bias", [5, 16, 128, 640])


def _group_rms(self, t, srcs, srcb, w, ngr, gains, gcol0, dst, dstb, tmpq, tmpqb):
    sc = self.sc
    for g in range(ngr):
        sc.op("dve", lambda e, g=g: e.tensor_tensor(out=tmpq[:, 0:w], in0=srcs[g][:, 0:w], in1=srcs[g][:, 0:w], op=ALU.mult), reads=[srcb[g]], writes=[tmpqb])
        sc.op("pe", lambda e, g=g: e.matmul(self.ps[:, 3, 0:w], lhsT=self.ones_f[:], rhs=tmpq[:, 0:w], start=(g == 0), stop=(g == ngr - 1)),
              reads=[tmpqb, self.cb], writes=[t["ms2b"]])
    r2, r2b = t["r2"][0], t["r2b"][0]
    self.rstd(r2[:, 0:w], self.ps[:, 3, 0:w], 1.0 / (ngr * 128), [t["ms2b"]], r2b)
    for g in range(ngr):
        sc.op("dve", lambda e, g=g: e.scalar_tensor_tensor(out=dst[:, g, 0:w], in0=srcs[g][:, 0:w], scalar=gains[:, gcol0 + g:gcol0 + g + 1], in1=r2[:, 0:w],
                                                           op0=ALU.mult, op1=ALU.mult), reads=[srcb[g], r2b, t["gb"]], writes=[dstb])


def _p1_mla(self, l):
    sc = self.sc
    with self.phase() as ph:
        win, winb = self.load_cast(ph, "win", self.din["mla_win"], [128, 8, 672])
        wuq, wuqb = self.load_cast(ph, "wuq", self.din["mla_wuq"], [128, 3, 1536])
        wkk, wkkb = self.load_cast(ph, "wkk", self.din["mla_wukvk"], [128, 2, 1024])
        wkv, wkvb = self.load_cast(ph, "wkv", self.din["mla_wukvv"], [128, 2, 1024])
        tmp = self.norm_tmp(ph)
        t = self.qk_tmp(ph)
        gm = self.sb(ph, "gm", [128, 5], F32)
        rmq = self.sb(ph, "rmq", [128, 96], F32)
        rmk = self.sb(ph, "rmk", [128, 32], F32)
        sc.dma("sp", gm[:], self.din["mla_g"], writes=[t["gb"]])
        sc.dma("sp", rmq[0:96, :], self.din["rmq96"], writes=[t["gb"]])
        sc.dma("sp", rmk[0:32, :], self.din["rmk32"], writes=[t["gb"]])
        xr = self.ring2(ph, "xc", [128, 8, 512], F32)
        hr = self.ring2(ph, "hT", [128, 8, 512], BF16)
        cr = self.ring2(ph, "cs", [128, 4, 512], F32)
        vr = self.vt_ring(ph, 16, 64, True)
        cl = [self.sb(ph, f"cl{g}", [128, 512], F32) for g in range(5)]
        clb = [Buf(f"cl{g}") for g in range(5)]
        cqn = self.sb(ph, "cqn", [128, 3, 512], BF16)
        cqnb = Buf("cqn")
        ckvn = self.sb(ph, "ckvn", [128, 2, 512], BF16)
        ckvnb = Buf("ckvn")
        tq = self.sb(ph, "tq", [128, 512], F32)
        tqb = Buf("tq")
        self.nps = Buf("nps")
        pbs = [Buf("pp1"), Buf("pp2")]
        gi = 0
        for ci, (t0, w) in enumerate(CHUNKS):
            hT, hb, xc, xb = self.chunk_front(ph, tmp, xr, hr, ci, t0, w)
            cs, csb = cr[ci % 2]
            sc.dma("sp", cs[0:96, 0, 0:w], self.din["cosq96"][:, t0:t0 + w], writes=[csb])
            sc.dma("sp", cs[0:96, 1, 0:w], self.din["sinq96"][:, t0:t0 + w], writes=[csb])
            sc.dma("sp", cs[0:32, 2, 0:w], self.din["cosk32"][:, t0:t0 + w], writes=[csb])
            sc.dma("sp", cs[0:32, 3, 0:w], self.din["sink32"][:, t0:t0 + w], writes=[csb])
            for g in range(5):
                bank = 1 + gi % 2
                pb = pbs[gi % 2]
                gi += 1
                self.proj_group(win, winb, g * 128, 128, hT, hb, w, bank, pb)
                sc.op("act", lambda e, g=g, bank=bank: e.activation(out=cl[g][:, 0:w], in_=self.ps[:, bank, 0:w], func=AF.Copy), reads=[pb], writes=[clb[g]])
            _group_rms(self, t, cl[0:3], clb[0:3], w, 3, gm, 0, cqn, cqnb, tq, tqb)
            _group_rms(self, t, cl[3:5], clb[3:5], w, 2, gm, 3, ckvn, ckvnb, tq, tqb)
            bank = 1 + gi % 2
            pb = pbs[gi % 2]
            gi += 1
            self.proj_group(win, winb, 640, 32, hT, hb, w, bank, pb)
            self.qk_post(t, bank, pb, w, None, False, True, (rmk, cs[:, 2, :], cs[:, 3, :], csb),
                         [(self.KT[h, 64:96, t0:t0 + w], slice(0, 32)) for h in range(16)], "KT", rows=32)
            for h in range(16):
                bank = 1 + gi % 2
                pb = pbs[gi % 2]
                gi += 1
                self.proj_group(wuq, wuqb, h * 96, 96, cqn, cqnb, w, bank, pb, nk=3)
                self.qk_post(t, bank, pb, w, None, False, True, (rmq, cs[:, 0, :], cs[:, 1, :], csb), self.QT[h, 0:96, t0:t0 + w], "QT", rows=96)
            for g in range(8):
                bank = 1 + gi % 2
                pb = pbs[gi % 2]
                gi += 1
                self.proj_group(wkk, wkkb, g * 128, 128, ckvn, ckvnb, w, bank, pb, nk=2)
                self.qk_post(t, bank, pb, w, None, False, False, None,
                             [(self.KT[2 * g, 0:64, t0:t0 + w], slice(0, 64)), (self.KT[2 * g + 1, 0:64, t0:t0 + w], slice(64, 128))], "KT")
            self.v_tokmajor(wkv, wkvb, 0, 1024, ckvn, ckvnb, t0, w, vr, 16, 64, True, nk=2)
        self.end_phase()


def _layer_mla(self, l):
    _p1_mla(self, l)
    if self.stop == "p1":
        return
    self.att_std(16, 96, lambda h: self.QT[h, 0:96, :], lambda h: self.KT[h, 0:96, :], lambda h: h * 65, 96 ** -0.5)
    self.out_phase(l, "mla_wout")


def _p1_qkv(self, l, wname, rope, nheads_v, hd_v, ones_col):
    sc = self.sc
    with self.phase() as ph:
        win, winb = self.load_cast(ph, "win", self.din[wname], [128, 8, 3072])
        tmp = self.norm_tmp(ph)
        t = self.qk_tmp(ph)
        rm = self.sb(ph, "rm", [128, 128], F32)
        sc.dma("sp", rm[:], self.din["rm64"], writes=[t["gb"]])
        xr = self.ring2(ph, "xc", [128, 8, 512], F32)
        hr = self.ring2(ph, "hT", [128, 8, 512], BF16)
        cr = self.ring2(ph, "cs", [128, 2, 512], F32)
        vr = self.vt_ring(ph, nheads_v, hd_v, ones_col)
        self.nps = Buf("nps")
        pbs = [Buf("pp1"), Buf("pp2")]
        for ci, (t0, w) in enumerate(CHUNKS):
            hT, hb, xc, xb = self.chunk_front(ph, tmp, xr, hr, ci, t0, w)
            cs, csb = cr[ci % 2]
            if rope:
                sc.dma("sp", cs[:, 0, 0:w], self.din["cos64"][:, t0:t0 + w], writes=[csb])
                sc.dma("sp", cs[:, 1, 0:w], self.din["sin64"][:, t0:t0 + w], writes=[csb])
            for g in range(16):
                bank = 1 + g % 2
                self.proj_group(win, winb, g * 128, 128, hT, hb, w, bank, pbs[g % 2])
                dst = (self.QT[g, :, t0:t0 + w], "QT") if g < 8 else (self.KT[g - 8, :, t0:t0 + w], "KT")
                self.qk_post(t, bank, pbs[g % 2], w, None, False, rope, (rm, cs[:, 0, :], cs[:, 1, :], csb), dst[0], dst[1])
            self.v_tokmajor(win, winb, 2048, 1024, hT, hb, t0, w, vr, nheads_v, hd_v, ones_col)
        self.end_phase()


def _att_diff(self, l):
    sc = self.sc
    lam_init = 0.8 - 0.6 * math.exp(-0.3 * l)
    with self.phase() as ph:
        lam = self.sb(ph, "lam", [128, 256], F32)
        lsc = self.sb(ph, "lsc", [128, 8], F32)
        lb = Buf("lam")
        sc.dma("sp", lam[:], self.din["diff_lam"], writes=[lb])
        sc.dma("sp", lsc[:, 4:5], self.din["diff_subg"], writes=[lb])
        sc.op("dve", lambda e: e.tensor_tensor(out=lam[:, 0:64], in0=lam[:, 0:64], in1=lam[:, 64:128], op=ALU.mult), reads=[lb], writes=[lb])
        sc.op("dve", lambda e: e.tensor_tensor(out=lam[:, 128:192], in0=lam[:, 128:192], in1=lam[:, 192:256], op=ALU.mult), reads=[lb], writes=[lb])
        sc.op("dve", lambda e: e.reduce_sum(out=lsc[:, 0:1], in_=lam[:, 0:64], axis=mybir.AxisListType.X), reads=[lb], writes=[lb])
        sc.op("dve", lambda e: e.reduce_sum(out=lsc[:, 1:2], in_=lam[:, 128:192], axis=mybir.AxisListType.X), reads=[lb], writes=[lb])
        sc.op("act", lambda e: e.activation(out=lsc[:, 0:2], in_=lsc[:, 0:2], func=AF.Exp), reads=[lb], writes=[lb])
        sc.op("dve", lambda e: e.tensor_tensor(out=lsc[:, 2:3], in0=lsc[:, 1:2], in1=lsc[:, 0:1], op=ALU.subtract), reads=[lb], writes=[lb])
        sc.op("dve", lambda e: e.tensor_scalar(out=lsc[:, 2:3], in0=lsc[:, 2:3], scalar1=-lam_init, scalar2=None, op0=ALU.add), reads=[lb], writes=[lb])
        sc.op("dve", lambda e: e.tensor_scalar(out=lsc[:, 5:6], in0=lsc[:, 4:5], scalar1=1.0 - lam_init, scalar2=None, op0=ALU.mult), reads=[lb], writes=[lb])
        qr = self.ring2(ph, "aq", [128, T], BF16)
        qr2 = self.ring2(ph, "aq2", [128, T], BF16)
        for (tl, tb_) in qr:
            sc.op("dve", lambda e: e.memset(tl[64:128, :], 0.0), writes=[tb_])
        for (tl, tb_) in qr2:
            sc.op("dve", lambda e: e.memset(tl[0:64, :], 0.0), writes=[tb_])
        kr = self.ring2(ph, "ak", [128, T], BF16)
        vr = self.ring2(ph, "av", [128, 66, 128], BF16)
        pr = [(self.sb(ph, f"pT{i}", [128, 2, 512], BF16), Buf(f"pT{i}")) for i in range(3)]
        r0 = self.sb(ph, "r0", [128, 512], F32)
        r1 = self.sb(ph, "r1", [128, 512], F32)
        o = self.sb(ph, "o", [128, 512], F32)
        o2 = self.sb(ph, "o2", [128, 512], F32)
        fb = Buf("fin")
        onr = self.ring2(ph, "on", [128, 512], BF16)
        sb_ = [Buf("S0"), Buf("S1")]
        accb = [Buf(f"acc{i}") for i in range(4)]
        jobs = []
        for h in range(8):
            for ci, (t0, w) in enumerate(CHUNKS):
                kbs = list(range(66)) if t0 < S else [64, 65]
                for ki, kb in enumerate(kbs):
                    jobs.append((h, t0, w, kb, ki == 0, ki == len(kbs) - 1))
        loaded = set()

        def load(h):
            if h in loaded or h >= 8:
                return
            loaded.add(h)
            (qt, qb), (kt, kb_), (vt, vb) = qr[h % 2], kr[h % 2], vr[h % 2]
            sc.dma("sp", qt[0:64, :], self.QT[h, 0:64, :], reads=[self.bufs["QT"]], writes=[qb])
            sc.dma("sp", qr2[h % 2][0][64:128, :], self.QT[h, 64:128, :], reads=[self.bufs["QT"]], writes=[qr2[h % 2][1]])
            sc.dma("sp", kt[:], self.KT[h], reads=[self.bufs["KT"]], writes=[kb_])
            sc.dma("sp", vt[:], self.Vd[:, :, h * 128:(h + 1) * 128].rearrange("b p d -> p b d"), reads=[self.bufs["Vd"]], writes=[vb])

        def emit_S(n):
            h, t0, w, kb, first, last = jobs[n]
            load(h)
            (qt, qb), (kt, kb_) = qr[h % 2], kr[h % 2]
            (qt2, qb2) = qr2[h % 2]
            sbank = 2 * (n % 2)
            for j in range(2):
                qq = qt if j == 0 else qt2
                sc.op("pe", lambda e, j=j: e.matmul(
                    self.ps[:, sbank + j, 0:w], lhsT=kt[:, kb * 128:(kb + 1) * 128], rhs=qq[:, t0:t0 + w],
                    start=True, stop=True), reads=[qb, qb2, kb_], writes=[sb_[n % 2]])

        fi = 0
        emit_S(0)
        for n in range(len(jobs)):
            h, t0, w, kb, first, last = jobs[n]
            if n + 1 < len(jobs):
                emit_S(n + 1)
            vt, vb = vr[h % 2]
            sbank = 2 * (n % 2)
            pT, pb = pr[n % 3]
            sc.op("act", lambda e: e.activation(out=pT[:, :, 0:w], in_=self.ps[:, sbank:sbank + 2, 0:w], func=AF.Exp, scale=0.125), reads=[sb_[n % 2]], writes=[pb])
            for j in range(2):
                sc.op("pe", lambda e, j=j: e.matmul(self.ps[:, 4 + 2 * j, 0:w], lhsT=vt[:, kb, :], rhs=pT[:, j, 0:w], start=first, stop=last),
                      reads=[pb, vb], writes=[accb[2 * j]])
                sc.op("pe", lambda e, j=j: e.matmul(self.ps[:, 5 + 2 * j, 0:w], lhsT=self.ones_b[:], rhs=pT[:, j, 0:w], start=first, stop=last),
                      reads=[pb, self.cb], writes=[accb[2 * j + 1]])
            if t0 == 0 and first:
                load(h + 1)
            if not last:
                continue
            on, onb = onr[fi % 2]
            fi += 1
            sc.op("dve", lambda e: e.reciprocal(out=r0[:, 0:w], in_=self.ps[:, 5, 0:w]), reads=[accb[1]], writes=[fb])
            sc.op("dve", lambda e: e.reciprocal(out=r1[:, 0:w], in_=self.ps[:, 7, 0:w]), reads=[accb[3]], writes=[fb])
            sc.op("dve", lambda e: e.tensor_tensor(out=r0[:, 0:w], in0=r0[:, 0:w], in1=self.ps[:, 4, 0:w], op=ALU.mult), reads=[accb[0], fb], writes=[fb])
            sc.op("dve", lambda e: e.tensor_tensor(out=r1[:, 0:w], in0=r1[:, 0:w], in1=self.ps[:, 6, 0:w], op=ALU.mult), reads=[accb[2], fb], writes=[fb])
            sc.op("dve", lambda e: e.scalar_tensor_tensor(out=o[:, 0:w], in0=r1[:, 0:w], scalar=lsc[:, 2:3], in1=r0[:, 0:w], op0=ALU.mult, op1=ALU.add),
                  reads=[fb, lb], writes=[fb])
            sc.op("dve", lambda e: e.tensor_tensor(out=o2[:, 0:w], in0=o[:, 0:w], in1=o[:, 0:w], op=ALU.mult), reads=[fb], writes=[fb])
            sc.op("pe", lambda e: e.matmul(self.ps[:, 5, 0:w], lhsT=self.ones_f[:], rhs=o2[:, 0:w], start=True, stop=True),
                  reads=[fb, self.cb], writes=[accb[1]])
            self.rstd(o2[:, 0:w], self.ps[:, 5, 0:w], 1.0 / 128, [accb[1], fb], fb)
            sc.op("dve", lambda e: e.scalar_tensor_tensor(out=on[:, 0:w], in0=o[:, 0:w], scalar=lsc[:, 5:6], in1=o2[:, 0:w], op0=ALU.mult, op1=ALU.mult),
                  reads=[fb, lb], writes=[onb])
            sc.dma("pool", self.aoT[h, :, t0:t0 + w], on[:, 0:w], reads=[onb], writes=[self.bufs["aoT"]])
        self.end_phase()


def _layer_diff(self, l):
    _p1_qkv(self, l, "diff_win", True, 8, 128, False)
    if self.stop == "p1":
        return
    _att_diff(self, l)
    self.out_phase(l, "diff_wout")


def _att_na(self, l):
    sc = self.sc
    with self.phase() as ph:
        qr = self.ring2(ph, "aq", [128, S], BF16)
        kr = self.ring2(ph, "ak", [128, T], BF16)
        for (tl, tb_) in qr + kr:
            sc.op("dve", lambda e: e.memset(tl[64:128, :], 0.0), writes=[tb_])
        vr = self.ring2(ph, "av", [128, 66, 65], BF16)
        br = self.ring2(ph, "nb", [128, 5, 640], F32)
        tbr = self.ring2(ph, "tb", [128, 640], F32)
        pr = [(self.sb(ph, f"pT{i}", [128, 896], BF16), Buf(f"pT{i}")) for i in range(3)]
        our = self.ring2(ph, "ou", [64, 128], F32)
        onr = self.ring2(ph, "on", [64, 512], BF16)
        rdr = self.ring2(ph, "rd", [128, 128], F32)
        for (tl, tb_) in rdr:
            sc.op("dve", lambda e: e.memset(tl[:], 0.0), writes=[tb_])
        sb_ = [Buf("S0"), Buf("S1")]
        accb = [Buf("acc0"), Buf("acc1")]
        bcb = Buf("bc")
        jobs = [(h, j) for h in range(16) for j in range(64)]
        loaded = set()

        def blocks(j):
            b0 = min(max(2 * j - 4, 0), 118)
            return [b0 // 2 + m for m in range(5)] + [64, 65]

        def load(h):
            if h in loaded or h >= 16:
                return
            loaded.add(h)
            (qt, qb), (kt, kb_), (vt, vb), (bt, bb) = qr[h % 2], kr[h % 2], vr[h % 2], br[h % 2]
            rows = slice((h % 2) * 64, (h % 2) * 64 + 64)
            sc.dma("sp", qt[0:64, :], self.QT[h // 2, rows, 0:S], reads=[self.bufs["QT"]], writes=[qb])
            sc.dma("sp", kt[0:64, :], self.KT[h // 2, rows, :], reads=[self.bufs["KT"]], writes=[kb_])
            sc.dma("sp", vt[:], self.Vd[:, :, h * 65:h * 65 + 65].rearrange("b p d -> p b d"), reads=[self.bufs["Vd"]], writes=[vb])
            sc.dma("sp", bt[:], self.din["na_bias"][:, h].rearrange("t p c -> p t c"), writes=[bb])

        def emit_S(n):
            h, j = jobs[n]
            load(h)
            (qt, qb), (kt, kb_) = qr[h % 2], kr[h % 2]
            sbank = 2 * (n % 2)
            for m, blk in enumerate(blocks(j)):
                bank, c0 = (sbank, m * 128) if m < 4 else (sbank + 1, (m - 4) * 128)
                sc.op("pe", lambda e: e.matmul(self.ps[:, bank, c0:c0 + 128], lhsT=kt[:, blk * 128:(blk + 1) * 128], rhs=qt[:, j * 128:(j + 1) * 128],
                                              start=True, stop=True), reads=[qb, kb_], writes=[sb_[n % 2]])

        pending = [None]

        def fin2():
            if pending[0] is None:
                return
            (h, j, ou, oub, on, onb, rd, rdb) = pending[0]
            pending[0] = None
            rows = slice((h % 2) * 64, (h % 2) * 64 + 64)
            jj = j % 4
            sc.op("pe", lambda e: e.matmul(self.ps[:, 6, 0:128], lhsT=self.sel64[:], rhs=rd[:, :], start=True, stop=True),
                  reads=[rdb, self.cb], writes=[bcb])
            sc.op("dve", lambda e: e.tensor_tensor(out=on[:, jj * 128:(jj + 1) * 128], in0=ou[:, :], in1=self.ps[0:64, 6, 0:128], op=ALU.mult),
                  reads=[oub, bcb], writes=[onb])
            if jj == 3:
                t0 = (j // 4) * 512
                sc.dma("pool", self.aoT[h // 2, rows, t0:t0 + 512], on[:, :], reads=[onb], writes=[self.bufs["aoT"]])

        emit_S(0)
        for n in range(len(jobs)):
            h, j = jobs[n]
            if n + 1 < len(jobs):
                emit_S(n + 1)
            (vt, vb), (bt, bb) = vr[h % 2], br[h % 2]
            pat = {0: 1, 1: 2, 62: 3, 63: 4}.get(j, 0)
            s2 = n % 2
            sbank = 2 * s2
            tb, tbb = tbr[s2]
            pT, pb = pr[n % 3]
            sc.op("dve", lambda e: e.scalar_tensor_tensor(out=tb[:, 0:512], in0=self.ps[:, sbank, 0:512], scalar=0.125, in1=bt[:, pat, 0:512],
                                                          op0=ALU.mult, op1=ALU.add), reads=[sb_[s2], bb], writes=[tbb])
            sc.op("dve", lambda e: e.scalar_tensor_tensor(out=tb[:, 512:640], in0=self.ps[:, sbank + 1, 0:128], scalar=0.125, in1=bt[:, pat, 512:640],
                                                          op0=ALU.mult, op1=ALU.add), reads=[sb_[s2], bb], writes=[tbb])
            sc.op("act", lambda e: e.activation(out=pT[:, 0:640], in_=tb[:, 0:640], func=AF.Exp), reads=[tbb], writes=[pb])
            sc.op("act", lambda e: e.activation(out=pT[:, 640:896], in_=self.ps[:, sbank + 1, 128:384], func=AF.Exp, scale=0.125),
                  reads=[sb_[s2]], writes=[pb])
            abank = 4 + s2
            for m, blk in enumerate(blocks(j)):
                sc.op("pe", lambda e: e.matmul(self.ps[0:65, abank, 0:128], lhsT=vt[:, blk, 0:65], rhs=pT[:, m * 128:(m + 1) * 128], start=(m == 0), stop=(m == 6)),
                      reads=[pb, vb], writes=[accb[s2]])
            if j == 0:
                load(h + 1)
            fin2()
            ou, oub = our[s2]
            on, onb = onr[(j // 4) % 2]
            rd, rdb = rdr[s2]
            sc.op("dve", lambda e: e.tensor_copy(out=ou[:, :], in_=self.ps[0:64, abank, 0:128]), reads=[accb[s2]], writes=[oub])
            sc.op("dve", lambda e: e.reciprocal(out=rd[64:65, :], in_=self.ps[64:65, abank, 0:128]), reads=[accb[s2]], writes=[rdb])
            pending[0] = (h, j, ou, oub, on, onb, rd, rdb)
        fin2()
        self.end_phase()


def _layer_na(self, l):
    _p1_qkv(self, l, "na_win", False, 16, 64, True)
    if self.stop == "p1":
        return
    _att_na(self, l)
    self.out_phase(l, "na_wout", do_ctx=False)


Builder.extra_inputs = _extra_inputs
Builder.layer_mla = _layer_mla
Builder.layer_diff = _layer_diff
Builder.layer_na = _layer_na


def _na_bias_tables(rpb):
    rpb = np.asarray(rpb, np.float32)
    out = np.empty((5, 16, 128, 640), np.float32)
    q = np.arange(128)
    qdr, qcol = q // 64, q % 64
    i = np.arange(640)
    for p, j in enumerate((2, 0, 1, 62, 63)):
        r = 2 * j + qdr
        rs = np.clip(r - 4, 0, 120)
        b0 = min(max(2 * j - 4, 0), 118)
        krow, kcol = b0 + i // 64, i % 64
        cs = np.clip(qcol - 8, 0, 48)
        inw = ((kcol[:, None] >= cs[None]) & (kcol[:, None] < cs[None] + 16) & (krow[:, None] >= rs[None]) & (krow[:, None] < rs[None] + 8))
        dr = np.clip(krow[:, None] - r[None] + 7, 0, 14)
        dc = np.clip(kcol[:, None] - qcol[None] + 15, 0, 30)
        for h in range(16):
            tbl = np.where(inw, rpb[h][dr, dc], np.float32(NEG)).astype(np.float32)
            out[p, h] = tbl.reshape(5, 128, 128).transpose(1, 0, 2).reshape(128, 640)
    return out


def host_inputs_extra(I, m):
    m["mla_win"] = _wfm(I["mla_w_in"][0])
    m["mla_g"] = np.ascontiguousarray(np.concatenate([_fm(I["mla_q_norm_g"][0]), _fm(I["mla_kv_norm_g"][0])], axis=1))
    m["mla_wuq"] = _wfm(I["mla_w_uq"][0])
    wukv = np.asarray(I["mla_w_ukv"][0], np.float32).reshape(256, 16, 128)
    m["mla_wukvk"] = _wfm(np.ascontiguousarray(wukv[:, :, :64]).reshape(256, 1024))
    m["mla_wukvv"] = _wfm(np.ascontiguousarray(wukv[:, :, 64:]).reshape(256, 1024))
    m["mla_wout"] = _wfm(I["mla_w_out"][0])
    c32, s32 = _rope_tables(32, 1)
    cq = np.ones((96, T), np.float32)
    sq = np.zeros((96, T), np.float32)
    cq[64:], sq[64:] = c32, s32
    m["cosq96"], m["sinq96"] = cq, sq
    m["cosk32"], m["sink32"] = c32, s32
    m["rmq96"] = _rot_matrix(32, 1, offset=64)
    m["rmk32"] = _rot_matrix(32, 1)
    m["diff_win"] = _wfm(I["diff_w_in"][0])
    m["diff_wout"] = _wfm(I["diff_w_out"][0])
    m["diff_lam"] = np.ascontiguousarray(np.broadcast_to(np.asarray(I["diff_lambda"][0], np.float32).reshape(1, 256), (128, 256)))
    m["diff_subg"] = _fm(I["diff_subln_g"][0])
    m["na_win"] = _wfm(I["na_w_in"][0])
    m["na_wout"] = _wfm(I["na_w_out"][0])
    m["na_bias"] = _na_bias_tables(I["na_rpb"][0])
    return m


_NC_CACHE = {}


def kernel(**inputs):
    if "nc" not in _NC_CACHE:
        b = Builder()
        _NC_CACHE["nc"] = b.build()
        _NC_CACHE["names"] = list(b.din.keys())
    nc = _NC_CACHE["nc"]
    shared = host_inputs(inputs, 0)
    host_inputs_extra(inputs, shared)
    in_maps = []
    for b in range(NCORES):
        m = dict(shared)
        if b > 0:
            m.update(host_inputs_core(inputs, b))
        in_maps.append({k: m[k] for k in _NC_CACHE["names"]})
    res = run_bass_kernel_spmd(nc, in_maps, core_ids=list(range(NCORES)))
    out = np.empty((NCORES, S, D), np.float32)
    for b in range(NCORES):
        out[b] = np.asarray(res.results[b]["outT"]).reshape(D, S).T
    return out
```

```python
import math
import numpy as np
import concourse.bass as bass
import concourse.mybir as mybir
from concourse.bass_utils import run_bass_kernel_spmd

F32 = mybir.dt.float32
BF16 = mybir.dt.bfloat16
AF = mybir.ActivationFunctionType
ALU = mybir.AluOpType

D = 1024
S = 8192
C = 256
T = S + C
GRID_W = 64
DEPTH = 4
FFN = 2816
EPS = 1e-6
NEG = -1e30
NCORES = 4

CHUNKS = [(i * 512, 512) for i in range(S // 512)] + [(S, C)]


class Dep:
    __slots__ = ("sem", "val", "key", "eng", "lval")

    def __init__(self, sem, val, key, eng, lval):
        self.sem, self.val, self.key, self.eng, self.lval = sem, val, key, eng, lval


class Buf:
    __slots__ = ("w", "r", "multi", "name")

    def __init__(self, name="", multi=False):
        self.w = {}
        self.r = {}
        self.multi = multi
        self.name = name


class Sched:
    NRING = 8
    EPOCH = 30000
    DEPOCH = 1800

    def __init__(self, nc, stack):
        self.nc = nc
        self.stack = stack
        self.h = {"pe": nc.tensor, "act": nc.scalar, "dve": nc.vector, "pool": nc.gpsimd, "sp": nc.sync}
        self.sems = {}
        self.cnt = {}
        self.known = {}
        for e in self.h:
            self.sems[e] = []
            self.cnt[e] = 0
            self.known[e] = {}
        self.ring = {}
        for q in ("sp", "pool", "act"):
            self.ring[q] = {"sems": [[] for _ in range(self.NRING)], "cnt": [0] * self.NRING, "n": 0}
        self.latest = {}
        self.nsem = 0

    def _sem(self, lst, ep, name):
        while len(lst) <= ep:
            self.nsem += 1
            lst.append(self.stack.enter_context(self.nc.semaphore(f"{name}_{len(lst)}")))
        return lst[ep]

    def _wait(self, eng, d):
        kn = self.known[eng]
        if kn.get(d.key, 0) >= d.val:
            return
        self.h[eng].wait_ge(d.sem, d.lval)
        kn[d.key] = d.val

    def _waits(self, eng, reads, writes):
        deps = {}

        def add(d):
            o = deps.get(d.key)
            if o is None or o.val < d.val:
                deps[d.key] = d

        for b in reads:
            for d in b.w.values():
                add(d)
        for b in writes:
            for d in b.r.values():
                if d.eng == eng and d.key == eng:
                    continue
                add(d)
            if not b.multi:
                for d in b.w.values():
                    if d.eng == eng and d.key == eng:
                        continue
                    add(d)
        for d in deps.values():
            if d.key == "pe" and eng == "pe":
                continue
            self._wait(eng, d)

    def _record(self, me, reads, writes):
        for b in reads:
            b.r[me.key] = me
        for b in writes:
            if b.multi:
                b.w[me.key] = me
            else:
                b.w = {me.key: me}
                b.r = {}
        self.latest[me.key] = me

    def op(self, eng, fn, reads=(), writes=()):
        self._waits(eng, reads, writes)
        ins = fn(self.h[eng])
        c = self.cnt[eng]
        ep, lv = c // self.EPOCH, c % self.EPOCH + 1
        sem = self._sem(self.sems[eng], ep, "s_" + eng)
        self.cnt[eng] = c + 1
        ins.then_inc(sem, 1)
        me = Dep(sem, c + 1, eng, eng, lv)
        self._record(me, reads, writes)
        return ins

    def dma(self, q, out, in_, reads=(), writes=()):
        self._waits(q, reads, writes)
        rg = self.ring[q]
        i = rg["n"] % self.NRING
        rg["n"] += 1
        c = rg["cnt"][i]
        ep, lv = c // self.DEPOCH, (c % self.DEPOCH + 1) * 16
        sem = self._sem(rg["sems"][i], ep, f"d_{q}{i}")
        rg["cnt"][i] = c + 1
        self.h[q].dma_start(out=out, in_=in_).then_inc(sem, 16)
        me = Dep(sem, c + 1, (q, i), q, lv)
        self._record(me, reads, writes)

    def barrier(self):
        for e in self.h:
            for d in self.latest.values():
                if d.key == e:
                    continue
                self._wait(e, d)


def _rope_tables(rot_dim, nrep, scale=1.0):
    t = np.arange(S)
    row = (t // GRID_W).astype(np.float32)
    col = (t % GRID_W).astype(np.float32)
    half = rot_dim // 2
    inv = (10000.0 ** (-np.arange(0, half, 2, dtype=np.float32) / half)).astype(np.float32)
    ar, ac = row[:, None] * inv, col[:, None] * inv
    ang = np.concatenate([ar, ar, ac, ac], axis=-1)
    cos = np.ones((T, rot_dim), np.float32)
    sin = np.zeros((T, rot_dim), np.float32)
    cos[:S] = np.cos(ang)
    sin[:S] = np.sin(ang)
    cosT = np.tile(cos.T, (nrep, 1)).astype(np.float32)
    sinT = np.tile(sin.T, (nrep, 1)).astype(np.float32)
    return np.ascontiguousarray(cosT), np.ascontiguousarray(sinT)


def _rot_matrix(rot_dim, nrep, offset=0, total=None):
    n = nrep * rot_dim + offset if total is None else total
    m = np.zeros((n, n), np.float32)
    q = rot_dim // 4
    for r in range(nrep):
        b = offset + r * rot_dim
        for i in range(q):
            m[b + q + i, b + i] = -1.0
            m[b + i, b + q + i] = 1.0
            m[b + 3 * q + i, b + 2 * q + i] = -1.0
            m[b + 2 * q + i, b + 3 * q + i] = 1.0
    return m


class Builder:
    def __init__(self, layers=(0, 1, 2, 3), debug=(), stop=None):
        from contextlib import ExitStack
        self.stop = stop
        self.layers = layers
        self.debug = debug
        self.stack = ExitStack()
        nc = self.nc = bass.Bass("TRN2", target_bir_lowering=False)
        self.sc = Sched(nc, self.stack)
        self.din = {}
        self.bufs = {}

    def inp(self, name, shape, dt=F32):
        t = self.nc.dram_tensor(name, list(shape), dt, kind="ExternalInput").ap()
        self.din[name] = t
        self.bufs[name] = Buf(name, multi=True)
        return t

    def scratch(self, name, shape, dt, kind="Internal"):
        t = self.nc.dram_tensor(name, list(shape), dt, kind=kind).ap()
        self.bufs[name] = Buf(name, multi=True)
        return t

    def sb(self, ph, name, shape, dt):
        self._uid = getattr(self, "_uid", 0) + 1
        t = ph.enter_context(self.nc.sbuf_tensor(f"sb{self._uid}_{name}", list(shape), dt))
        return t

    def load_cast(self, ph, name, src_ap, shape, q="sp", cast_eng="pool", piece=2048):
        sc = self.sc
        dst = self.sb(ph, name, shape, BF16)
        dbuf = Buf(name, multi=True)
        a, b = shape[1], shape[2]
        if not hasattr(self, "_stg") or self._stg_ph is not ph:
            self._stg = [self.sb(ph, f"stg{i}", [128, piece], F32) for i in range(3)]
            self._stgb = [Buf(f"stg{i}") for i in range(3)]
            self._dq = 0
            self._stg_ph = ph
            self._stg_i = 0
        for ai in range(a):
            for b0 in range(0, b, piece):
                w = min(piece, b - b0)
                i = self._stg_i % 3
                self._stg_i += 1
                st, sb_ = self._stg[i], self._stgb[i]
                self._dq += 1
                sc.dma(("sp", "pool")[self._dq % 2], st[:, 0:w], src_ap[:, ai, b0:b0 + w], reads=[], writes=[sb_])
                ce = ("pool", "act", "dve")[self._stg_i % 3]
                if ce == "act":
                    sc.op("act", lambda e, st=st, w=w, ai=ai, b0=b0: e.activation(out=dst[:, ai, b0:b0 + w], in_=st[:, 0:w], func=AF.Copy),
                          reads=[sb_], writes=[dbuf])
                else:
                    sc.op(ce, lambda e, st=st, w=w, ai=ai, b0=b0: e.tensor_copy(out=dst[:, ai, b0:b0 + w], in_=st[:, 0:w]),
                          reads=[sb_], writes=[dbuf])
        return dst, dbuf

    def phase(self):
        from contextlib import ExitStack
        return ExitStack()

    def end_phase(self):
        self.sc.barrier()
        for b in self.bufs.values():
            b.w = {}
            b.r = {}
        self._stg_ph = None

    def consts(self):
        nc, sc, st = self.nc, self.sc, self.stack
        self.ps = st.enter_context(nc.psum_tensor("ps", [128, 8, 512], F32))
        self.ones_f = st.enter_context(nc.sbuf_tensor("c_ones_f", [128, 128], F32))
        self.ones_b = st.enter_context(nc.sbuf_tensor("c_ones_b", [128, 128], BF16))
        self.bd64 = st.enter_context(nc.sbuf_tensor("c_bd64", [128, 128], F32))
        self.mhalf = st.enter_context(nc.sbuf_tensor("c_mhalf", [128, 512], F32))
        self.modv = st.enter_context(nc.sbuf_tensor("c_modv", [128, 2, 48], F32))
        self.lvec = st.enter_context(nc.sbuf_tensor("c_lvec", [128, NV], F32))
        self.scv = st.enter_context(nc.sbuf_tensor("c_scv", [128, 8, 2], F32))
        self.epst = st.enter_context(nc.sbuf_tensor("c_eps", [128, 1], F32))
        self.sel64 = st.enter_context(nc.sbuf_tensor("c_sel64", [128, 128], F32))
        self.cb = Buf("consts")
        self.modb = Buf("modv")
        self.lvb = Buf("lvec")
        self.scb = Buf("scv")
        sc.op("dve", lambda e: e.memset(self.ones_f[:], 1.0), writes=[self.cb])
        sc.op("dve", lambda e: e.memset(self.ones_b[:], 1.0), writes=[self.cb])
        sc.op("dve", lambda e: e.memset(self.bd64[:], 0.0), writes=[self.cb])
        sc.op("dve", lambda e: e.memset(self.bd64[0:64, 0:64], 1.0), writes=[self.cb])
        sc.op("dve", lambda e: e.memset(self.bd64[64:128, 64:128], 1.0), writes=[self.cb])
        sc.op("dve", lambda e: e.memset(self.mhalf[:], -0.5), writes=[self.cb])
        sc.op("dve", lambda e: e.memset(self.epst[:], EPS), writes=[self.cb])
        sc.op("dve", lambda e: e.memset(self.sel64[:], 0.0), writes=[self.cb])
        sc.op("dve", lambda e: e.memset(self.sel64[64:65, :], 1.0), writes=[self.cb])
        sc.dma("sp", self.scv[:], self.din["cfm"], writes=[self.scb])
        sc.op("act", lambda e: e.activation(out=self.scv[:], in_=self.scv[:], func=AF.Silu), reads=[self.scb], writes=[self.scb])

    def mod_phase(self, l):
        nc, sc = self.nc, self.sc
        with self.phase() as ph:
            sc.dma("sp", self.lvec[:], self.din["lvec"][l], writes=[self.lvb])
            wst = [self.sb(ph, f"adaw{i}", [128, 8, 512], F32) for i in range(2)]
            wb = [Buf(f"adaw{i}") for i in range(2)]
            mps = self.ps[:, 0, 0:96].rearrange("p (g s) -> p g s", s=2)
            mpb = Buf("modps")
            for pc in range(12):
                i = pc % 2
                sc.dma("sp" if pc % 2 == 0 else "pool", wst[i][:], self.din["adaw"][l, :, :, pc * 512:(pc + 1) * 512], writes=[wb[i]])
                for g4 in range(4):
                    g = pc * 4 + g4
                    for kc in range(8):
                        sc.op("pe", lambda e, i=i, g4=g4, kc=kc, g=g: e.matmul(
                            mps[:, g, :], lhsT=wst[i][:, kc, g4 * 128:(g4 + 1) * 128], rhs=self.scv[:, kc, :],
                            start=(kc == 0), stop=(kc == 7)), reads=[wb[i], self.scb], writes=[mpb])
            mod = self.sb(ph, "modraw", [128, 2, 48], F32)
            mb = Buf("modraw")
            for s_ in range(2):
                sc.op("dve", lambda e, s_=s_: e.tensor_tensor(out=mod[:, s_, :], in0=mps[:, :, s_], in1=self.lvec[:, LV_ADAB:LV_ADAB + 48], op=ALU.add),
                      reads=[mpb, self.lvb], writes=[mb])
            for s_ in range(2):
                for which, (shc, scc, gc, ng) in enumerate(((0, 8, 16, LV_N1G), (24, 32, 40, LV_N2G))):
                    o = which * 24
                    sc.op("dve", lambda e, s_=s_, scc=scc, ng=ng, o=o: e.scalar_tensor_tensor(
                        out=self.modv[:, s_, o:o + 8], in0=mod[:, s_, scc:scc + 8], scalar=1.0, in1=self.lvec[:, ng:ng + 8],
                        op0=ALU.add, op1=ALU.mult), reads=[mb, self.lvb], writes=[self.modb])
                    sc.op("dve", lambda e, s_=s_, shc=shc, o=o: e.tensor_copy(out=self.modv[:, s_, o + 8:o + 16], in_=mod[:, s_, shc:shc + 8]),
                          reads=[mb], writes=[self.modb])
                    sc.op("dve", lambda e, s_=s_, gc=gc, o=o: e.tensor_copy(out=self.modv[:, s_, o + 16:o + 24], in_=mod[:, s_, gc:gc + 8]),
                          reads=[mb], writes=[self.modb])
            self.end_phase()

    def rstd(self, out_ap, in_ap, inv_n, reads, outb, rows=128):
        sc = self.sc
        sc.op("act", lambda e: e.activation(out=out_ap, in_=in_ap, func=AF.Sqrt, bias=self.epst[0:rows, 0:1], scale=inv_n), reads=list(reads) + [self.cb], writes=[outb])
        sc.op("dve", lambda e: e.reciprocal(out=out_ap, in_=out_ap), reads=[outb], writes=[outb])

    def norm_mod(self, xc, xb, w, stream, which, hT, hb, tmp, psb):
        sc = self.sc
        o = which * 24
        sq, sqb, r, rb, h1, h1b = tmp["sq"], tmp["sqb"], tmp["r"], tmp["rb"], tmp["h1"], tmp["h1b"]
        bank, pb = psb
        sc.op("act", lambda e: e.activation(out=sq[:, :, 0:w], in_=xc[:, :, 0:w], func=AF.Square), reads=[xb], writes=[sqb])
        for kc in range(8):
            sc.op("pe", lambda e, kc=kc: e.matmul(self.ps[:, bank, 0:w], lhsT=self.ones_f[:], rhs=sq[:, kc, 0:w], start=(kc == 0), stop=(kc == 7)),
                  reads=[sqb, self.cb], writes=[pb])
        self.rstd(r[:, 0:w], self.ps[:, bank, 0:w], 1.0 / D, [pb], rb)
        if hT is None:
            return
        A = self.modv[:, stream, o:o + 8].unsqueeze(2).to_broadcast([128, 8, w])
        rbc = r[:, 0:w].unsqueeze(1).to_broadcast([128, 8, w])
        sc.op("dve", lambda e: e.tensor_tensor(out=h1[:, :, 0:w], in0=xc[:, :, 0:w], in1=A, op=ALU.mult), reads=[xb, self.modb], writes=[h1b])
        sc.op("dve", lambda e: e.tensor_tensor(out=h1[:, :, 0:w], in0=h1[:, :, 0:w], in1=rbc, op=ALU.mult), reads=[h1b, rb], writes=[h1b])
        for kc in range(8):
            sc.op("act", lambda e, kc=kc: e.activation(out=hT[:, kc, 0:w], in_=h1[:, kc, 0:w], func=AF.Identity,
                                                       bias=self.modv[:, stream, o + 8 + kc:o + 9 + kc], scale=1.0), reads=[h1b, self.modb], writes=[hb])

    def norm_tmp(self, ph):
        return {"sq": self.sb(ph, "nsq", [128, 8, 512], F32), "sqb": Buf("nsq"),
                "r": self.sb(ph, "nr", [128, 512], F32), "rb": Buf("nr"),
                "h1": self.sb(ph, "nh1", [128, 8, 512], F32), "h1b": Buf("nh1")}

    def xres_ap(self, t0, w):
        return self.xres[:, :, t0:t0 + w].rearrange("k p t -> p k t")

    def qk_post(self, ph_t, src_bank, srcb, w, gain_ap, normalize, rope, cs, dst_ap, dstb_name, rows=128):
        sc = self.sc
        t = ph_t
        i = t["i"] % 2
        t["i"] += 1
        qs, qsb = t["qs"][i], t["qsb"][i]
        q2, q2b = t["q2"][i], t["q2b"][i]
        r2, r2b = t["r2"][i], t["r2b"][i]
        qn, qnb = t["qn"][i], t["qnb"][i]
        qf, qfb = t["qf"][i], t["qfb"][i]
        R = slice(0, rows)
        if not normalize and not rope:
            sc.op("act", lambda e: e.activation(out=qf[R, 0:w], in_=self.ps[R, src_bank, 0:w], func=AF.Copy), reads=[srcb], writes=[qfb])
            dsts = dst_ap if isinstance(dst_ap, list) else [(dst_ap, slice(0, rows))]
            for (d_ap, rs) in dsts:
                sc.dma("pool", d_ap, qf[rs, 0:w], reads=[qfb], writes=[self.bufs[dstb_name]])
            return
        sc.op("act", lambda e: e.activation(out=qs[R, 0:w], in_=self.ps[R, src_bank, 0:w], func=AF.Copy), reads=[srcb], writes=[qsb])
        cur, curb = qs, qsb
        if normalize:
            sc.op("dve", lambda e: e.tensor_tensor(out=q2[R, 0:w], in0=qs[R, 0:w], in1=qs[R, 0:w], op=ALU.mult), reads=[qsb], writes=[q2b])
            sc.op("pe", lambda e: e.matmul(self.ps[R, 3, 0:w], lhsT=self.bd64[R, R], rhs=q2[R, 0:w], start=True, stop=True), reads=[q2b, self.cb], writes=[t["ms2b"]])
            self.rstd(r2[R, 0:w], self.ps[R, 3, 0:w], 1.0 / 64, [t["ms2b"]], r2b, rows=rows)
            sc.op("dve", lambda e: e.scalar_tensor_tensor(out=qn[R, 0:w], in0=qs[R, 0:w], scalar=gain_ap, in1=r2[R, 0:w], op0=ALU.mult, op1=ALU.mult),
                  reads=[qsb, r2b, t["gb"]], writes=[qnb])
            cur, curb = qn, qnb
        if rope:
            rm, cos, sin, csb = cs
            sc.op("pe", lambda e: e.matmul(self.ps[R, 4, 0:w], lhsT=rm[R, R], rhs=cur[R, 0:w], start=True, stop=True), reads=[curb, t["gb"]], writes=[t["rotb"]])
            sc.op("dve", lambda e: e.tensor_tensor(out=q2[R, 0:w], in0=cur[R, 0:w], in1=cos[R, 0:w], op=ALU.mult), reads=[curb, csb], writes=[q2b])
            sc.op("dve", lambda e: e.tensor_tensor(out=r2[R, 0:w], in0=self.ps[R, 4, 0:w], in1=sin[R, 0:w], op=ALU.mult), reads=[t["rotb"], csb], writes=[r2b])
            sc.op("dve", lambda e: e.tensor_tensor(out=qf[R, 0:w], in0=q2[R, 0:w], in1=r2[R, 0:w], op=ALU.add), reads=[q2b, r2b], writes=[qfb])
        else:
            sc.op("act", lambda e: e.activation(out=qf[R, 0:w], in_=cur[R, 0:w], func=AF.Copy), reads=[curb], writes=[qfb])
        dsts = dst_ap if isinstance(dst_ap, list) else [(dst_ap, slice(0, rows))]
        for (d_ap, rs) in dsts:
            sc.dma("sp", d_ap, qf[rs, 0:w], reads=[qfb], writes=[self.bufs[dstb_name]])

    def qk_tmp(self, ph):
        t = {"i": 0, "ms2b": Buf("ms2"), "rotb": Buf("rot"), "gb": Buf("gains")}
        for nm, dt in (("qs", F32), ("q2", F32), ("r2", F32), ("qn", F32), ("qf", BF16)):
            t[nm] = [self.sb(ph, f"{nm}{i}", [128, 512], dt) for i in range(2)]
            t[nm + "b"] = [Buf(f"{nm}{i}") for i in range(2)]
        return t

    def proj_group(self, wt, wtb, col0, ncols, hT, hb, w, bank, pb, nk=8):
        sc = self.sc
        for kc in range(nk):
            sc.op("pe", lambda e, kc=kc: e.matmul(self.ps[0:ncols, bank, 0:w], lhsT=wt[:, kc, col0:col0 + ncols], rhs=hT[:, kc, 0:w],
                                                   start=(kc == 0), stop=(kc == nk - 1)), reads=[wtb, hb], writes=[pb])

    def v_tokmajor(self, wt, wtb, col0, ncols, hT, hb, t0, w, vt_ring, nheads, hd, ones_col, nk=8):
        sc = self.sc
        vw = hd + (1 if ones_col else 0)
        for ti in range(w // 128):
            vt, vb = vt_ring[self._vi % 2]
            self._vi += 1
            for n0 in range(0, ncols, 512):
                nn = min(512, ncols - n0)
                bank = 5 + n0 // 512
                for kc in range(nk):
                    sc.op("pe", lambda e, kc=kc, n0=n0, nn=nn, bank=bank: e.matmul(
                        self.ps[:, bank, 0:nn], lhsT=hT[:, kc, ti * 128:(ti + 1) * 128], rhs=wt[:, kc, col0 + n0:col0 + n0 + nn],
                        start=(kc == 0), stop=(kc == nk - 1)), reads=[wtb, hb], writes=[self.vpb[bank - 5]])
                h0 = n0 // hd
                nh = nn // hd
                sc.op("act", lambda e, bank=bank, nn=nn, h0=h0, nh=nh: e.activation(
                    out=vt[:, h0:h0 + nh, 0:hd], in_=self.ps[:, bank, 0:nn].rearrange("p (h d) -> p h d", d=hd), func=AF.Copy),
                    reads=[self.vpb[bank - 5]], writes=[vb])
            blk = (t0 + ti * 128) // 128
            sc.dma("sp", self.Vd[blk, :, 0:nheads * vw], vt[:, 0:nheads, 0:vw].rearrange("p h d -> p (h d)") if False else vt[:, 0:nheads, 0:vw],
                   reads=[vb], writes=[self.bufs["Vd"]])

    def v_tokmajor(self, wt, wtb, col0, ncols, hT, hb, t0, w, vt_ring, nheads, hd, ones_col, nk=8):
        sc = self.sc
        vw = hd + (1 if ones_col else 0)
        for ti in range(w // 128):
            vt, vb = vt_ring[self._vi % 2]
            self._vi += 1
            vt3 = vt[:, 0:nheads * vw].rearrange("p (h d) -> p h d", d=vw)
            for n0 in range(0, ncols, 512):
                nn = min(512, ncols - n0)
                bank = 5 + n0 // 512
                for kc in range(nk):
                    sc.op("pe", lambda e, kc=kc, n0=n0, nn=nn, bank=bank, ti=ti: e.matmul(
                        self.ps[:, bank, 0:nn], lhsT=hT[:, kc, ti * 128:(ti + 1) * 128], rhs=wt[:, kc, col0 + n0:col0 + n0 + nn],
                        start=(kc == 0), stop=(kc == nk - 1)), reads=[wtb, hb], writes=[self.vpb[bank - 5]])
                h0 = n0 // hd
                nh = nn // hd
                sc.op("act", lambda e, bank=bank, nn=nn, h0=h0, nh=nh, vt3=vt3: e.activation(
                    out=vt3[:, h0:h0 + nh, 0:hd], in_=self.ps[:, bank, 0:nn].rearrange("p (h d) -> p h d", d=hd), func=AF.Copy),
                    reads=[self.vpb[bank - 5]], writes=[vb])
            blk = (t0 + ti * 128) // 128
            sc.dma("sp", self.Vd[blk, :, 0:nheads * vw], vt[:, 0:nheads * vw], reads=[vb], writes=[self.bufs["Vd"]])

    def vt_ring(self, ph, nheads, hd, ones_col):
        sc = self.sc
        vw = hd + (1 if ones_col else 0)
        ring = []
        for i in range(2):
            vt = self.sb(ph, f"vt{i}", [128, 1040], BF16)
            vb = Buf(f"vt{i}")
            if ones_col:
                sc.op("dve", lambda e, vt=vt: e.memset(vt[:, 0:nheads * vw], 1.0), writes=[vb])
            ring.append((vt, vb))
        self._vi = 0
        self.vpb = [Buf("vps0"), Buf("vps1")]
        return ring

    def chunk_front(self, ph, tmp, xr, hr, ci, t0, w, which=0):
        sc = self.sc
        stream = 0 if t0 < S else 1
        xc, xb = xr[ci % 2]
        hT, hb = hr[ci % 2]
        sc.dma("sp", xc[:, :, 0:w], self.xres_ap(t0, w), reads=[self.bufs["xres"]], writes=[xb])
        self.norm_mod(xc, xb, w, stream, which, hT, hb, tmp, (0, self.nps))
        return hT, hb, xc, xb

    def ring2(self, ph, name, shape, dt):
        return [(self.sb(ph, f"{name}{i}", shape, dt), Buf(f"{name}{i}")) for i in range(2)]

    def p1_gqa(self, l):
        sc = self.sc
        with self.phase() as ph:
            win, winb = self.load_cast(ph, "win", self.din["gqa_win"], [128, 8, 1536])
            tmp = self.norm_tmp(ph)
            t = self.qk_tmp(ph)
            gq = self.sb(ph, "gq", [128, 2], F32)
            rm = self.sb(ph, "rm", [128, 128], F32)
            sc.dma("sp", gq[:], self.din["gqa_g"], writes=[t["gb"]])
            sc.dma("sp", rm[:], self.din["rm64"], writes=[t["gb"]])
            xr = self.ring2(ph, "xc", [128, 8, 512], F32)
            hr = self.ring2(ph, "hT", [128, 8, 512], BF16)
            cr = self.ring2(ph, "cs", [128, 2, 512], F32)
            vr = self.vt_ring(ph, 4, 64, True)
            self.nps = Buf("nps")
            pbs = [Buf("pp1"), Buf("pp2")]
            for ci, (t0, w) in enumerate(CHUNKS):
                hT, hb, xc, xb = self.chunk_front(ph, tmp, xr, hr, ci, t0, w)
                cs, csb = cr[ci % 2]
                sc.dma("sp", cs[:, 0, 0:w], self.din["cos64"][:, t0:t0 + w], writes=[csb])
                sc.dma("sp", cs[:, 1, 0:w], self.din["sin64"][:, t0:t0 + w], writes=[csb])
                for g in range(10):
                    bank = 1 + g % 2
                    self.proj_group(win, winb, g * 128, 128, hT, hb, w, bank, pbs[g % 2])
                    dst = (self.QT[g, :, t0:t0 + w], "QT") if g < 8 else (self.KT[g - 8, :, t0:t0 + w], "KT")
                    self.qk_post(t, bank, pbs[g % 2], w, gq[:, 0:1] if g < 8 else gq[:, 1:2], True, True,
                                 (rm, cs[:, 0, :], cs[:, 1, :], csb), dst[0], dst[1])
                self.v_tokmajor(win, winb, 1280, 256, hT, hb, t0, w, vr, 4, 64, True)
            self.end_phase()

    def att_std(self, nheads, dqk, qsrc, ksrc, vcol, scale):
        sc = self.sc
        with self.phase() as ph:
            qr = self.ring2(ph, "aq", [128, T], BF16)
            kr = self.ring2(ph, "ak", [128, T], BF16)
            vr = self.ring2(ph, "av", [128, 66, 65], BF16)
            pr = [(self.sb(ph, f"pT{i}", [128, 2, 512], BF16), Buf(f"pT{i}")) for i in range(3)]
            our = self.ring2(ph, "ou", [64, 512], F32)
            onr = self.ring2(ph, "on", [64, 512], BF16)
            rdr = self.ring2(ph, "rd", [128, 512], F32)
            for (tl, tb_) in rdr:
                sc.op("dve", lambda e: e.memset(tl[:], 0.0), writes=[tb_])
            if dqk < 128:
                for (tl, tb_) in qr + kr:
                    sc.op("dve", lambda e: e.memset(tl[64:128, :], 0.0), writes=[tb_])
            sb_ = [Buf(f"S{i}") for i in range(2)]
            accb = [Buf("acc0"), Buf("acc1")]
            bcb = Buf("bc")
            jobs = []
            for h in range(nheads):
                for ci, (t0, w) in enumerate(CHUNKS):
                    kbs = list(range(66)) if t0 < S else [64, 65]
                    pairs = [kbs[i:i + 2] for i in range(0, len(kbs), 2)]
                    for pi, pr_ in enumerate(pairs):
                        jobs.append((h, ci, t0, w, pr_, pi == 0, pi == len(pairs) - 1))
            loaded = set()

            def load(h):
                if h in loaded or h >= nheads:
                    return
                loaded.add(h)
                (qt, qb), (kt, kb_), (vt, vb) = qr[h % 2], kr[h % 2], vr[h % 2]
                sc.dma("sp", qt[0:dqk, :], qsrc(h), reads=[self.bufs["QT"]], writes=[qb])
                sc.dma("sp", kt[0:dqk, :], ksrc(h), reads=[self.bufs["KT"]], writes=[kb_])
                sc.dma("sp", vt[:], self.Vd[:, :, vcol(h):vcol(h) + 65].rearrange("b p d -> p b d"), reads=[self.bufs["Vd"]], writes=[vb])

            def emit_S(n):
                h, ci, t0, w, pr_, first, last = jobs[n]
                load(h)
                (qt, qb), (kt, kb_) = qr[h % 2], kr[h % 2]
                sbank = 2 * (n % 2)
                for j, kb in enumerate(pr_):
                    sc.op("pe", lambda e, j=j, kb=kb: e.matmul(
                        self.ps[:, sbank + j, 0:w], lhsT=kt[:, kb * 128:(kb + 1) * 128], rhs=qt[:, t0:t0 + w], start=True, stop=True),
                        reads=[qb, kb_], writes=[sb_[n % 2]])

            chunk_ctr = [0]
            pending = [None]

            def fin2():
                if pending[0] is None:
                    return
                (h, t0, w, ou, oub, on, onb, rd, rdb) = pending[0]
                pending[0] = None
                sc.op("pe", lambda e: e.matmul(self.ps[:, 6, 0:w], lhsT=self.sel64[:], rhs=rd[:, 0:w], start=True, stop=True),
                      reads=[rdb, self.cb], writes=[bcb])
                sc.op("dve", lambda e: e.tensor_tensor(out=on[:, 0:w], in0=ou[:, 0:w], in1=self.ps[0:64, 6, 0:w], op=ALU.mult),
                      reads=[oub, bcb], writes=[onb])
                sc.dma("pool", self.aoT[h // 2, (h % 2) * 64:(h % 2) * 64 + 64, t0:t0 + w], on[:, 0:w], reads=[onb], writes=[self.bufs["aoT"]])

            emit_S(0)
            for n in range(len(jobs)):
                h, ci, t0, w, pr_, first, last = jobs[n]
                if n + 1 < len(jobs):
                    emit_S(n + 1)
                vt, vb = vr[h % 2]
                sbank = 2 * (n % 2)
                npair = len(pr_)
                pT, pb = pr[n % 3]
                sc.op("act", lambda e: e.activation(out=pT[:, 0:npair, 0:w], in_=self.ps[:, sbank:sbank + npair, 0:w], func=AF.Exp, scale=scale),
                      reads=[sb_[n % 2]], writes=[pb])
                ab = chunk_ctr[0] % 2
                for j, kb in enumerate(pr_):
                    sc.op("pe", lambda e, j=j, kb=kb: e.matmul(
                        self.ps[0:65, 4 + ab, 0:w], lhsT=vt[:, kb, 0:65], rhs=pT[:, j, 0:w], start=(first and j == 0), stop=(last and j == npair - 1)),
                        reads=[pb, vb], writes=[accb[ab]])
                if ci == 0 and first:
                    load(h + 1)
                fin2()
                if last:
                    fi = chunk_ctr[0]
                    chunk_ctr[0] += 1
                    ou, oub = our[fi % 2]
                    on, onb = onr[fi % 2]
                    rd, rdb = rdr[fi % 2]
                    sc.op("dve", lambda e: e.tensor_copy(out=ou[:, 0:w], in_=self.ps[0:64, 4 + ab, 0:w]), reads=[accb[ab]], writes=[oub])
                    sc.op("dve", lambda e: e.reciprocal(out=rd[64:65, 0:w], in_=self.ps[64:65, 4 + ab, 0:w]), reads=[accb[ab]], writes=[rdb])
                    pending[0] = (h, t0, w, ou, oub, on, onb, rd, rdb)
            fin2()
            self.end_phase()

    def out_phase(self, l, wout_name, do_ctx=True):
        sc = self.sc
        with self.phase() as ph:
            wo, wob = self.load_cast(ph, "wo", self.din[wout_name], [128, 8, 1024])
            tmp = self.norm_tmp(ph)
            xr = self.ring2(ph, "xc", [128, 8, 512], F32)
            ar = self.ring2(ph, "ao", [128, 8, 512], BF16)
            hr = self.ring2(ph, "h2", [128, 8, 512], BF16)
            self.nps = Buf("nps")
            pbs = [Buf("op1"), Buf("op2")]
            chunks = CHUNKS if do_ctx else CHUNKS[:-1]
            for ci, (t0, w) in enumerate(chunks):
                stream = 0 if t0 < S else 1
                xc, xb = xr[ci % 2]
                ao, ab = ar[ci % 2]
                h2, hb = hr[ci % 2]
                sc.dma("sp", xc[:, :, 0:w], self.xres_ap(t0, w), reads=[self.bufs["xres"]], writes=[xb])
                sc.dma("sp", ao[:, :, 0:w], self.aoT[:, :, t0:t0 + w].rearrange("k p t -> p k t"), reads=[self.bufs["aoT"]], writes=[ab])
                for og in range(8):
                    bank = 1 + og % 2
                    self.proj_group(wo, wob, og * 128, 128, ao, ab, w, bank, pbs[og % 2])
                    sc.op("dve", lambda e, og=og, bank=bank: e.scalar_tensor_tensor(
                        out=xc[:, og, 0:w], in0=self.ps[:, bank, 0:w], scalar=self.modv[:, stream, 16 + og:17 + og], in1=xc[:, og, 0:w],
                        op0=ALU.mult, op1=ALU.add), reads=[pbs[og % 2], xb, self.modb], writes=[xb])
                sc.dma("pool", self.xres_ap(t0, w), xc[:, :, 0:w], reads=[xb], writes=[self.bufs["xres"]])
                self.norm_mod(xc, xb, w, stream, 1, h2, hb, tmp, (0, self.nps))
                sc.dma("pool", self.h2T[:, :, t0:t0 + w].rearrange("k p t -> p k t"), h2[:, :, 0:w], reads=[hb], writes=[self.bufs["h2T"]])
            self.end_phase()

    def ffn_phase(self, l, do_ctx=True):
        sc = self.sc
        W = 510
        with self.phase() as ph:
            wu, wub = self.load_cast(ph, "wu", self.din["ffn_up"][l], [128, 8, 2 * FFN], piece=512)
            wd, wdb = self.load_cast(ph, "wd", self.din["ffn_dn"][l], [128, 22, D], piece=512)
            hr = self.ring2(ph, "fh", [128, 8, W + 2], BF16)
            gT = self.sb(ph, "gT", [128, 22, W], BF16)
            gTb = [Buf(f"gT{f}") for f in range(22)]
            ur = [(self.sb(ph, f"fu{i}", [128, 2, W], F32), Buf(f"fu{i}")) for i in range(3)]
            sgr = [(self.sb(ph, f"fsg{i}", [128, W], F32), Buf(f"fsg{i}")) for i in range(3)]
            xor = self.ring2(ph, "fxo", [128, W], F32)
            upb = [Buf(f"up{i}") for i in range(6)]
            dnb = [Buf("dn0"), Buf("dn1")]
            seqs = [(0, S)] + ([(S, T)] if do_ctx else [])
            chunks = [(t0, min(W, b - t0), a, b) for (a, b) in seqs for t0 in range(a, b, W)]
            cw0 = LV_CONVW
            ui = 0
            xi = 0
            for ci, (t0, w, s0, s1) in enumerate(chunks):
                hh, hb = hr[ci % 2]
                lo = 1 if t0 > s0 else 0
                hi = 1 if t0 + w < s1 else 0
                if not lo:
                    sc.op("dve", lambda e, hh=hh: e.memset(hh[:, :, 0:1], 0.0), writes=[hb])
                if not hi:
                    sc.op("dve", lambda e, hh=hh: e.memset(hh[:, :, w + 1:w + 2], 0.0), writes=[hb])
                sc.dma("sp", hh[:, :, 1 - lo:w + 1 + hi], self.h2T[:, :, t0 - lo:t0 + w + hi].rearrange("k p t -> p k t"),
                       reads=[self.bufs["h2T"]], writes=[hb])
                for f in range(22):
                    uu, ub = ur[ui % 3]
                    ui += 1
                    for vg, grp in enumerate((f, f + 22)):
                        bank = 2 * ((ui - 1) % 3) + vg
                        pb = upb[bank]
                        self_ps = self.ps
                        for kc in range(8):
                            sc.op("pe", lambda e, kc=kc, grp=grp, bank=bank, hh=hh: e.matmul(
                                self_ps[:, bank, 0:w + 2], lhsT=wu[:, kc, grp * 128:(grp + 1) * 128], rhs=hh[:, kc, 0:w + 2],
                                start=(kc == 0), stop=(kc == 7)), reads=[wub, hb], writes=[pb])
                        c0 = cw0 + grp * 3
                        cbias = LV_CONVB + grp
                        sc.op("act", lambda e, bank=bank, vg=vg, uu=uu, c0=c0, cbias=cbias: e.activation(
                            out=uu[:, vg, 0:w], in_=self_ps[:, bank, 0:w], func=AF.Identity, bias=self.lvec[:, cbias:cbias + 1], scale=self.lvec[:, c0:c0 + 1]),
                            reads=[pb, self.lvb], writes=[ub])
                        for tap in (1, 2):
                            sc.op("dve", lambda e, bank=bank, vg=vg, uu=uu, c0=c0, tap=tap: e.scalar_tensor_tensor(
                                out=uu[:, vg, 0:w], in0=self_ps[:, bank, tap:tap + w], scalar=self.lvec[:, c0 + tap:c0 + tap + 1], in1=uu[:, vg, 0:w],
                                op0=ALU.mult, op1=ALU.add), reads=[pb, ub, self.lvb], writes=[ub])
                    sg, sgb = sgr[(ui - 1) % 3]
                    sc.op("act", lambda e, uu=uu, sg=sg: e.activation(out=sg[:, 0:w], in_=uu[:, 1, 0:w], func=AF.Silu), reads=[ub], writes=[sgb])
                    sc.op("dve", lambda e, uu=uu, sg=sg, f=f: e.tensor_tensor(out=gT[:, f, 0:w], in0=sg[:, 0:w], in1=uu[:, 0, 0:w], op=ALU.mult),
                          reads=[sgb, ub], writes=[gTb[f]])
                stream = 0 if t0 < S else 1
                for og in range(8):
                    bank = 6 + og % 2
                    pb = dnb[og % 2]
                    xo, xob = xor[xi % 2]
                    xi += 1
                    sc.dma("sp", xo[:, 0:w], self.xres[og, :, t0:t0 + w], reads=[self.bufs["xres"]], writes=[xob])
                    for f in range(22):
                        sc.op("pe", lambda e, f=f, og=og, bank=bank: e.matmul(
                            self.ps[:, bank, 0:w], lhsT=wd[:, f, og * 128:(og + 1) * 128], rhs=gT[:, f, 0:w], start=(f == 0), stop=(f == 21)),
                            reads=[wdb, gTb[f]], writes=[pb])
                    sc.op("dve", lambda e, og=og, bank=bank, xo=xo: e.scalar_tensor_tensor(
                        out=xo[:, 0:w], in0=self.ps[:, bank, 0:w], scalar=self.modv[:, stream, 40 + og:41 + og], in1=xo[:, 0:w],
                        op0=ALU.mult, op1=ALU.add), reads=[pb, xob, self.modb], writes=[xob])
                    sc.dma("pool", self.xres[og, :, t0:t0 + w], xo[:, 0:w], reads=[xob], writes=[self.bufs["xres"]])
            self.end_phase()

    def final_phase(self):
        sc = self.sc
        with self.phase() as ph:
            tmp = self.norm_tmp(ph)
            xr = self.ring2(ph, "xc", [128, 8, 512], F32)
            fg = self.sb(ph, "fg", [128, 8], F32)
            fgb = Buf("fg")
            sc.dma("sp", fg[:], self.din["fng"], writes=[fgb])
            self.nps = Buf("nps")
            for ci, (t0, w) in enumerate(CHUNKS[:-1]):
                xc, xb = xr[ci % 2]
                sc.dma("sp", xc[:, :, 0:w], self.xres_ap(t0, w), reads=[self.bufs["xres"]], writes=[xb])
                self.norm_mod(xc, xb, w, 0, 0, None, None, tmp, (0, self.nps))
                h1, h1b, r, rb = tmp["h1"], tmp["h1b"], tmp["r"], tmp["rb"]
                sc.op("dve", lambda e, xc=xc: e.tensor_tensor(out=h1[:, :, 0:w], in0=xc[:, :, 0:w], in1=fg[:, 0:8].unsqueeze(2).to_broadcast([128, 8, w]), op=ALU.mult),
                      reads=[xb, fgb], writes=[h1b])
                sc.op("dve", lambda e: e.tensor_tensor(out=h1[:, :, 0:w], in0=h1[:, :, 0:w], in1=r[:, 0:w].unsqueeze(1).to_broadcast([128, 8, w]), op=ALU.mult),
                      reads=[h1b, rb], writes=[h1b])
                sc.dma("pool", self.outT[:, :, t0:t0 + w].rearrange("k p t -> p k t"), h1[:, :, 0:w], reads=[h1b], writes=[self.bufs["outT"]])
            self.end_phase()

    def build(self):
        nc, sc = self.nc, self.sc
        L = self.layers
        self.inp("xT_in", [8, 128, T])
        self.inp("cfm", [128, 8, 2])
        self.inp("adaw", [DEPTH, 128, 8, 6 * D])
        self.inp("lvec", [DEPTH, 128, NV])
        self.inp("fng", [128, 8])
        self.inp("ffn_up", [DEPTH, 128, 8, 2 * FFN])
        self.inp("ffn_dn", [DEPTH, 128, 22, D])
        self.inp("cos64", [128, T])
        self.inp("sin64", [128, T])
        self.inp("rm64", [128, 128])
        self.inp("gqa_win", [128, 8, 1536])
        self.inp("gqa_wout", [128, 8, 1024])
        self.inp("gqa_g", [128, 2])
        self.extra_inputs()
        self.xres = self.scratch("xres", [8, 128, T], F32)
        self.h2T = self.scratch("h2T", [8, 128, T], BF16)
        self.QT = self.scratch("QT", [16, 128, T], BF16)
        self.KT = self.scratch("KT", [16, 128, T], BF16)
        self.Vd = self.scratch("Vd", [66, 128, 1040], BF16)
        self.aoT = self.scratch("aoT", [8, 128, T], BF16)
        self.outT = self.scratch("outT", [8, 128, S], F32, kind="ExternalOutput")
        if "xres" in self.debug:
            self.dbg_x = self.scratch("dbg_x", [8, 128, T], F32, kind="ExternalOutput")
        self.consts()
        for k in range(8):
            sc.dma("sp" if k % 2 == 0 else "pool", self.xres[k], self.din["xT_in"][k], writes=[self.bufs["xres"]])
        self.end_phase()
        for l in L:
            last = (l == DEPTH - 1)
            self.mod_phase(l)
            if self.stop == "mod":
                break
            if l == 0:
                self.p1_gqa(l)
                if self.stop == "p1":
                    break
                self.att_std(16 if self.stop != "att1" else 1, 64,
                             lambda h: self.QT[h // 2, (h % 2) * 64:(h % 2) * 64 + 64, :],
                             lambda h: self.KT[(h // 4) // 2, ((h // 4) % 2) * 64:((h // 4) % 2) * 64 + 64, :],
                             lambda h: (h // 4) * 65, 64 ** -0.5)
                if self.stop in ("att", "att1"):
                    break
                self.out_phase(l, "gqa_wout")
                if self.stop == "out":
                    break
            elif l == 1:
                self.layer_mla(l)
            elif l == 2:
                self.layer_diff(l)
            else:
                self.layer_na(l)
            self.ffn_phase(l, do_ctx=not last)
        if "xres" in self.debug:
            for k in range(8):
                sc.dma("sp", self.dbg_x[k], self.xres[k], reads=[self.bufs["xres"]], writes=[Buf("dbg")])
            self.end_phase()
        self.final_phase()
        self.sc.barrier()
        self.stack.close()
        return nc

    def extra_inputs(self):
        pass


LV_ADAB = 0
LV_N1G = 48
LV_N2G = 56
LV_CONVB = 64
LV_CONVW = 108
NV = 108 + 132


def _fm(v):
    v = np.asarray(v, np.float32)
    return np.ascontiguousarray(v.reshape(-1, 128).T)


def _wfm(w):
    w = np.asarray(w, np.float32)
    K, N = w.shape
    return np.ascontiguousarray(w.reshape(K // 128, 128, N).transpose(1, 0, 2))


def host_inputs_core(I, b):
    m = {}
    xt = np.concatenate([np.asarray(I["x"][b], np.float32), np.asarray(I["ctx"][b], np.float32)], axis=0)
    m["xT_in"] = np.ascontiguousarray(xt.T.reshape(8, 128, T))
    cf = np.stack([_fm(I["c"][b]), _fm(I["c_ctx"])], axis=-1)
    m["cfm"] = np.ascontiguousarray(cf)
    return m


def host_inputs(inputs, b):
    I = inputs
    m = host_inputs_core(inputs, b)
    m["adaw"] = np.stack([_wfm(I["ada_w"][l]) for l in range(DEPTH)])
    lv = np.zeros((DEPTH, 128, NV), np.float32)
    for l in range(DEPTH):
        lv[l, :, LV_ADAB:LV_ADAB + 48] = _fm(I["ada_b"][l])
        lv[l, :, LV_N1G:LV_N1G + 8] = _fm(I["norm1_g"][l])
        lv[l, :, LV_N2G:LV_N2G + 8] = _fm(I["norm2_g"][l])
        lv[l, :, LV_CONVB:LV_CONVB + 44] = _fm(I["ffn_conv_b"][l])
        cw = np.stack([_fm(I["ffn_conv_w"][l][k]) for k in range(3)], axis=-1)
        lv[l, :, LV_CONVW:LV_CONVW + 132] = cw.reshape(128, 132)
    m["lvec"] = lv
    m["fng"] = _fm(I["final_norm_g"])
    m["ffn_up"] = np.stack([_wfm(I["ffn_w_up"][l]) for l in range(DEPTH)])
    m["ffn_dn"] = np.stack([_wfm(I["ffn_w_down"][l]) for l in range(DEPTH)])
    c64, s64 = _rope_tables(64, 2)
    m["cos64"], m["sin64"] = c64, s64
    m["rm64"] = _rot_matrix(64, 2)
    m["gqa_win"] = _wfm(I["gqa_w_in"][0])
    m["gqa_wout"] = _wfm(I["gqa_w_out"][0])
    m["gqa_g"] = np.ascontiguousarray(np.stack([np.tile(np.asarray(I["gqa_q_norm_g"][0], np.float32), 2),
                                                np.tile(np.asarray(I["gqa_k_norm_g"][0], np.float32), 2)], axis=-1))
    return m


def _extra_inputs(self):
    self.inp("mla_win", [128, 8, 672])
    self.inp("mla_g", [128, 5])
    self.inp("mla_wuq", [128, 3, 1536])
    self.inp("mla_wukvk", [128, 2, 1024])
    self.inp("mla_wukvv", [128, 2, 1024])
    self.inp("mla_wout", [128, 8, 1024])
    self.inp("rmq96", [96, 96])
    self.inp("cosq96", [96, T])
    self.inp("sinq96", [96, T])
    self.inp("rmk32", [32, 32])
    self.inp("cosk32", [32, T])
    self.inp("sink32", [32, T])
    self.inp("diff_win", [128, 8, 3072])
    self.inp("diff_wout", [128, 8, 1024])
    self.inp("diff_lam", [128, 256])
    self.inp("diff_subg", [128, 1])
    self.inp("na_win", [128, 8, 3072])
    self.inp("na_wout", [128, 8, 1024])
    self.inp("na_bias", [5, 16, 128, 640])


def _group_rms(self, t, srcs, srcb, w, ngr, gains, gcol0, dst, dstb, tmpq, tmpqb):
    sc = self.sc
    for g in range(ngr):
        sc.op("dve", lambda e, g=g: e.tensor_tensor(out=tmpq[:, 0:w], in0=srcs[g][:, 0:w], in1=srcs[g][:, 0:w], op=ALU.mult), reads=[srcb[g]], writes=[tmpqb])
        sc.op("pe", lambda e, g=g: e.matmul(self.ps[:, 3, 0:w], lhsT=self.ones_f[:], rhs=tmpq[:, 0:w], start=(g == 0), stop=(g == ngr - 1)),
              reads=[tmpqb, self.cb], writes=[t["ms2b"]])
    r2, r2b = t["r2"][0], t["r2b"][0]
    self.rstd(r2[:, 0:w], self.ps[:, 3, 0:w], 1.0 / (ngr * 128), [t["ms2b"]], r2b)
    for g in range(ngr):
        sc.op("dve", lambda e, g=g: e.scalar_tensor_tensor(out=dst[:, g, 0:w], in0=srcs[g][:, 0:w], scalar=gains[:, gcol0 + g:gcol0 + g + 1], in1=r2[:, 0:w],
                                                           op0=ALU.mult, op1=ALU.mult), reads=[srcb[g], r2b, t["gb"]], writes=[dstb])


def _p1_mla(self, l):
    sc = self.sc
    with self.phase() as ph:
        win, winb = self.load_cast(ph, "win", self.din["mla_win"], [128, 8, 672])
        wuq, wuqb = self.load_cast(ph, "wuq", self.din["mla_wuq"], [128, 3, 1536])
        wkk, wkkb = self.load_cast(ph, "wkk", self.din["mla_wukvk"], [128, 2, 1024])
        wkv, wkvb = self.load_cast(ph, "wkv", self.din["mla_wukvv"], [128, 2, 1024])
        tmp = self.norm_tmp(ph)
        t = self.qk_tmp(ph)
        gm = self.sb(ph, "gm", [128, 5], F32)
        rmq = self.sb(ph, "rmq", [128, 96], F32)
        rmk = self.sb(ph, "rmk", [128, 32], F32)
        sc.dma("sp", gm[:], self.din["mla_g"], writes=[t["gb"]])
        sc.dma("sp", rmq[0:96, :], self.din["rmq96"], writes=[t["gb"]])
        sc.dma("sp", rmk[0:32, :], self.din["rmk32"], writes=[t["gb"]])
        xr = self.ring2(ph, "xc", [128, 8, 512], F32)
        hr = self.ring2(ph, "hT", [128, 8, 512], BF16)
        cr = self.ring2(ph, "cs", [128, 4, 512], F32)
        vr = self.vt_ring(ph, 16, 64, True)
        cl = [self.sb(ph, f"cl{g}", [128, 512], F32) for g in range(5)]
        clb = [Buf(f"cl{g}") for g in range(5)]
        cqn = self.sb(ph, "cqn", [128, 3, 512], BF16)
        cqnb = Buf("cqn")
        ckvn = self.sb(ph, "ckvn", [128, 2, 512], BF16)
        ckvnb = Buf("ckvn")
        tq = self.sb(ph, "tq", [128, 512], F32)
        tqb = Buf("tq")
        self.nps = Buf("nps")
        pbs = [Buf("pp1"), Buf("pp2")]
        gi = 0
        for ci, (t0, w) in enumerate(CHUNKS):
            hT, hb, xc, xb = self.chunk_front(ph, tmp, xr, hr, ci, t0, w)
            cs, csb = cr[ci % 2]
            sc.dma("sp", cs[0:96, 0, 0:w], self.din["cosq96"][:, t0:t0 + w], writes=[csb])
            sc.dma("sp", cs[0:96, 1, 0:w], self.din["sinq96"][:, t0:t0 + w], writes=[csb])
            sc.dma("sp", cs[0:32, 2, 0:w], self.din["cosk32"][:, t0:t0 + w], writes=[csb])
            sc.dma("sp", cs[0:32, 3, 0:w], self.din["sink32"][:, t0:t0 + w], writes=[csb])
            for g in range(5):
                bank = 1 + gi % 2
                pb = pbs[gi % 2]
                gi += 1
                self.proj_group(win, winb, g * 128, 128, hT, hb, w, bank, pb)
                sc.op("act", lambda e, g=g, bank=bank: e.activation(out=cl[g][:, 0:w], in_=self.ps[:, bank, 0:w], func=AF.Copy), reads=[pb], writes=[clb[g]])
            _group_rms(self, t, cl[0:3], clb[0:3], w, 3, gm, 0, cqn, cqnb, tq, tqb)
            _group_rms(self, t, cl[3:5], clb[3:5], w, 2, gm, 3, ckvn, ckvnb, tq, tqb)
            bank = 1 + gi % 2
            pb = pbs[gi % 2]
            gi += 1
            self.proj_group(win, winb, 640, 32, hT, hb, w, bank, pb)
            self.qk_post(t, bank, pb, w, None, False, True, (rmk, cs[:, 2, :], cs[:, 3, :], csb),
                         [(self.KT[h, 64:96, t0:t0 + w], slice(0, 32)) for h in range(16)], "KT", rows=32)
            for h in range(16):
                bank = 1 + gi % 2
                pb = pbs[gi % 2]
                gi += 1
                self.proj_group(wuq, wuqb, h * 96, 96, cqn, cqnb, w, bank, pb, nk=3)
                self.qk_post(t, bank, pb, w, None, False, True, (rmq, cs[:, 0, :], cs[:, 1, :], csb), self.QT[h, 0:96, t0:t0 + w], "QT", rows=96)
            for g in range(8):
                bank = 1 + gi % 2
                pb = pbs[gi % 2]
                gi += 1
                self.proj_group(wkk, wkkb, g * 128, 128, ckvn, ckvnb, w, bank, pb, nk=2)
                self.qk_post(t, bank, pb, w, None, False, False, None,
                             [(self.KT[2 * g, 0:64, t0:t0 + w], slice(0, 64)), (self.KT[2 * g + 1, 0:64, t0:t0 + w], slice(64, 128))], "KT")
            self.v_tokmajor(wkv, wkvb, 0, 1024, ckvn, ckvnb, t0, w, vr, 16, 64, True, nk=2)
        self.end_phase()


def _layer_mla(self, l):
    _p1_mla(self, l)
    if self.stop == "p1":
        return
    self.att_std(16, 96, lambda h: self.QT[h, 0:96, :], lambda h: self.KT[h, 0:96, :], lambda h: h * 65, 96 ** -0.5)
    self.out_phase(l, "mla_wout")


def _p1_qkv(self, l, wname, rope, nheads_v, hd_v, ones_col):
    sc = self.sc
    with self.phase() as ph:
        win, winb = self.load_cast(ph, "win", self.din[wname], [128, 8, 3072])
        tmp = self.norm_tmp(ph)
        t = self.qk_tmp(ph)
        rm = self.sb(ph, "rm", [128, 128], F32)
        sc.dma("sp", rm[:], self.din["rm64"], writes=[t["gb"]])
        xr = self.ring2(ph, "xc", [128, 8, 512], F32)
        hr = self.ring2(ph, "hT", [128, 8, 512], BF16)
        cr = self.ring2(ph, "cs", [128, 2, 512], F32)
        vr = self.vt_ring(ph, nheads_v, hd_v, ones_col)
        self.nps = Buf("nps")
        pbs = [Buf("pp1"), Buf("pp2")]
        for ci, (t0, w) in enumerate(CHUNKS):
            hT, hb, xc, xb = self.chunk_front(ph, tmp, xr, hr, ci, t0, w)
            cs, csb = cr[ci % 2]
            if rope:
                sc.dma("sp", cs[:, 0, 0:w], self.din["cos64"][:, t0:t0 + w], writes=[csb])
                sc.dma("sp", cs[:, 1, 0:w], self.din["sin64"][:, t0:t0 + w], writes=[csb])
            for g in range(16):
                bank = 1 + g % 2
                self.proj_group(win, winb, g * 128, 128, hT, hb, w, bank, pbs[g % 2])
                dst = (self.QT[g, :, t0:t0 + w], "QT") if g < 8 else (self.KT[g - 8, :, t0:t0 + w], "KT")
                self.qk_post(t, bank, pbs[g % 2], w, None, False, rope, (rm, cs[:, 0, :], cs[:, 1, :], csb), dst[0], dst[1])
            self.v_tokmajor(win, winb, 2048, 1024, hT, hb, t0, w, vr, nheads_v, hd_v, ones_col)
        self.end_phase()


def _att_diff(self, l):
    sc = self.sc
    lam_init = 0.8 - 0.6 * math.exp(-0.3 * l)
    with self.phase() as ph:
        lam = self.sb(ph, "lam", [128, 256], F32)
        lsc = self.sb(ph, "lsc", [128, 8], F32)
        lb = Buf("lam")
        sc.dma("sp", lam[:], self.din["diff_lam"], writes=[lb])
        sc.dma("sp", lsc[:, 4:5], self.din["diff_subg"], writes=[lb])
        sc.op("dve", lambda e: e.tensor_tensor(out=lam[:, 0:64], in0=lam[:, 0:64], in1=lam[:, 64:128], op=ALU.mult), reads=[lb], writes=[lb])
        sc.op("dve", lambda e: e.tensor_tensor(out=lam[:, 128:192], in0=lam[:, 128:192], in1=lam[:, 192:256], op=ALU.mult), reads=[lb], writes=[lb])
        sc.op("dve", lambda e: e.reduce_sum(out=lsc[:, 0:1], in_=lam[:, 0:64], axis=mybir.AxisListType.X), reads=[lb], writes=[lb])
        sc.op("dve", lambda e: e.reduce_sum(out=lsc[:, 1:2], in_=lam[:, 128:192], axis=mybir.AxisListType.X), reads=[lb], writes=[lb])
        sc.op("act", lambda e: e.activation(out=lsc[:, 0:2], in_=lsc[:, 0:2], func=AF.Exp), reads=[lb], writes=[lb])
        sc.op("dve", lambda e: e.tensor_tensor(out=lsc[:, 2:3], in0=lsc[:, 1:2], in1=lsc[:, 0:1], op=ALU.subtract), reads=[lb], writes=[lb])
        sc.op("dve", lambda e: e.tensor_scalar(out=lsc[:, 2:3], in0=lsc[:, 2:3], scalar1=-lam_init, scalar2=None, op0=ALU.add), reads=[lb], writes=[lb])
        sc.op("dve", lambda e: e.tensor_scalar(out=lsc[:, 5:6], in0=lsc[:, 4:5], scalar1=1.0 - lam_init, scalar2=None, op0=ALU.mult), reads=[lb], writes=[lb])
        qr = self.ring2(ph, "aq", [128, T], BF16)
        qr2 = self.ring2(ph, "aq2", [128, T], BF16)
        for (tl, tb_) in qr:
            sc.op("dve", lambda e: e.memset(tl[64:128, :], 0.0), writes=[tb_])
        for (tl, tb_) in qr2:
            sc.op("dve", lambda e: e.memset(tl[0:64, :], 0.0), writes=[tb_])
        kr = self.ring2(ph, "ak", [128, T], BF16)
        vr = self.ring2(ph, "av", [128, 66, 128], BF16)
        pr = [(self.sb(ph, f"pT{i}", [128, 2, 512], BF16), Buf(f"pT{i}")) for i in range(3)]
        r0 = self.sb(ph, "r0", [128, 512], F32)
        r1 = self.sb(ph, "r1", [128, 512], F32)
        o = self.sb(ph, "o", [128, 512], F32)
        o2 = self.sb(ph, "o2", [128, 512], F32)
        fb = Buf("fin")
        onr = self.ring2(ph, "on", [128, 512], BF16)
        sb_ = [Buf("S0"), Buf("S1")]
        accb = [Buf(f"acc{i}") for i in range(4)]
        jobs = []
        for h in range(8):
            for ci, (t0, w) in enumerate(CHUNKS):
                kbs = list(range(66)) if t0 < S else [64, 65]
                for ki, kb in enumerate(kbs):
                    jobs.append((h, t0, w, kb, ki == 0, ki == len(kbs) - 1))
        loaded = set()

        def load(h):
            if h in loaded or h >= 8:
                return
            loaded.add(h)
            (qt, qb), (kt, kb_), (vt, vb) = qr[h % 2], kr[h % 2], vr[h % 2]
            sc.dma("sp", qt[0:64, :], self.QT[h, 0:64, :], reads=[self.bufs["QT"]], writes=[qb])
            sc.dma("sp", qr2[h % 2][0][64:128, :], self.QT[h, 64:128, :], reads=[self.bufs["QT"]], writes=[qr2[h % 2][1]])
            sc.dma("sp", kt[:], self.KT[h], reads=[self.bufs["KT"]], writes=[kb_])
            sc.dma("sp", vt[:], self.Vd[:, :, h * 128:(h + 1) * 128].rearrange("b p d -> p b d"), reads=[self.bufs["Vd"]], writes=[vb])

        def emit_S(n):
            h, t0, w, kb, first, last = jobs[n]
            load(h)
            (qt, qb), (kt, kb_) = qr[h % 2], kr[h % 2]
            (qt2, qb2) = qr2[h % 2]
            sbank = 2 * (n % 2)
            for j in range(2):
                qq = qt if j == 0 else qt2
                sc.op("pe", lambda e, j=j: e.matmul(
                    self.ps[:, sbank + j, 0:w], lhsT=kt[:, kb * 128:(kb + 1) * 128], rhs=qq[:, t0:t0 + w],
                    start=True, stop=True), reads=[qb, qb2, kb_], writes=[sb_[n % 2]])

        fi = 0
        emit_S(0)
        for n in range(len(jobs)):
            h, t0, w, kb, first, last = jobs[n]
            if n + 1 < len(jobs):
                emit_S(n + 1)
            vt, vb = vr[h % 2]
            sbank = 2 * (n % 2)
            pT, pb = pr[n % 3]
            sc.op("act", lambda e: e.activation(out=pT[:, :, 0:w], in_=self.ps[:, sbank:sbank + 2, 0:w], func=AF.Exp, scale=0.125), reads=[sb_[n % 2]], writes=[pb])
            for j in range(2):
                sc.op("pe", lambda e, j=j: e.matmul(self.ps[:, 4 + 2 * j, 0:w], lhsT=vt[:, kb, :], rhs=pT[:, j, 0:w], start=first, stop=last),
                      reads=[pb, vb], writes=[accb[2 * j]])
                sc.op("pe", lambda e, j=j: e.matmul(self.ps[:, 5 + 2 * j, 0:w], lhsT=self.ones_b[:], rhs=pT[:, j, 0:w], start=first, stop=last),
                      reads=[pb, self.cb], writes=[accb[2 * j + 1]])
            if t0 == 0 and first:
                load(h + 1)
            if not last:
                continue
            on, onb = onr[fi % 2]
            fi += 1
            sc.op("dve", lambda e: e.reciprocal(out=r0[:, 0:w], in_=self.ps[:, 5, 0:w]), reads=[accb[1]], writes=[fb])
            sc.op("dve", lambda e: e.reciprocal(out=r1[:, 0:w], in_=self.ps[:, 7, 0:w]), reads=[accb[3]], writes=[fb])
            sc.op("dve", lambda e: e.tensor_tensor(out=r0[:, 0:w], in0=r0[:, 0:w], in1=self.ps[:, 4, 0:w], op=ALU.mult), reads=[accb[0], fb], writes=[fb])
            sc.op("dve", lambda e: e.tensor_tensor(out=r1[:, 0:w], in0=r1[:, 0:w], in1=self.ps[:, 6, 0:w], op=ALU.mult), reads=[accb[2], fb], writes=[fb])
            sc.op("dve", lambda e: e.scalar_tensor_tensor(out=o[:, 0:w], in0=r1[:, 0:w], scalar=lsc[:, 2:3], in1=r0[:, 0:w], op0=ALU.mult, op1=ALU.add),
                  reads=[fb, lb], writes=[fb])
            sc.op("dve", lambda e: e.tensor_tensor(out=o2[:, 0:w], in0=o[:, 0:w], in1=o[:, 0:w], op=ALU.mult), reads=[fb], writes=[fb])
            sc.op("pe", lambda e: e.matmul(self.ps[:, 5, 0:w], lhsT=self.ones_f[:], rhs=o2[:, 0:w], start=True, stop=True),
                  reads=[fb, self.cb], writes=[accb[1]])
            self.rstd(o2[:, 0:w], self.ps[:, 5, 0:w], 1.0 / 128, [accb[1], fb], fb)
            sc.op("dve", lambda e: e.scalar_tensor_tensor(out=on[:, 0:w], in0=o[:, 0:w], scalar=lsc[:, 5:6], in1=o2[:, 0:w], op0=ALU.mult, op1=ALU.mult),
                  reads=[fb, lb], writes=[onb])
            sc.dma("pool", self.aoT[h, :, t0:t0 + w], on[:, 0:w], reads=[onb], writes=[self.bufs["aoT"]])
        self.end_phase()


def _layer_diff(self, l):
    _p1_qkv(self, l, "diff_win", True, 8, 128, False)
    if self.stop == "p1":
        return
    _att_diff(self, l)
    self.out_phase(l, "diff_wout")


def _att_na(self, l):
    sc = self.sc
    with self.phase() as ph:
        qr = self.ring2(ph, "aq", [128, S], BF16)
        kr = self.ring2(ph, "ak", [128, T], BF16)
        for (tl, tb_) in qr + kr:
            sc.op("dve", lambda e: e.memset(tl[64:128, :], 0.0), writes=[tb_])
        vr = self.ring2(ph, "av", [128, 66, 65], BF16)
        br = self.ring2(ph, "nb", [128, 5, 640], F32)
        tbr = self.ring2(ph, "tb", [128, 640], F32)
        pr = [(self.sb(ph, f"pT{i}", [128, 896], BF16), Buf(f"pT{i}")) for i in range(3)]
        our = self.ring2(ph, "ou", [64, 128], F32)
        onr = self.ring2(ph, "on", [64, 512], BF16)
        rdr = self.ring2(ph, "rd", [128, 128], F32)
        for (tl, tb_) in rdr:
            sc.op("dve", lambda e: e.memset(tl[:], 0.0), writes=[tb_])
        sb_ = [Buf("S0"), Buf("S1")]
        accb = [Buf("acc0"), Buf("acc1")]
        bcb = Buf("bc")
        jobs = [(h, j) for h in range(16) for j in range(64)]
        loaded = set()

        def blocks(j):
            b0 = min(max(2 * j - 4, 0), 118)
            return [b0 // 2 + m for m in range(5)] + [64, 65]

        def load(h):
            if h in loaded or h >= 16:
                return
            loaded.add(h)
            (qt, qb), (kt, kb_), (vt, vb), (bt, bb) = qr[h % 2], kr[h % 2], vr[h % 2], br[h % 2]
            rows = slice((h % 2) * 64, (h % 2) * 64 + 64)
            sc.dma("sp", qt[0:64, :], self.QT[h // 2, rows, 0:S], reads=[self.bufs["QT"]], writes=[qb])
            sc.dma("sp", kt[0:64, :], self.KT[h // 2, rows, :], reads=[self.bufs["KT"]], writes=[kb_])
            sc.dma("sp", vt[:], self.Vd[:, :, h * 65:h * 65 + 65].rearrange("b p d -> p b d"), reads=[self.bufs["Vd"]], writes=[vb])
            sc.dma("sp", bt[:], self.din["na_bias"][:, h].rearrange("t p c -> p t c"), writes=[bb])

        def emit_S(n):
            h, j = jobs[n]
            load(h)
            (qt, qb), (kt, kb_) = qr[h % 2], kr[h % 2]
            sbank = 2 * (n % 2)
            for m, blk in enumerate(blocks(j)):
                bank, c0 = (sbank, m * 128) if m < 4 else (sbank + 1, (m - 4) * 128)
                sc.op("pe", lambda e: e.matmul(self.ps[:, bank, c0:c0 + 128], lhsT=kt[:, blk * 128:(blk + 1) * 128], rhs=qt[:, j * 128:(j + 1) * 128],
                                              start=True, stop=True), reads=[qb, kb_], writes=[sb_[n % 2]])

        pending = [None]

        def fin2():
            if pending[0] is None:
                return
            (h, j, ou, oub, on, onb, rd, rdb) = pending[0]
            pending[0] = None
            rows = slice((h % 2) * 64, (h % 2) * 64 + 64)
            jj = j % 4
            sc.op("pe", lambda e: e.matmul(self.ps[:, 6, 0:128], lhsT=self.sel64[:], rhs=rd[:, :], start=True, stop=True),
                  reads=[rdb, self.cb], writes=[bcb])
            sc.op("dve", lambda e: e.tensor_tensor(out=on[:, jj * 128:(jj + 1) * 128], in0=ou[:, :], in1=self.ps[0:64, 6, 0:128], op=ALU.mult),
                  reads=[oub, bcb], writes=[onb])
            if jj == 3:
                t0 = (j // 4) * 512
                sc.dma("pool", self.aoT[h // 2, rows, t0:t0 + 512], on[:, :], reads=[onb], writes=[self.bufs["aoT"]])

        emit_S(0)
        for n in range(len(jobs)):
            h, j = jobs[n]
            if n + 1 < len(jobs):
                emit_S(n + 1)
            (vt, vb), (bt, bb) = vr[h % 2], br[h % 2]
            pat = {0: 1, 1: 2, 62: 3, 63: 4}.get(j, 0)
            s2 = n % 2
            sbank = 2 * s2
            tb, tbb = tbr[s2]
            pT, pb = pr[n % 3]
            sc.op("dve", lambda e: e.scalar_tensor_tensor(out=tb[:, 0:512], in0=self.ps[:, sbank, 0:512], scalar=0.125, in1=bt[:, pat, 0:512],
                                                          op0=ALU.mult, op1=ALU.add), reads=[sb_[s2], bb], writes=[tbb])
            sc.op("dve", lambda e: e.scalar_tensor_tensor(out=tb[:, 512:640], in0=self.ps[:, sbank + 1, 0:128], scalar=0.125, in1=bt[:, pat, 512:640],
                                                          op0=ALU.mult, op1=ALU.add), reads=[sb_[s2], bb], writes=[tbb])
            sc.op("act", lambda e: e.activation(out=pT[:, 0:640], in_=tb[:, 0:640], func=AF.Exp), reads=[tbb], writes=[pb])
            sc.op("act", lambda e: e.activation(out=pT[:, 640:896], in_=self.ps[:, sbank + 1, 128:384], func=AF.Exp, scale=0.125),
                  reads=[sb_[s2]], writes=[pb])
            abank = 4 + s2
            for m, blk in enumerate(blocks(j)):
                sc.op("pe", lambda e: e.matmul(self.ps[0:65, abank, 0:128], lhsT=vt[:, blk, 0:65], rhs=pT[:, m * 128:(m + 1) * 128], start=(m == 0), stop=(m == 6)),
                      reads=[pb, vb], writes=[accb[s2]])
            if j == 0:
                load(h + 1)
            fin2()
            ou, oub = our[s2]
            on, onb = onr[(j // 4) % 2]
            rd, rdb = rdr[s2]
            sc.op("dve", lambda e: e.tensor_copy(out=ou[:, :], in_=self.ps[0:64, abank, 0:128]), reads=[accb[s2]], writes=[oub])
            sc.op("dve", lambda e: e.reciprocal(out=rd[64:65, :], in_=self.ps[64:65, abank, 0:128]), reads=[accb[s2]], writes=[rdb])
            pending[0] = (h, j, ou, oub, on, onb, rd, rdb)
        fin2()
        self.end_phase()


def _layer_na(self, l):
    _p1_qkv(self, l, "na_win", False, 16, 64, True)
    if self.stop == "p1":
        return
    _att_na(self, l)
    self.out_phase(l, "na_wout", do_ctx=False)


Builder.extra_inputs = _extra_inputs
Builder.layer_mla = _layer_mla
Builder.layer_diff = _layer_diff
Builder.layer_na = _layer_na


def _na_bias_tables(rpb):
    rpb = np.asarray(rpb, np.float32)
    out = np.empty((5, 16, 128, 640), np.float32)
    q = np.arange(128)
    qdr, qcol = q // 64, q % 64
    i = np.arange(640)
    for p, j in enumerate((2, 0, 1, 62, 63)):
        r = 2 * j + qdr
        rs = np.clip(r - 4, 0, 120)
        b0 = min(max(2 * j - 4, 0), 118)
        krow, kcol = b0 + i // 64, i % 64
        cs = np.clip(qcol - 8, 0, 48)
        inw = ((kcol[:, None] >= cs[None]) & (kcol[:, None] < cs[None] + 16) & (krow[:, None] >= rs[None]) & (krow[:, None] < rs[None] + 8))
        dr = np.clip(krow[:, None] - r[None] + 7, 0, 14)
        dc = np.clip(kcol[:, None] - qcol[None] + 15, 0, 30)
        for h in range(16):
            tbl = np.where(inw, rpb[h][dr, dc], np.float32(NEG)).astype(np.float32)
            out[p, h] = tbl.reshape(5, 128, 128).transpose(1, 0, 2).reshape(128, 640)
    return out


def host_inputs_extra(I, m):
    m["mla_win"] = _wfm(I["mla_w_in"][0])
    m["mla_g"] = np.ascontiguousarray(np.concatenate([_fm(I["mla_q_norm_g"][0]), _fm(I["mla_kv_norm_g"][0])], axis=1))
    m["mla_wuq"] = _wfm(I["mla_w_uq"][0])
    wukv = np.asarray(I["mla_w_ukv"][0], np.float32).reshape(256, 16, 128)
    m["mla_wukvk"] = _wfm(np.ascontiguousarray(wukv[:, :, :64]).reshape(256, 1024))
    m["mla_wukvv"] = _wfm(np.ascontiguousarray(wukv[:, :, 64:]).reshape(256, 1024))
    m["mla_wout"] = _wfm(I["mla_w_out"][0])
    c32, s32 = _rope_tables(32, 1)
    cq = np.ones((96, T), np.float32)
    sq = np.zeros((96, T), np.float32)
    cq[64:], sq[64:] = c32, s32
    m["cosq96"], m["sinq96"] = cq, sq
    m["cosk32"], m["sink32"] = c32, s32
    m["rmq96"] = _rot_matrix(32, 1, offset=64)
    m["rmk32"] = _rot_matrix(32, 1)
    m["diff_win"] = _wfm(I["diff_w_in"][0])
    m["diff_wout"] = _wfm(I["diff_w_out"][0])
    m["diff_lam"] = np.ascontiguousarray(np.broadcast_to(np.asarray(I["diff_lambda"][0], np.float32).reshape(1, 256), (128, 256)))
    m["diff_subg"] = _fm(I["diff_subln_g"][0])
    m["na_win"] = _wfm(I["na_w_in"][0])
    m["na_wout"] = _wfm(I["na_w_out"][0])
    m["na_bias"] = _na_bias_tables(I["na_rpb"][0])
    return m


_NC_CACHE = {}


def kernel(**inputs):
    if "nc" not in _NC_CACHE:
        b = Builder()
        _NC_CACHE["nc"] = b.build()
        _NC_CACHE["names"] = list(b.din.keys())
    nc = _NC_CACHE["nc"]
    shared = host_inputs(inputs, 0)
    host_inputs_extra(inputs, shared)
    in_maps = []
    for b in range(NCORES):
        m = dict(shared)
        if b > 0:
            m.update(host_inputs_core(inputs, b))
        in_maps.append({k: m[k] for k in _NC_CACHE["names"]})
    res = run_bass_kernel_spmd(nc, in_maps, core_ids=list(range(NCORES)))
    out = np.empty((NCORES, S, D), np.float32)
    for b in range(NCORES):
        out[b] = np.asarray(res.results[b]["outT"]).reshape(D, S).T
    return out
```
